# Optimizing a Trainium2 kernel written in Bass

```python
import math
import jax, jax.numpy as jnp
from jax import lax
import numpy as np

D_MODEL = 1024
BATCH = 2
SEQ = 8192
DEPTH = 2
DEC_BATCH = 32
DEC_SEQ = 1
PAST_LEN = 8192
PAGE_SIZE = 128

N_META = 16
BLOCK = 128
HEAD_DIM = 64
SSD_WIDTH = 3 * D_MODEL // 8
SSD_HEADS = SSD_WIDTH // HEAD_DIM
SSD_GROUPS = 2
SSD_STATE = 128
SSD_CONV = 4
SSD_XBC = SSD_WIDTH + 2 * SSD_GROUPS * SSD_STATE
SB_WIDTH = 3 * D_MODEL // 8
SB_HEADS = SB_WIDTH // HEAD_DIM
SB_BIAS_INIT = -6.0
RG_WIDTH = D_MODEL // 4
RG_BLOCKS = 4
RG_BLOCK_DIM = RG_WIDTH // RG_BLOCKS
RG_CONV = 4
RG_C = 8.0
MIX_WIDTH = SSD_WIDTH + SB_WIDTH + RG_WIDTH
IN_COLS = 2 * SSD_WIDTH + 2 * SSD_GROUPS * SSD_STATE + SSD_HEADS + 3 * SB_WIDTH + 2 * RG_WIDTH
D_FF = 2816
FFN_CONV = 3
EPS = 1e-6

kernel_name = 'hymba_ssd_stickbreak_rglru_convffn_step'


def _in_splits():
    sizes = (SSD_WIDTH, SSD_XBC, SSD_HEADS, SB_WIDTH, SB_WIDTH, SB_WIDTH, RG_WIDTH, RG_WIDTH)
    return [int(s) for s in np.cumsum(sizes)[:-1]]


def rms_norm(x, g):
    xf = x.astype(jnp.float32)
    y = xf * lax.rsqrt(jnp.mean(xf * xf, axis=-1, keepdims=True) + EPS)
    return (y * g.astype(jnp.float32)).astype(x.dtype)


def causal_dwconv(x, buf, w, b):
    K = w.shape[0]
    L = x.shape[1]
    xp = jnp.concatenate([buf.astype(x.dtype), x], axis=1)
    y = b + xp[:, 0:L] * w[0]
    for j in range(1, K):
        y = y + xp[:, j:j + L] * w[j]
    return y, xp[:, -(K - 1):]


def ssd_scan(x, dt, A, Bm, Cm, h0, chunk):
    b, L, H, P = x.shape
    nc = L // chunk
    rep = H // Bm.shape[2]

    def to_chunks(t):
        return jnp.moveaxis(t.reshape((b, nc, chunk) + t.shape[2:]), 1, 0)

    tri = jnp.tril(jnp.ones((chunk, chunk), dtype=bool))

    def step(h, inp):
        xc, dtc, Bc, Cc = inp
        Bh = jnp.repeat(Bc, rep, axis=2)
        Ch = jnp.repeat(Cc, rep, axis=2)
        cum = jnp.cumsum(dtc * A, axis=1)
        seg = cum[:, :, None, :] - cum[:, None, :, :]
        decay = jnp.exp(jnp.where(tri[None, :, :, None], seg, -jnp.inf))
        cb = jnp.einsum('bthn,bshn->btsh', Ch, Bh)
        w = cb * decay * dtc[:, None, :, :]
        y_intra = jnp.einsum('btsh,bshp->bthp', w, xc)
        y_inter = jnp.einsum('bthn,bhpn->bthp', Ch, h) * jnp.exp(cum)[..., None]
        dec_end = jnp.exp(cum[:, -1:, :] - cum) * dtc
        h_new = h * jnp.exp(cum[:, -1])[:, :, None, None] + jnp.einsum('bsh,bshp,bshn->bhpn', dec_end, xc, Bh)
        return h_new, y_intra + y_inter

    h_fin, ys = lax.scan(step, h0, (to_chunks(x), to_chunks(dt), to_chunks(Bm), to_chunks(Cm)))
    return jnp.moveaxis(ys, 0, 1).reshape(b, L, H, P), h_fin


def ssd_prompt_scan(x, dt, A, Bm, Cm, h0):
    y0, h = ssd_scan(x[:, :N_META], dt[:, :N_META], A, Bm[:, :N_META], Cm[:, :N_META], h0, N_META)
    y1, h = ssd_scan(x[:, N_META:], dt[:, N_META:], A, Bm[:, N_META:], Cm[:, N_META:], h, BLOCK)
    return jnp.concatenate([y0, y1], axis=1), h


def ssd_sample_scan(x, dt, A, Bm, Cm, h0):
    return ssd_scan(x, dt, A, Bm, Cm, h0, x.shape[1])


def ssd_mixer(z, xbc, dt_raw, conv_buf, h0, p, scan_fn):
    b, L, _ = xbc.shape
    xbc_c, new_buf = causal_dwconv(xbc, conv_buf, p['ssd_conv_w'], p['ssd_conv_b'])
    xbc_c = jax.nn.silu(xbc_c.astype(jnp.float32))
    xs = xbc_c[..., :SSD_WIDTH].reshape(b, L, SSD_HEADS, HEAD_DIM)
    Bm = xbc_c[..., SSD_WIDTH:SSD_WIDTH + SSD_GROUPS * SSD_STATE].reshape(b, L, SSD_GROUPS, SSD_STATE)
    Cm = xbc_c[..., SSD_WIDTH + SSD_GROUPS * SSD_STATE:].reshape(b, L, SSD_GROUPS, SSD_STATE)
    dt = jax.nn.softplus(dt_raw.astype(jnp.float32) + p['ssd_dt_bias'].astype(jnp.float32))
    A = -jnp.exp(p['ssd_a_log'].astype(jnp.float32))
    y, h_new = scan_fn(xs, dt, A, Bm, Cm, h0.astype(jnp.float32))
    y = y + xs * p['ssd_d'].astype(jnp.float32)[:, None]
    y = y.reshape(b, L, SSD_WIDTH) * jax.nn.silu(z.astype(jnp.float32))
    return rms_norm(y, p['ssd_norm']), new_buf, h_new


def stick_breaking(q, k, v, bias, q_pos, k_pos):
    z = jnp.einsum('bqhd,bkhd->bhqk', q, k, preferred_element_type=jnp.float32) * (HEAD_DIM ** -0.5)
    z = z + bias.astype(jnp.float32)[None, :, None, None]
    mask = k_pos[None, :] < q_pos[:, None]
    log_1m = jnp.where(mask, jax.nn.log_sigmoid(-z), 0.0)
    after = lax.cumsum(log_1m, axis=3, reverse=True) - log_1m
    w = jnp.where(mask, jnp.exp(jax.nn.log_sigmoid(z) + after), 0.0)
    return jnp.einsum('bhqk,bkhd->bqhd', w, v.astype(jnp.float32))


def sb_prompt(q, k, v, bias):
    b, T, H, D = q.shape
    pos = jnp.arange(T, dtype=jnp.int32)
    o_meta = stick_breaking(q[:, :N_META], k[:, :N_META], v[:, :N_META], bias, pos[:N_META], pos[:N_META])
    nb = (T - N_META) // BLOCK
    qb = jnp.moveaxis(q[:, N_META:].reshape(b, nb, BLOCK, H, D), 1, 0)
    pb = pos[N_META:].reshape(nb, BLOCK)
    ob = lax.map(lambda a: stick_breaking(a[0], k, v, bias, a[1], pos), (qb, pb))
    ob = jnp.moveaxis(ob, 0, 1).reshape(b, T - N_META, H, D)
    return jnp.concatenate([o_meta, ob], axis=1)


def sb_sample(q, k, v, bias, ck, cv, page_table):
    db, L = q.shape[0], q.shape[1]
    past = page_table.shape[1] * ck.shape[1]
    pk = ck[page_table].reshape(db, past, SB_HEADS, HEAD_DIM)
    pv = cv[page_table].reshape(db, past, SB_HEADS, HEAD_DIM)
    k_all = jnp.concatenate([pk.astype(k.dtype), k], axis=1)
    v_all = jnp.concatenate([pv.astype(v.dtype), v], axis=1)
    q_pos = past + jnp.arange(L, dtype=jnp.int32)
    k_pos = jnp.arange(past + L, dtype=jnp.int32)
    return stick_breaking(q, k_all, v_all, bias, q_pos, k_pos)


def rglru_mixer(xr, gr, conv_buf, h0, p):
    b, L, _ = xr.shape
    xc, new_buf = causal_dwconv(xr, conv_buf, p['rg_conv_w'], p['rg_conv_b'])
    xc = xc.astype(jnp.float32)
    xb = xc.reshape(b, L, RG_BLOCKS, RG_BLOCK_DIM)
    r = jax.nn.sigmoid(jnp.einsum('blhi,hij->blhj', xb, p['rg_wa']) + p['rg_ba']).reshape(b, L, RG_WIDTH)
    i = jax.nn.sigmoid(jnp.einsum('blhi,hij->blhj', xb, p['rg_wx']) + p['rg_bx']).reshape(b, L, RG_WIDTH)
    log_a = RG_C * r * jax.nn.log_sigmoid(p['rg_lambda'].astype(jnp.float32))
    a = jnp.exp(log_a)
    u = jnp.sqrt(-jnp.expm1(2.0 * log_a)) * (i * xc)
    u = u.at[:, 0].add(a[:, 0] * h0.astype(jnp.float32))

    def comb(l, rr):
        return (l[0] * rr[0], rr[0] * l[1] + rr[1])

    _, h = lax.associative_scan(comb, (a, u), axis=1)
    y = h * jax.nn.gelu(gr.astype(jnp.float32))
    return y, new_buf, h[:, -1]


def layer(x, p, ssd_buf, ssd_h, rg_buf, rg_h, ffn_buf, ssd_scan_fn, attend_fn):
    b, L, _ = x.shape
    h = rms_norm(x, p['norm1'])
    proj = jnp.einsum('bld,dc->blc', h, p['w_in'])
    z, xbc, dt_raw, q, k, v, xr, gr = jnp.split(proj, _in_splits(), axis=-1)
    y_ssd, ssd_buf, ssd_h = ssd_mixer(z, xbc, dt_raw, ssd_buf, ssd_h, p, ssd_scan_fn)
    q = rms_norm(q.reshape(b, L, SB_HEADS, HEAD_DIM), p['q_norm'])
    k = rms_norm(k.reshape(b, L, SB_HEADS, HEAD_DIM), p['k_norm'])
    v = v.reshape(b, L, SB_HEADS, HEAD_DIM)
    y_sb = rms_norm(attend_fn(q, k, v, p['sb_bias']).reshape(b, L, SB_WIDTH), p['sb_out_norm'])
    y_rg, rg_buf, rg_h = rglru_mixer(xr, gr, rg_buf, rg_h, p)
    y_rg = rms_norm(y_rg, p['rg_out_norm'])
    mix = jnp.concatenate([y_ssd, y_sb, y_rg], axis=-1).astype(x.dtype)
    x = x + jnp.einsum('blm,md->bld', mix, p['w_out'])
    h = rms_norm(x, p['norm2'])
    g, u = jnp.split(jnp.einsum('bld,df->blf', h, p['w_up']), 2, axis=-1)
    g, ffn_buf = causal_dwconv(g, ffn_buf, p['ffn_conv_w'], p['ffn_conv_b'])
    x = x + jnp.einsum('blf,fd->bld', jax.nn.silu(g) * u, p['w_down'])
    return x, (k, v, ssd_h, ssd_buf, rg_h, rg_buf, ffn_buf)


def setup_inputs(seed: int = 0) -> dict:
    key = jax.random.key(seed)
    ks = iter(jax.random.split(key, 64))

    def nrm(shape, s=1.0):
        return s * jax.random.normal(next(ks), shape, jnp.float32)

    def unif(shape, lo, hi):
        return jax.random.uniform(next(ks), shape, jnp.float32, lo, hi)

    n_pages = PAST_LEN // PAGE_SIZE
    n_used = DEC_BATCH * n_pages
    n_pool = n_used + max(1, n_used // 4)
    perm = jax.random.permutation(next(ks), n_pool).astype(jnp.int32)
    page_table = perm[:n_used].reshape(DEC_BATCH, n_pages)

    dt0 = jnp.exp(unif((DEPTH, SSD_HEADS), math.log(1e-3), math.log(1e-1)))
    a_rg = unif((DEPTH, RG_WIDTH), 0.9, 0.999) ** (1.0 / RG_C)

    return {
        'x_prompt': nrm((BATCH, SEQ, D_MODEL)),
        'x_sample': nrm((DEC_BATCH, DEC_SEQ, D_MODEL)),
        'cache_k': nrm((DEPTH, n_pool, PAGE_SIZE, SB_HEADS, HEAD_DIM)),
        'cache_v': nrm((DEPTH, n_pool, PAGE_SIZE, SB_HEADS, HEAD_DIM)),
        'page_table': page_table,
        'state_ssm': nrm((DEPTH, DEC_BATCH, SSD_HEADS, HEAD_DIM, SSD_STATE), 0.1),
        'state_ssm_conv': nrm((DEPTH, DEC_BATCH, SSD_CONV - 1, SSD_XBC)),
        'state_rg': nrm((DEPTH, DEC_BATCH, RG_WIDTH), 0.5),
        'state_rg_conv': nrm((DEPTH, DEC_BATCH, RG_CONV - 1, RG_WIDTH)),
        'state_ffn_conv': nrm((DEPTH, DEC_BATCH, FFN_CONV - 1, D_FF)),
        'meta_tokens': nrm((N_META, D_MODEL)),
        'norm1': 1.0 + nrm((DEPTH, D_MODEL), 0.02),
        'w_in': nrm((DEPTH, D_MODEL, IN_COLS), D_MODEL ** -0.5),
        'ssd_conv_w': nrm((DEPTH, SSD_CONV, SSD_XBC), SSD_CONV ** -0.5),
        'ssd_conv_b': nrm((DEPTH, SSD_XBC), 0.02),
        'ssd_dt_bias': dt0 + jnp.log(-jnp.expm1(-dt0)),
        'ssd_a_log': jnp.log(unif((DEPTH, SSD_HEADS), 1.0, 16.0)),
        'ssd_d': 1.0 + nrm((DEPTH, SSD_HEADS), 0.02),
        'ssd_norm': 1.0 + nrm((DEPTH, SSD_WIDTH), 0.02),
        'q_norm': 1.0 + nrm((DEPTH, HEAD_DIM), 0.02),
        'k_norm': 1.0 + nrm((DEPTH, HEAD_DIM), 0.02),
        'sb_bias': SB_BIAS_INIT + nrm((DEPTH, SB_HEADS), 0.1),
        'sb_out_norm': 1.0 + nrm((DEPTH, SB_WIDTH), 0.02),
        'rg_conv_w': nrm((DEPTH, RG_CONV, RG_WIDTH), RG_CONV ** -0.5),
        'rg_conv_b': nrm((DEPTH, RG_WIDTH), 0.02),
        'rg_wa': nrm((DEPTH, RG_BLOCKS, RG_BLOCK_DIM, RG_BLOCK_DIM), RG_BLOCK_DIM ** -0.5),
        'rg_ba': nrm((DEPTH, RG_BLOCKS, RG_BLOCK_DIM), 0.02),
        'rg_wx': nrm((DEPTH, RG_BLOCKS, RG_BLOCK_DIM, RG_BLOCK_DIM), RG_BLOCK_DIM ** -0.5),
        'rg_bx': nrm((DEPTH, RG_BLOCKS, RG_BLOCK_DIM), 0.02),
        'rg_lambda': jnp.log(a_rg) - jnp.log1p(-a_rg),
        'rg_out_norm': 1.0 + nrm((DEPTH, RG_WIDTH), 0.02),
        'w_out': nrm((DEPTH, MIX_WIDTH, D_MODEL), MIX_WIDTH ** -0.5),
        'norm2': 1.0 + nrm((DEPTH, D_MODEL), 0.02),
        'w_up': nrm((DEPTH, D_MODEL, 2 * D_FF), D_MODEL ** -0.5),
        'ffn_conv_w': nrm((DEPTH, FFN_CONV, D_FF), FFN_CONV ** -0.5),
        'ffn_conv_b': nrm((DEPTH, D_FF), 0.02),
        'w_down': nrm((DEPTH, D_FF, D_MODEL), D_FF ** -0.5),
    }


def reference(x_prompt, x_sample, cache_k, cache_v, page_table, state_ssm, state_ssm_conv, state_rg,
              state_rg_conv, state_ffn_conv, meta_tokens, norm1, w_in, ssd_conv_w, ssd_conv_b, ssd_dt_bias,
              ssd_a_log, ssd_d, ssd_norm, q_norm, k_norm, sb_bias, sb_out_norm, rg_conv_w, rg_conv_b, rg_wa, rg_ba,
              rg_wx, rg_bx, rg_lambda, rg_out_norm, w_out, norm2, w_up, ffn_conv_w, ffn_conv_b, w_down):
    bp = x_prompt.shape[0]
    dtype = x_prompt.dtype
    meta = jnp.broadcast_to(meta_tokens.astype(dtype)[None], (bp, N_META, D_MODEL))
    xp = jnp.concatenate([meta, x_prompt], axis=1)
    xs = x_sample

    def sample_attend(ck, cv):
        return lambda q, k, v, bias: sb_sample(q, k, v, bias, ck, cv, page_table)

    outs_p = []
    outs_s = []
    for l in range(DEPTH):
        p = {
            'norm1': norm1[l], 'w_in': w_in[l], 'ssd_conv_w': ssd_conv_w[l], 'ssd_conv_b': ssd_conv_b[l],
            'ssd_dt_bias': ssd_dt_bias[l], 'ssd_a_log': ssd_a_log[l], 'ssd_d': ssd_d[l], 'ssd_norm': ssd_norm[l],
            'q_norm': q_norm[l], 'k_norm': k_norm[l], 'sb_bias': sb_bias[l], 'sb_out_norm': sb_out_norm[l],
            'rg_conv_w': rg_conv_w[l], 'rg_conv_b': rg_conv_b[l], 'rg_wa': rg_wa[l], 'rg_ba': rg_ba[l],
            'rg_wx': rg_wx[l], 'rg_bx': rg_bx[l], 'rg_lambda': rg_lambda[l], 'rg_out_norm': rg_out_norm[l],
            'w_out': w_out[l], 'norm2': norm2[l], 'w_up': w_up[l], 'ffn_conv_w': ffn_conv_w[l],
            'ffn_conv_b': ffn_conv_b[l], 'w_down': w_down[l],
        }
        xp, st_p = layer(
            xp, p,
            jnp.zeros((bp, SSD_CONV - 1, SSD_XBC), dtype),
            jnp.zeros((bp, SSD_HEADS, HEAD_DIM, SSD_STATE), jnp.float32),
            jnp.zeros((bp, RG_CONV - 1, RG_WIDTH), dtype),
            jnp.zeros((bp, RG_WIDTH), jnp.float32),
            jnp.zeros((bp, FFN_CONV - 1, D_FF), dtype),
            ssd_prompt_scan, sb_prompt)
        xs, st_s = layer(
            xs, p, state_ssm_conv[l], state_ssm[l], state_rg_conv[l], state_rg[l], state_ffn_conv[l],
            ssd_sample_scan, sample_attend(cache_k[l], cache_v[l]))
        outs_p.append(st_p)
        outs_s.append(st_s)

    def stk(outs, i):
        return jnp.stack([o[i] for o in outs], axis=0)

    y_prompt = xp[:, N_META:]
    y_sample = xs
    return (y_prompt, y_sample,
            stk(outs_p, 0), stk(outs_p, 1), stk(outs_p, 2), stk(outs_p, 3), stk(outs_p, 4), stk(outs_p, 5), stk(outs_p, 6),
            stk(outs_s, 0), stk(outs_s, 1), stk(outs_s, 2), stk(outs_s, 3), stk(outs_s, 4), stk(outs_s, 5), stk(outs_s, 6))
```

```python
import contextlib
import numpy as np
import concourse.bass as bass
import concourse.mybir as mybir
from concourse.bass_utils import run_bass_kernel_spmd

F32 = mybir.dt.float32
BF16 = mybir.dt.bfloat16
I32 = mybir.dt.int32
ALU = mybir.AluOpType
AF = mybir.ActivationFunctionType
AX = mybir.AxisListType

ENGS = ("pe", "dve", "act", "pool", "sp")
NCORES = 8
D = 1024
DEPTH = 2
NMETA = 16
HD = 64
NH = 6
SSDW = 384
XBC = 896
NST = 128
RGW = 256
MIXW = 1024
INC = 2950
DFF = 2816
NFC = DFF // 128
DB = 32
EPS = 1e-6
CG_Z = [(0, 128), (128, 128), (256, 128)]
CG_XBC = [(384 + 128 * i, 128) for i in range(7)]
CG_DT = (1280, 6)
CG_Q = [(1286 + 128 * i, 128) for i in range(3)]
CG_K = [(1670 + 128 * i, 128) for i in range(3)]
CG_V = [(2054 + 128 * i, 128) for i in range(3)]
CG_XR = [(2438 + 128 * i, 128) for i in range(2)]
CG_GR = [(2694 + 128 * i, 128) for i in range(2)]

PT = {}
_c = 0
for _n, _w in [("g1", 8), ("g2", 8), ("cwx", 28), ("cbx", 7), ("cwr", 8), ("cbr", 2), ("cwf", 66), ("cbf", 22),
               ("dtb", 1), ("alog", 1), ("dsk", 3), ("ssdn", 3), ("qn", 1), ("kn", 1), ("sbb", 6), ("sbn", 6),
               ("rba", 2), ("rbx", 2), ("rlam", 2), ("rgn", 2)]:
    PT[_n] = _c
    _c += _w
NPT = _c


class Buf:
    def __init__(self, name):
        self.name = name
        self.writers = []
        self.readers = []
        self.dsem = None


class TB:
    def __init__(self, t, b):
        self.t = t
        self.b = b

    def __getitem__(self, k):
        return self.t[k]


class Prog:
    def __init__(self, nc, stack):
        self.nc = nc
        self.ops = {e: [] for e in ENGS}
        self.cnt = {e: 0 for e in ENGS}
        self.sems = {}
        self.dma_tot = {}
        self.waited = {e: {} for e in ENGS}
        self._stack = stack
        self.n_dma_sems = 90
        self.dma_rr = 0

    def sem(self, key):
        if key not in self.sems:
            self.sems[key] = self._stack.enter_context(self.nc.semaphore("s_" + key))
        return self.sems[key]

    def _dsem(self, b):
        if b.dsem is None:
            b.dsem = f"d{self.dma_rr % self.n_dma_sems}"
            self.dma_rr += 1
            self.dma_tot.setdefault(b.dsem, 0)
        return b.dsem

    def _collect(self, eng, reads, writes):
        need = {}
        def add(tok):
            k, v = tok
            if v > need.get(k, 0):
                need[k] = v
        for b in reads:
            for t in b.writers:
                add(t)
        for b in writes:
            for t in b.writers:
                add(t)
            for t in b.readers:
                add(t)
        waits = []
        w = self.waited[eng]
        for k, v in need.items():
            if eng == "pe" and k == "e_pe":
                continue
            if w.get(k, 0) >= v:
                continue
            w[k] = v
            waits.append((k, v))
        return waits

    def _commit(self, tok, reads, writes):
        for b in reads:
            b.readers.append(tok)
            if len(b.readers) > 64:
                mx = {}
                for k, v in b.readers:
                    mx[k] = max(mx.get(k, 0), v)
                b.readers = list(mx.items())
        for b in writes:
            b.writers = [tok]
            b.readers = []

    def op(self, eng, emit, reads=(), writes=()):
        reads = [r.b if isinstance(r, TB) else r for r in reads]
        writes = [r.b if isinstance(r, TB) else r for r in writes]
        waits = self._collect(eng, reads, writes)
        self.cnt[eng] += 1
        tok = ("e_" + eng, self.cnt[eng])
        self.ops[eng].append((waits, emit, ("e_" + eng, 1)))
        self._commit(tok, reads, writes)

    def dma(self, eng, emit, reads=(), writes=(), sbuf=None):
        reads = [r.b if isinstance(r, TB) else r for r in reads]
        writes = [r.b if isinstance(r, TB) else r for r in writes]
        waits = self._collect(eng, reads, writes)
        b = sbuf if sbuf is not None else (list(writes) + list(reads))[0]
        if isinstance(b, TB):
            b = b.b
        k = self._dsem(b)
        self.dma_tot[k] += 16
        tok = (k, self.dma_tot[k])
        self.ops[eng].append((waits, emit, (k, 16)))
        self._commit(tok, reads, writes)

    def cc(self, emit, key):
        self.dma_tot.setdefault(key, 0)
        self.dma_tot[key] += 1
        self.ops["pool"].append(([], emit, (key, None)))
        return (key, self.dma_tot[key])

    def barrier(self):
        targets = {}
        for e in ENGS:
            if self.cnt[e] > 0:
                targets["e_" + e] = self.cnt[e]
        for k, v in self.dma_tot.items():
            if v > 0:
                targets[k] = v
        for e in ENGS:
            waits = []
            w = self.waited[e]
            for k, v in targets.items():
                if w.get(k, 0) >= v:
                    continue
                w[k] = v
                waits.append((k, v))
            if waits:
                self.ops[e].append((waits, None, None))

    def emit(self, block):
        handles = {"pe": "tensor", "dve": "vector", "act": "scalar", "pool": "gpsimd", "sp": "sync"}
        for e in ENGS:
            for waits, emit, inc in self.ops[e]:
                for k, v in waits:
                    self.sem(k)
                if inc is not None:
                    self.sem(inc[0])
        for e in ENGS:
            ops = self.ops[e]
            if not ops:
                continue
            def body(eh, ops=ops):
                for waits, emit, inc in ops:
                    for k, v in waits:
                        eh.wait_ge(self.sems[k], v)
                    if emit is not None:
                        ins = emit(eh)
                        if inc[1] is None:
                            ins.then_inc(self.sems[inc[0]])
                        else:
                            ins.then_inc(self.sems[inc[0]], inc[1])
            getattr(block, handles[e])(body)


class K:
    def __init__(self, SEQ, NPAGES, NPOOL, do_sample=True):
        self.SEQ = SEQ
        self.T = NMETA + SEQ
        self.NPAGES = NPAGES
        self.NPOOL = NPOOL
        self.NB = 1 + SEQ // 128
        self.do_sample = do_sample
        self.nc = bass.Bass("TRN2", target_bir_lowering=False)
        self.build()

    def din(self, name, shape, dt=F32):
        return self.nc.dram_tensor(name, list(shape), dt, kind="ExternalInput").ap()

    def dout(self, name, shape, dt=F32):
        return self.nc.dram_tensor(name, list(shape), dt, kind="ExternalOutput").ap()

    def dscr(self, name, shape, dt=F32):
        return self.nc.dram_tensor(name, list(shape), dt).ap()

    def sb(self, st, name, shape, dt=F32):
        self._uid = getattr(self, "_uid", 0) + 1
        name = f"sb{self._uid}_{name}"
        t = st.enter_context(self.nc.sbuf_tensor(name, list(shape), dt))
        return TB(t, Buf(name))

    def tt(self, eng, out, in0, in1, op, R, W):
        self.P.op(eng, lambda e: e.tensor_tensor(out=out, in0=in0, in1=in1, op=op), R, W)

    def ts(self, eng, out, in0, s1, s2, op0, op1, R, W):
        if s2 is None:
            self.P.op(eng, lambda e: e.tensor_scalar(out=out, in0=in0, scalar1=s1, scalar2=None, op0=op0), R, W)
        else:
            self.P.op(eng, lambda e: e.tensor_scalar(out=out, in0=in0, scalar1=s1, scalar2=s2, op0=op0, op1=op1), R, W)

    def stt(self, eng, out, in0, scalar, in1, op0, op1, R, W):
        eng = "dve"
        self.P.op(eng, lambda e: e.scalar_tensor_tensor(out=out, in0=in0, scalar=scalar, in1=in1, op0=op0, op1=op1), R, W)

    def act(self, out, in_, func, R, W, bias=None, scale=None):
        kw = {}
        if bias is not None:
            kw["bias"] = bias
        if scale is not None:
            kw["scale"] = scale
        self.P.op("act", lambda e: e.activation(out=out, in_=in_, func=func, **kw), R, W)

    def cp(self, eng, out, in_, R, W):
        if eng == "act":
            self.P.op("act", lambda e: e.activation(out=out, in_=in_, func=AF.Copy), R, W)
        else:
            self.P.op(eng, lambda e: e.tensor_copy(out=out, in_=in_), R, W)

    def mm(self, out, lhsT, rhs, start, stop, R, W):
        self.P.op("pe", lambda e: e.matmul(out, lhsT=lhsT, rhs=rhs, start=start, stop=stop), R, W)

    def tr(self, out, in_, ident, R, W):
        self.P.op("pe", lambda e: e.transpose(out, in_, ident), R, W)

    def dma(self, eng, out, in_, R, W, sbuf=None):
        if eng == "pool":
            eng = "sp"
        self.P.dma(eng, lambda e: e.dma_start(out=out, in_=in_), R, W, sbuf=sbuf)

    def memset(self, eng, ap, val, W):
        self.P.op(eng, lambda e: e.memset(ap, val), [], W)

    def silu_chain(self, v, out, tmp, R, W, n_eng="dve"):
        self.act(tmp, v, AF.Exp, R, W, scale=-1.0)
        self.act(tmp, tmp, AF.Ln, W, W, bias=1.0)
        self.act(tmp, tmp, AF.Exp, W, W, scale=-1.0)

    def build(self):
        nc = self.nc
        T, SEQ = self.T, self.SEQ
        I = self.I = {}
        O = self.O = {}
        I["xT_p"] = self.din("xT_p", [D, T])
        I["xT_s"] = self.din("xT_s", [D, DB])
        I["w_in"] = self.din("w_in", [DEPTH, D, INC])
        I["w_out"] = self.din("w_out", [DEPTH, MIXW, D])
        I["w_up"] = self.din("w_up", [DEPTH, D, 2 * DFF])
        I["w_dn"] = self.din("w_dn", [DEPTH, DFF, D])
        I["ptab"] = self.din("ptab", [DEPTH, 128, NPT])
        I["rgw"] = self.din("rgw", [DEPTH, 2, 2, 128, 128])
        I["consts"] = self.din("consts", [128, 1024])
        I["sel6"] = self.din("sel6", [6, 3, 128])
        I["i6"] = self.din("i6", [6, 6])
        if self.do_sample:
            I["ck"] = [self.din(f"ck{l_}", [self.NPOOL, 128 * HD]) for l_ in range(DEPTH)]
            I["cv"] = [self.din(f"cv{l_}", [self.NPOOL, 128 * HD]) for l_ in range(DEPTH)]
            I["ptab_pg"] = self.din("ptab_pg", [self.NPAGES, DB], I32)
            I["hsel"] = self.din("hsel", [3, 128, HD])
            I["sbh"] = self.din("sbh", [128, DEPTH])
            I["s_ssm"] = self.din("s_ssm", [DEPTH, 3, 128, DB, NST])
            I["s_ssmc"] = self.din("s_ssmc", [DEPTH, 7, 128, 3, DB])
            I["s_rg"] = self.din("s_rg", [DEPTH, 2, 128, DB])
            I["s_rgc"] = self.din("s_rgc", [DEPTH, 2, 128, 3, DB])
            I["s_ffc"] = self.din("s_ffc", [DEPTH, NFC, 128, 2, DB])
        O["yT_p"] = self.dout("yT_p", [D, T])
        O["kT_p"] = self.dout("kT_p", [DEPTH, SSDW, T])
        O["vT_p"] = self.dout("vT_p", [DEPTH, SSDW, T])
        O["ssm_p"] = self.dout("ssm_p", [DEPTH, 128, SSDW])
        O["ssmc_p"] = self.dout("ssmc_p", [DEPTH, 7, 128, 3])
        O["rg_p"] = self.dout("rg_p", [DEPTH, 2, 128])
        O["rgc_p"] = self.dout("rgc_p", [DEPTH, 2, 128, 3])
        O["ffc_p"] = self.dout("ffc_p", [DEPTH, NFC, 128, 2])
        if self.do_sample:
            O["yT_s"] = self.dout("yT_s", [D, DB])
            O["kT_s"] = self.dout("kT_s", [DEPTH, SSDW, DB])
            O["vT_s"] = self.dout("vT_s", [DEPTH, SSDW, DB])
            O["ssm_s"] = self.dout("ssm_s", [DEPTH, 3, 128, DB, NST])
            O["ssmc_s"] = self.dout("ssmc_s", [DEPTH, 7, 128, 3, DB])
            O["rg_s"] = self.dout("rg_s", [DEPTH, 2, 128, DB])
            O["rgc_s"] = self.dout("rgc_s", [DEPTH, 2, 128, 3, DB])
            O["ffc_s"] = self.dout("ffc_s", [DEPTH, NFC, 128, 2, DB])
        S = self.S = {}
        S["xT1"] = self.dscr("xT1", [D, T])
        S["QT"] = self.dscr("QT", [3, 128, T], BF16)
        S["KT"] = self.dscr("KT", [3, 128, T], BF16)
        S["Vt"] = self.dscr("Vt", [T, SSDW], BF16)
        S["mixT"] = self.dscr("mixT", [MIXW, T], BF16)
        if self.do_sample:
            S["xs1"] = self.dscr("xs1", [D, DB])
            S["mixs"] = self.dscr("mixs", [MIXW, DB], BF16)
            S["qs"] = self.dscr("qs", [SSDW, DB])
            S["ag_in"] = [self.dscr(f"ag_in{l_}", [HD, DB]) for l_ in range(DEPTH)]
            S["ag_out"] = [self.dscr(f"ag_out{l_}", [NCORES * HD, DB]) for l_ in range(DEPTH)]

        with contextlib.ExitStack() as st:
            self.P = Prog(nc, st)
            self.ps = [TB(st.enter_context(nc.psum_tensor(f"ps{i}", [128, 512], F32)), Buf(f"ps{i}")) for i in range(7)]
            self.psb = TB(st.enter_context(nc.psum_tensor("psb", [128, 1024], BF16)), Buf("psb"))
            cst = self.sb(st, "cst", [128, 1024])
            self.dma("sp", cst[:], I["consts"][:, :], [], [cst])
            self.cst = cst
            self.identf = cst[:, 0:128]
            self.triU_f = cst[:, 128:256]
            self.tri_ts = cst[:, 256:384]
            self.tri_st = cst[:, 384:512]
            cbf = self.sb(st, "cbf", [128, 640], BF16)
            self.cbf = cbf
            self.cp("dve", cbf[:, 0:128], cst[:, 0:128], [cst], [cbf])
            self.ts("dve", cbf[:, 128:256], cst[:, 128:256], -1.0, None, ALU.mult, None, [cst], [cbf])
            self.memset("pool", cbf[:, 256:384], 1.0 / 1024.0, [cbf])
            self.memset("pool", cbf[:, 384:512], 1.0, [cbf])
            self.memset("pool", cbf[:, 512:640], 1.0 / 64.0, [cbf])
            self.ident_bf = cbf[:, 0:128]
            self.ntriU_bf = cbf[:, 128:256]
            self.ones1024_bf = cbf[:, 256:384]
            self.ones_bf = cbf[:, 384:512]
            bo = self.sb(st, "bo", [128, 128], BF16)
            self.memset("pool", bo[:], 0.0, [bo])
            self.memset("pool", bo[0:64, 0:64], 1.0 / 64.0, [bo])
            self.memset("pool", bo[64:128, 64:128], 1.0 / 64.0, [bo])
            self.bo = bo
            cf = self.sb(st, "cf", [128, 512])
            self.memset("pool", cf[:, 0:128], 1.0, [cf])
            self.memset("pool", cf[:, 128:256], -1.0, [cf])
            self.memset("pool", cf[:, 256:384], 0.0, [cf])
            self.memset("pool", cf[:, 384:512], 1.0 / 384.0, [cf])
            self.cf = cf
            sel6 = self.sb(st, "sel6", [6, 3, 128])
            self.dma("sp", sel6[:], I["sel6"][:, :, :], [], [sel6])
            self.sel6 = sel6
            i6 = self.sb(st, "i6", [6, 6])
            self.dma("sp", i6[:], I["i6"][:, :], [], [i6])
            self.i6 = i6
            ptab = self.sb(st, "ptab", [128, DEPTH, NPT])
            self.dma("sp", ptab[:], I["ptab"].rearrange("l p c -> p l c"), [], [ptab])
            self.ptab = ptab
            der = self.sb(st, "der", [128, DEPTH, 16])
            self.der = der
            for l in range(DEPTH):
                self.act(der[0:6, l, 0:1], ptab[0:6, l, PT["alog"]:PT["alog"] + 1], AF.Exp, [ptab], [der])
                self.ts("dve", der[0:6, l, 0:1], der[0:6, l, 0:1], -1.0, None, ALU.mult, None, [der], [der])
                lam = ptab[:, l, PT["rlam"]:PT["rlam"] + 2]
                self.act(der[:, l, 1:3], lam, AF.Exp, [ptab], [der], scale=-1.0)
                self.act(der[:, l, 1:3], der[:, l, 1:3], AF.Ln, [der], [der], bias=1.0)
                self.ts("dve", der[:, l, 1:3], der[:, l, 1:3], -8.0, None, ALU.mult, None, [der], [der])
                self.ts("dve", der[:, l, 3:5], der[:, l, 1:3], 2.0, None, ALU.mult, None, [der], [der])
                self.ts("dve", der[:, l, 5:7], ptab[:, l, PT["rba"]:PT["rba"] + 2], -1.0, None, ALU.mult, None, [ptab], [der])
                self.ts("dve", der[:, l, 7:9], ptab[:, l, PT["rbx"]:PT["rbx"] + 2], -1.0, None, ALU.mult, None, [ptab], [der])
                self.ts("dve", der[:, l, 9:10], ptab[:, l, PT["qn"]:PT["qn"] + 1], 0.125, None, ALU.mult, None, [ptab], [der])
            self.P.barrier()

            for l in range(DEPTH):
                x_in = I["xT_p"] if l == 0 else S["xT1"]
                x_out = S["xT1"] if l < DEPTH - 1 else O["yT_p"]
                xs_in = (I["xT_s"] if l == 0 else S["xs1"]) if self.do_sample else None
                xs_out = (S["xs1"] if l < DEPTH - 1 else O["yT_s"]) if self.do_sample else None
                self.phase_a(l, x_in, xs_in)
                self.P.barrier()
                self.phase_b(l)
                self.P.barrier()
                if self.do_sample:
                    self.phase_b_sample(l)
                    self.P.barrier()
                self.phase_c(l, x_in, x_out, xs_in, xs_out)
                self.P.barrier()
            with nc.allow_non_contiguous_dma(reason="small strided state outputs"), nc.Block() as block:
                self.P.emit(block)

    def load_cast(self, st, name, src_rows, ncols, nkc, scale_cols=None, l=0, chunk=2048):
        w = self.sb(st, name, [128, nkc, ncols], BF16)
        st2 = contextlib.ExitStack()
        stg = [self.sb(st2, name + f"_stg{i}", [128, chunk]) for i in range(2)]
        i = 0
        for kc in range(nkc):
            for c0 in range(0, ncols, chunk):
                cw = min(chunk, ncols - c0)
                sg = stg[i % 2]
                self.dma("sp" if i % 2 == 0 else "act", sg[:, 0:cw], src_rows[kc * 128:(kc + 1) * 128, c0:c0 + cw], [], [sg])
                eng = "dve" if i % 2 == 0 else "pool"
                if scale_cols is not None:
                    scc = scale_cols[kc] if isinstance(scale_cols, (list, tuple)) else scale_cols + kc
                    sc = self.ptab[:, l, scc:scc + 1]
                    self.ts(eng, w[:, kc, c0:c0 + cw], sg[:, 0:cw], sc, None, ALU.mult, None, [sg, self.ptab], [w])
                else:
                    self.cp(eng, w[:, kc, c0:c0 + cw], sg[:, 0:cw], [sg], [w])
                i += 1
        self.P.barrier()
        st2.close()
        return w

    def tiles(self, tw=512):
        tl = [(0, NMETA)]
        for c0 in range(NMETA, self.T, tw):
            tl.append((c0, tw))
        return tl

    def phase_a(self, l, x_in, xs_in):
        nc, P, I, O, S = self.nc, self.P, self.I, self.O, self.S
        pt = self.ptab
        der = self.der
        def pc(name, k=0, n=1):
            return pt[:, l, PT[name] + k:PT[name] + k + n]
        with contextlib.ExitStack() as st:
            W = self.load_cast(st, "win", I["w_in"][l], INC, 8, scale_cols=PT["g1"], l=l, chunk=1475)
            rgw = self.sb(st, "rgw", [128, 4, 128], BF16)
            rgs = self.sb(st, "rgs", [128, 4, 128])
            self.dma("sp", rgs[:], I["rgw"][l].rearrange("a g p c -> p (a g) c"), [], [rgs])
            self.cp("dve", rgw[:], rgs[:], [rgs], [rgw])
            if self.do_sample:
                self.phase_a_sample(l, xs_in, W, rgw)
            NW = 256
            xt = [self.sb(st, f"xt{i}", [128, 8, NW]) for i in range(1)]
            hb = [self.sb(st, f"hb{i}", [128, 8, NW], BF16) for i in range(1)]
            sq = self.sb(st, "sq", [128, 8, NW], BF16)
            rstd = self.sb(st, "rstd", [128, NW])
            zz = self.sb(st, "zz", [128, 3, NW])
            sz = self.sb(st, "sz", [128, 3, NW])
            xp = self.sb(st, "xp", [128, 7, 3 + NW])
            xcv = self.sb(st, "xcv", [128, 7, NW])
            xtmp = self.sb(st, "xtmp", [128, 7, NW])
            xcb = self.sb(st, "xcb", [128, 7, NW], BF16)
            dtT = self.sb(st, "dtT", [6, NW])
            aT = self.sb(st, "aT", [6, NW])
            cumT = self.sb(st, "cumT", [6, NW])
            qk = self.sb(st, "qk", [128, 6, NW])
            qksq = self.sb(st, "qksq", [128, 6, NW], BF16)
            qkr = self.sb(st, "qkr", [128, 6, NW])
            qkb = self.sb(st, "qkb", [128, 6, NW], BF16)
            vv = self.sb(st, "vv", [128, 3, NW])
            vb = self.sb(st, "vb", [128, 3, NW], BF16)
            vtok = self.sb(st, "vtok", [128, 4, SSDW], BF16)
            xrp = self.sb(st, "xrp", [128, 2, 3 + NW])
            rga = self.sb(st, "rga", [128, 2, NW])
            rgb_ = self.sb(st, "rgb", [128, 2, NW], BF16)
            rgt = [self.sb(st, f"rgt{i}", [128, 2, NW]) for i in range(5)]
            grr = self.sb(st, "grr", [128, 2, NW])
            hprev = self.sb(st, "hprev", [128, 2])
            state = self.sb(st, "state", [128, SSDW])
            stateb = self.sb(st, "stateb", [128, SSDW], BF16)
            ytk = self.sb(st, "ytk", [128, SSDW])
            ytmp = self.sb(st, "ytmp", [128, SSDW])
            yg = self.sb(st, "yg", [128, 3, NW])
            ysq = self.sb(st, "ysq", [128, 3, NW], BF16)
            yr = self.sb(st, "yr", [128, NW])
            mixb = self.sb(st, "mixb", [128, 3, NW], BF16)
            mixr = self.sb(st, "mixr", [128, 2, NW], BF16)
            dctok = self.sb(st, "dctok", [128, 12])
            smallt = self.sb(st, "smallt", [128, 24])
            rhsb = self.sb(st, "rhsb", [6, 6 * 128])
            dec = self.sb(st, "dec", [128, 6, 128])
            cbm = self.sb(st, "cbm", [128, 2, 128])
            wT = self.sb(st, "wT", [128, 6, 128], BF16)
            xdt = self.sb(st, "xdt", [128, SSDW], BF16)
            xde = self.sb(st, "xde", [128, SSDW], BF16)
            btok = self.sb(st, "btok", [128, 256], BF16)
            self.memset("dve", smallt[:], 0.0, [smallt])
            self.memset("dve", dctok[:], 0.0, [dctok])
            self.memset("dve", state[:], 0.0, [state])
            self.memset("dve", stateb[:], 0.0, [stateb])
            self.memset("dve", hprev[:], 0.0, [hprev])
            self.memset("pool", xp[:, :, 0:3], 0.0, [xp])
            self.memset("pool", xrp[:, :, 0:3], 0.0, [xrp])
            ps = self.ps
            psb = self.psb
            tl = self.tiles(NW)
            psi = 0
            for ti, (c0, n) in enumerate(tl):
                X = xt[0]
                H = hb[0]
                self.dma("sp", X[:, :, 0:n], x_in[:, c0:c0 + n].rearrange("(k p) t -> p k t", p=128), [], [X])
                self.tt("pool", sq[:, :, 0:n], X[:, :, 0:n], X[:, :, 0:n], ALU.mult, [X], [sq])
                self.cp("dve", H[:, :, 0:n], X[:, :, 0:n], [X], [H])
                pm = ps[6]
                for kc in range(8):
                    self.mm(pm[:, 0:n], self.ones1024_bf, sq[:, kc, 0:n], kc == 0, kc == 7, [sq, self.cbf], [pm])
                self.act(rstd[:, 0:n], pm[:, 0:n], AF.Ln, [pm], [rstd], bias=EPS)
                self.act(rstd[:, 0:n], rstd[:, 0:n], AF.Exp, [rstd], [rstd], scale=-0.5)

                def proj(c_start, rows, dst_ap, dstb, eng="dve"):
                    nonlocal psi
                    pp = ps[psi % 4]
                    psi += 1
                    for kc in range(8):
                        self.mm(pp[0:rows, 0:n], W[:, kc, c_start:c_start + rows], H[:, kc, 0:n], kc == 0, kc == 7, [W, H], [pp])
                    self.tt(eng, dst_ap, pp[0:rows, 0:n], rstd[0:rows, 0:n], ALU.mult, [pp, rstd], [dstb])

                for g, (cs, r) in enumerate(CG_Z):
                    proj(cs, r, zz[:, g, 0:n], zz)
                for g, (cs, r) in enumerate(CG_XBC):
                    proj(cs, r, xp[:, g, 3:3 + n], xp)
                proj(CG_DT[0], 6, dtT[0:6, 0:n], dtT)
                for g, (cs, r) in enumerate(CG_Q + CG_K):
                    proj(cs, r, qk[:, g, 0:n], qk)
                for g, (cs, r) in enumerate(CG_V):
                    proj(cs, r, vv[:, g, 0:n], vv)
                for g, (cs, r) in enumerate(CG_XR):
                    proj(cs, r, xrp[:, g, 3:3 + n], xrp)
                for g, (cs, r) in enumerate(CG_GR):
                    proj(cs, r, grr[:, g, 0:n], grr)

                self.act(sz[:, :, 0:n], zz[:, :, 0:n], AF.Exp, [zz], [sz], scale=-1.0)
                self.act(sz[:, :, 0:n], sz[:, :, 0:n], AF.Ln, [sz], [sz], bias=1.0)
                self.act(sz[:, :, 0:n], sz[:, :, 0:n], AF.Exp, [sz], [sz], scale=-1.0)
                self.tt("pool", sz[:, :, 0:n], sz[:, :, 0:n], zz[:, :, 0:n], ALU.mult, [sz, zz], [sz])
                for g in range(7):
                    eng = "dve" if g % 2 == 0 else "pool"
                    self.ts(eng, xcv[:, g, 0:n], xp[:, g, 0:n], pc("cwx", 4 * g), pc("cbx", g), ALU.mult, ALU.add, [xp, pt], [xcv])
                    for j in range(1, 4):
                        self.stt(eng, xcv[:, g, 0:n], xp[:, g, j:j + n], pc("cwx", 4 * g + j), xcv[:, g, 0:n], ALU.mult, ALU.add, [xp, pt, xcv], [xcv])
                if ti == len(tl) - 1:
                    self.dma("pool", O["ssmc_p"][l].rearrange("g p j -> p g j"), xp[:, :, n:n + 3], [xp], [], sbuf=xp)
                self.act(xtmp[:, :, 0:n], xcv[:, :, 0:n], AF.Exp, [xcv], [xtmp], scale=-1.0)
                self.act(xtmp[:, :, 0:n], xtmp[:, :, 0:n], AF.Ln, [xtmp], [xtmp], bias=1.0)
                self.act(xtmp[:, :, 0:n], xtmp[:, :, 0:n], AF.Exp, [xtmp], [xtmp], scale=-1.0)
                self.tt("dve", xtmp[:, :, 0:n], xtmp[:, :, 0:n], xcv[:, :, 0:n], ALU.mult, [xtmp, xcv], [xtmp])
                xc = xtmp
                self.cp("pool", xcb[:, :, 0:n], xc[:, :, 0:n], [xc], [xcb])
                self.cp("pool", xp[:, :, 0:3], xp[:, :, n:n + 3], [xp], [xp])
                self.act(dtT[0:6, 0:n], dtT[0:6, 0:n], AF.Exp, [dtT, pt], [dtT], bias=pt[0:6, l, PT["dtb"]:PT["dtb"] + 1])
                self.act(dtT[0:6, 0:n], dtT[0:6, 0:n], AF.Ln, [dtT], [dtT], bias=1.0)
                self.ts("dve", aT[0:6, 0:n], dtT[0:6, 0:n], der[0:6, l, 0:1], None, ALU.mult, None, [dtT, der], [aT])
                self.tt("pool", qksq[:, :, 0:n], qk[:, :, 0:n], qk[:, :, 0:n], ALU.mult, [qk], [qksq])
                for g in range(6):
                    pp = ps[psi % 4]
                    psi += 1
                    self.mm(pp[:, 0:n], self.bo[:, :], qksq[:, g, 0:n], True, True, [self.bo, qksq], [pp])
                    self.act(qkr[:, g, 0:n], pp[:, 0:n], AF.Ln, [pp], [qkr], bias=EPS)
                self.act(qkr[:, :, 0:n], qkr[:, :, 0:n], AF.Exp, [qkr], [qkr], scale=-0.5)
                for g in range(6):
                    gain = der[:, l, 9:10] if g < 3 else pc("kn")
                    self.stt("dve", qkr[:, g, 0:n], qk[:, g, 0:n], gain, qkr[:, g, 0:n], ALU.mult, ALU.mult, [qk, qkr, der, pt], [qkr])
                qkn = qkr
                self.cp("pool", qkb[:, :, 0:n], qkn[:, :, 0:n], [qkn], [qkb])
                self.dma("pool", S["QT"][:, :, c0:c0 + n].rearrange("g p t -> p g t"), qkb[:, 0:3, 0:n], [qkb], [], sbuf=qkb)
                self.dma("pool", S["KT"][:, :, c0:c0 + n].rearrange("g p t -> p g t"), qkb[:, 3:6, 0:n], [qkb], [], sbuf=qkb)
                self.dma("pool", O["kT_p"][l, :, c0:c0 + n].rearrange("(g p) t -> p g t", p=128), qkn[:, 3:6, 0:n], [qkn], [], sbuf=qkn)
                self.dma("pool", O["vT_p"][l, :, c0:c0 + n].rearrange("(g p) t -> p g t", p=128), vv[:, :, 0:n], [vv], [], sbuf=vv)
                self.cp("pool", vb[:, :, 0:n], vv[:, :, 0:n], [vv], [vb])
                nch = max(1, n // 128)
                L = min(n, 128)
                for ch in range(nch):
                    for g in range(3):
                        self.tr(psb[0:L, g * 128:(g + 1) * 128], vb[:, g, ch * L:(ch + 1) * L], self.ident_bf, [vb, self.cbf], [psb])
                    self.cp("dve", vtok[0:L, ch, :], psb[0:L, 0:SSDW], [psb], [vtok])
                if n >= 128:
                    self.dma("pool", S["Vt"][c0:c0 + n, :].rearrange("(c p) f -> p c f", p=128), vtok[:, 0:nch, :], [vtok], [], sbuf=vtok)
                else:
                    self.dma("pool", S["Vt"][c0:c0 + n, :], vtok[0:n, 0, :], [vtok], [], sbuf=vtok)

                for g in range(2):
                    eng = "pool"
                    self.ts(eng, rga[:, g, 0:n], xrp[:, g, 0:n], pc("cwr", 4 * g), pc("cbr", g), ALU.mult, ALU.add, [xrp, pt], [rga])
                    for j in range(1, 4):
                        self.stt(eng, rga[:, g, 0:n], xrp[:, g, j:j + n], pc("cwr", 4 * g + j), rga[:, g, 0:n], ALU.mult, ALU.add, [xrp, pt, rga], [rga])
                if ti == len(tl) - 1:
                    self.dma("pool", O["rgc_p"][l].rearrange("g p j -> p g j"), xrp[:, :, n:n + 3], [xrp], [], sbuf=xrp)
                self.cp("pool", xrp[:, :, 0:3], xrp[:, :, n:n + 3], [xrp], [xrp])
                self.cp("dve", rgb_[:, :, 0:n], rga[:, :, 0:n], [rga], [rgb_])
                R_, I_, A_, U_, G_ = rgt
                for g in range(2):
                    for (ai, dst, nb) in ((0, R_, 5), (1, I_, 7)):
                        pp = ps[psi % 4]
                        psi += 1
                        self.mm(pp[:, 0:n], rgw[:, ai * 2 + g, :], rgb_[:, g, 0:n], True, True, [rgw, rgb_], [pp])
                        self.act(dst[:, g, 0:n], pp[:, 0:n], AF.Exp, [pp, der], [dst], bias=der[:, l, nb + g:nb + g + 1], scale=-1.0)
                for dst in (R_, I_):
                    self.act(dst[:, :, 0:n], dst[:, :, 0:n], AF.Ln, [dst], [dst], bias=1.0)
                    self.act(dst[:, :, 0:n], dst[:, :, 0:n], AF.Exp, [dst], [dst], scale=-1.0)
                for g in range(2):
                    self.act(A_[:, g, 0:n], R_[:, g, 0:n], AF.Exp, [R_, der], [A_], scale=der[:, l, 1 + g:2 + g])
                    self.act(U_[:, g, 0:n], R_[:, g, 0:n], AF.Exp, [R_, der], [U_], scale=der[:, l, 3 + g:4 + g])
                self.ts("dve", U_[:, :, 0:n], U_[:, :, 0:n], -1.0, 1.0, ALU.mult, ALU.add, [U_], [U_])
                self.act(U_[:, :, 0:n], U_[:, :, 0:n], AF.Ln, [U_], [U_])
                self.act(U_[:, :, 0:n], U_[:, :, 0:n], AF.Exp, [U_], [U_], scale=0.5)
                self.tt("dve", U_[:, :, 0:n], U_[:, :, 0:n], I_[:, :, 0:n], ALU.mult, [U_, I_], [U_])
                self.tt("dve", U_[:, :, 0:n], U_[:, :, 0:n], rga[:, :, 0:n], ALU.mult, [U_, rga], [U_])
                for g in range(2):
                    self.P.op("dve", lambda e, g=g: e.tensor_tensor_scan(out=R_[:, g, 0:n], data0=A_[:, g, 0:n], data1=U_[:, g, 0:n],
                                                                          initial=hprev[:, g:g + 1], op0=ALU.mult, op1=ALU.add),
                              [A_, U_, hprev], [R_])
                self.cp("dve", hprev[:, :], R_[:, :, n - 1], [R_], [hprev])
                if ti == len(tl) - 1:
                    self.dma("pool", O["rg_p"][l].rearrange("g p -> p g"), hprev[:, :], [hprev], [], sbuf=hprev)
                self.tt("pool", G_[:, :, 0:n], grr[:, :, 0:n], grr[:, :, 0:n], ALU.mult, [grr], [G_])
                self.ts("pool", G_[:, :, 0:n], G_[:, :, 0:n], 0.044715, 1.0, ALU.mult, ALU.add, [G_], [G_])
                self.tt("pool", G_[:, :, 0:n], G_[:, :, 0:n], grr[:, :, 0:n], ALU.mult, [G_, grr], [G_])
                self.act(G_[:, :, 0:n], G_[:, :, 0:n], AF.Exp, [G_], [G_], scale=-2.0 * 0.7978845608028654)
                self.act(G_[:, :, 0:n], G_[:, :, 0:n], AF.Ln, [G_], [G_], bias=1.0)
                self.act(G_[:, :, 0:n], G_[:, :, 0:n], AF.Exp, [G_], [G_], scale=-1.0)
                self.tt("pool", G_[:, :, 0:n], G_[:, :, 0:n], grr[:, :, 0:n], ALU.mult, [G_, grr], [G_])
                self.tt("dve", I_[:, :, 0:n], R_[:, :, 0:n], G_[:, :, 0:n], ALU.mult, [R_, G_], [I_])
                self.tt("pool", A_[:, :, 0:n], I_[:, :, 0:n], I_[:, :, 0:n], ALU.mult, [I_], [A_])
                self.cp("pool", rgb_[:, :, 0:n], A_[:, :, 0:n], [A_], [rgb_])
                pm = ps[6]
                for g in range(2):
                    self.mm(pm[:, 0:n], self.ones_bf, rgb_[:, g, 0:n], g == 0, g == 1, [rgb_, self.cbf], [pm])
                self.act(yr[:, 0:n], pm[:, 0:n], AF.Ln, [pm], [yr], bias=EPS * RGW, scale=1.0)
                self.act(yr[:, 0:n], yr[:, 0:n], AF.Exp, [yr], [yr], scale=-0.5)
                for g in range(2):
                    self.stt("dve", mixr[:, g, 0:n], I_[:, g, 0:n], 16.0, yr[:, 0:n], ALU.mult, ALU.mult, [I_, yr], [mixr])
                self.dma("pool", S["mixT"][768:1024, c0:c0 + n].rearrange("(g p) t -> p g t", p=128), mixr[:, :, 0:n], [mixr], [], sbuf=mixr)

                for ch in range(nch):
                    o = ch * L
                    xsb = [xcb[:, g, o:o + L] for g in range(3)]
                    Bb = [xcb[:, 3 + g, o:o + L] for g in range(2)]
                    Cb = [xcb[:, 5 + g, o:o + L] for g in range(2)]
                    self.P.op("dve", lambda e, o=o: e.tensor_tensor_scan(out=cumT[0:6, o:o + L], data0=aT[0:6, o:o + L], data1=self.cf[0:6, 256:256 + L],
                                                                          initial=0.0, op0=ALU.add, op1=ALU.add), [aT, self.cf], [cumT])
                    p7 = ps[5]
                    self.tr(p7[0:L, 0:6], dtT[0:6, o:o + L], self.identf[0:6, 0:6], [dtT, self.cst], [p7])
                    self.tr(p7[0:L, 6:12], cumT[0:6, o:o + L], self.identf[0:6, 0:6], [cumT, self.cst], [p7])
                    self.cp("dve", dctok[0:L, :], p7[0:L, 0:12], [p7], [dctok])
                    self.tt("dve", rhsb[0:6, 0:6 * L].rearrange("p (h t) -> p h t", h=6), cumT[0:6, o:o + L].unsqueeze(1).to_broadcast([6, 6, L]),
                            self.i6[0:6, 0:6].unsqueeze(2).to_broadcast([6, 6, L]), ALU.mult, [cumT, self.i6], [rhsb])
                    pb0, pb1 = ps[psi % 4], ps[(psi + 1) % 4]
                    psi += 2
                    self.mm(pb0[:, 0:3 * L], self.cf[0:6, 0:128], rhsb[0:6, 0:3 * L], True, True, [self.cf, rhsb], [pb0])
                    self.mm(pb1[:, 0:3 * L], self.cf[0:6, 0:128], rhsb[0:6, 3 * L:6 * L], True, True, [self.cf, rhsb], [pb1])
                    for h in range(6):
                        pb = pb0 if h < 3 else pb1
                        hh = h % 3
                        self.ts("dve" if h % 2 == 0 else "dve", dec[0:L, h, 0:L], pb[0:L, hh * L:(hh + 1) * L], dctok[0:L, 6 + h:7 + h], 0.0,
                                ALU.subtract, ALU.min, [pb, dctok], [dec])
                    self.act(dec[0:L, :, 0:L], dec[0:L, :, 0:L], AF.Exp, [dec], [dec])
                    self.cp("dve", smallt[:, 12:15], pb0[:, 0:3 * L].rearrange("p (h t) -> p h t", h=3)[:, :, L - 1], [pb0], [smallt])
                    self.cp("dve", smallt[:, 15:18], pb1[:, 0:3 * L].rearrange("p (h t) -> p h t", h=3)[:, :, L - 1], [pb1], [smallt])
                    self.tt("dve", smallt[0:L, 0:6], smallt[0:L, 12:18], dctok[0:L, 6:12], ALU.subtract, [smallt, dctok], [smallt])
                    self.cp("dve", smallt[0:L, 6:12], dctok[0:L, 6:12], [dctok], [smallt])
                    self.act(smallt[:, 0:18], smallt[:, 0:18], AF.Exp, [smallt], [smallt])
                    pcb = ps[psi % 4]
                    psi += 1
                    for g in range(2):
                        self.mm(pcb[0:L, g * L:(g + 1) * L], Bb[g], Cb[g], True, True, [xcb], [pcb])
                    self.tt("dve", cbm[0:L, :, 0:L], pcb[0:L, 0:2 * L].rearrange("p (g t) -> p g t", g=2),
                            self.tri_ts[0:L, 0:L].unsqueeze(1).to_broadcast([L, 2, L]), ALU.mult, [pcb, self.cst], [cbm])
                    for g in range(2):
                        self.tt("pool", wT[0:L, 3 * g:3 * g + 3, 0:L], dec[0:L, 3 * g:3 * g + 3, 0:L],
                                cbm[0:L, g, 0:L].unsqueeze(1).to_broadcast([L, 3, L]), ALU.mult, [dec, cbm], [wT])
                    for g in range(3):
                        self.tr(psb[0:L, g * 128:(g + 1) * 128], xsb[g], self.ident_bf, [xcb, self.cbf], [psb])
                    for g in range(2):
                        self.tr(psb[0:L, 384 + g * 128:384 + (g + 1) * 128], Bb[g], self.ident_bf, [xcb, self.cbf], [psb])
                    self.tt("dve", xdt[0:L, :].rearrange("p (h d) -> p h d", h=6), psb[0:L, 0:SSDW].rearrange("p (h d) -> p h d", h=6),
                            dctok[0:L, 0:6].unsqueeze(2).to_broadcast([L, 6, HD]), ALU.mult, [psb, dctok], [xdt])
                    self.cp("dve", btok[0:L, :], psb[0:L, 384:640], [psb], [btok])
                    self.tt("pool", xde[0:L, :].rearrange("p (h d) -> p h d", h=6), xdt[0:L, :].rearrange("p (h d) -> p h d", h=6), smallt[0:L, 0:6].unsqueeze(2).to_broadcast([L, 6, HD]), ALU.mult, [xdt, smallt], [xde])
                    py = ps[4]
                    for h in range(6):
                        self.mm(py[0:L, h * HD:(h + 1) * HD], wT[0:L, h, 0:L], xdt[0:L, h * HD:(h + 1) * HD], True, True, [wT, xdt], [py])
                    pyi = ps[psi % 4]
                    psi += 1
                    for g in range(2):
                        self.mm(pyi[0:L, g * 192:(g + 1) * 192], Cb[g], stateb[:, g * 192:(g + 1) * 192], True, True, [xcb, stateb], [pyi])
                    self.tt("dve", ytmp[0:L, :].rearrange("p (h d) -> p h d", h=6), pyi[0:L, 0:SSDW].rearrange("p (h d) -> p h d", h=6),
                            smallt[0:L, 6:12].unsqueeze(2).to_broadcast([L, 6, HD]), ALU.mult, [pyi, smallt], [ytmp])
                    self.tt("dve", ytk[0:L, :], py[0:L, 0:SSDW], ytmp[0:L, :], ALU.add, [py, ytmp], [ytk])
                    pu = ps[psi % 4]
                    psi += 1
                    for g in range(2):
                        self.mm(pu[:, g * 192:(g + 1) * 192], btok[0:L, g * 128:(g + 1) * 128], xde[0:L, g * 192:(g + 1) * 192], True, True, [btok, xde], [pu])
                    self.tt("dve", state[:, :].rearrange("p (h d) -> p h d", h=6), state[:, :].rearrange("p (h d) -> p h d", h=6), smallt[:, 12:18].unsqueeze(2).to_broadcast([128, 6, HD]), ALU.mult, [state, smallt], [state])
                    self.tt("dve", state[:, :], state[:, :], pu[:, 0:SSDW], ALU.add, [state, pu], [state])
                    self.cp("pool", stateb[:, :], state[:, :], [state], [stateb])
                    pt_ = ps[psi % 4]
                    psi += 1
                    for g in range(3):
                        self.tr(pt_[:, g * L:(g + 1) * L], ytk[0:L, g * 128:(g + 1) * 128], self.identf[0:L, 0:L], [ytk, self.cst], [pt_])
                    for g in range(3):
                        self.stt("dve", yg[:, g, o:o + L], xc[:, g, o:o + L], pc("dsk", g), pt_[:, g * L:(g + 1) * L], ALU.mult, ALU.add, [xc, pt, pt_], [yg])
                if ti == len(tl) - 1:
                    self.dma("pool", O["ssm_p"][l], state[:, :], [state], [], sbuf=state)
                self.tt("dve", yg[:, :, 0:n], yg[:, :, 0:n], sz[:, :, 0:n], ALU.mult, [yg, sz], [yg])
                self.tt("pool", ysq[:, :, 0:n], yg[:, :, 0:n], yg[:, :, 0:n], ALU.mult, [yg], [ysq])
                pm = ps[6]
                for g in range(3):
                    self.mm(pm[:, 0:n], self.ones_bf, ysq[:, g, 0:n], g == 0, g == 2, [ysq, self.cbf], [pm])
                self.act(yr[:, 0:n], pm[:, 0:n], AF.Ln, [pm], [yr], bias=EPS * SSDW, scale=1.0)
                self.act(yr[:, 0:n], yr[:, 0:n], AF.Exp, [yr], [yr], scale=-0.5)
                for g in range(3):
                    self.stt("dve", mixb[:, g, 0:n], yg[:, g, 0:n], float(np.sqrt(SSDW)), yr[:, 0:n], ALU.mult, ALU.mult, [yg, yr], [mixb])
                self.dma("pool", S["mixT"][0:384, c0:c0 + n].rearrange("(g p) t -> p g t", p=128), mixb[:, :, 0:n], [mixb], [], sbuf=mixb)


    def phase_a_sample(self, l, xs_in, W, rgw):
        nc, P, I, O, S = self.nc, self.P, self.I, self.O, self.S
        pt, der, ps, psb = self.ptab, self.der, self.ps, self.psb
        n = DB
        def pc(name, k=0, nn=1):
            return pt[:, l, PT[name] + k:PT[name] + k + nn]
        with contextlib.ExitStack() as st:
            X = self.sb(st, "sX", [128, 8, n])
            H = self.sb(st, "sH", [128, 8, n], BF16)
            sq = self.sb(st, "ssq", [128, 8, n], BF16)
            rstd = self.sb(st, "srstd", [128, n])
            zz = self.sb(st, "szz", [128, 3, n])
            sz = self.sb(st, "ssz", [128, 3, n])
            xpS = self.sb(st, "sxpS", [128, 7, 4, n])
            xcv = self.sb(st, "sxcv", [128, 7, n])
            xc = self.sb(st, "sxc", [128, 7, n])
            dtT = self.sb(st, "sdtT", [6, 2 * n])
            qk = self.sb(st, "sqk", [128, 6, n])
            qksq = self.sb(st, "sqksq", [128, 6, n], BF16)
            qkr = self.sb(st, "sqkr", [128, 6, n])
            vv = self.sb(st, "svv", [128, 3, n])
            xrS = self.sb(st, "sxrS", [128, 2, 4, n])
            rga = self.sb(st, "srga", [128, 2, n])
            rgb_ = self.sb(st, "srgb", [128, 2, n], BF16)
            rgt = [self.sb(st, f"srgt{i}", [128, 2, n]) for i in range(5)]
            grr = self.sb(st, "sgrr", [128, 2, n])
            h0 = self.sb(st, "sh0", [128, 2, n])
            dE = self.sb(st, "sdE", [128, 3, 2 * n])
            diag = self.sb(st, "sdiag", [128, n, NST])
            bc = [self.sb(st, f"sbc{i}", [128, n, NST]) for i in range(4)]
            St = self.sb(st, "sSt", [128, n, NST])
            tmpS = self.sb(st, "stmpS", [128, n, NST])
            y1 = self.sb(st, "sy1", [128, 3, n])
            cbp = self.sb(st, "scbp", [128, 2, n])
            CB = self.sb(st, "sCB", [128, 2, n])
            xdt = self.sb(st, "sxdt", [128, 3, n])
            yg = self.sb(st, "syg", [128, 3, n])
            ysq = self.sb(st, "sysq", [128, 3, n], BF16)
            yr = self.sb(st, "syr", [128, n])
            mixb = self.sb(st, "smixb", [128, 3, n], BF16)
            mixr = self.sb(st, "smixr", [128, 2, n], BF16)
            psi = 0
            self.dma("sp", X[:, :, :], xs_in[:, :].rearrange("(k p) t -> p k t", p=128), [], [X])
            self.dma("sp", xpS[:, :, 0:3, :], I["s_ssmc"][l].rearrange("g p j b -> p g j b"), [], [xpS])
            self.dma("sp", xrS[:, :, 0:3, :], I["s_rgc"][l].rearrange("g p j b -> p g j b"), [], [xrS])
            self.dma("sp", h0[:, :, :], I["s_rg"][l].rearrange("g p b -> p g b"), [], [h0])
            self.tt("pool", sq[:], X[:], X[:], ALU.mult, [X], [sq])
            self.cp("dve", H[:], X[:], [X], [H])
            pm = ps[6]
            for kc in range(8):
                self.mm(pm[:, 0:n], self.ones1024_bf, sq[:, kc, :], kc == 0, kc == 7, [sq, self.cbf], [pm])
            self.act(rstd[:, :], pm[:, 0:n], AF.Ln, [pm], [rstd], bias=EPS)
            self.act(rstd[:, :], rstd[:, :], AF.Exp, [rstd], [rstd], scale=-0.5)

            def proj(c_start, rows, dst_ap, dstb):
                nonlocal psi
                pp = ps[psi % 4]
                psi += 1
                for kc in range(8):
                    self.mm(pp[0:rows, 0:n], W[:, kc, c_start:c_start + rows], H[:, kc, :], kc == 0, kc == 7, [W, H], [pp])
                self.tt("dve", dst_ap, pp[0:rows, 0:n], rstd[0:rows, :], ALU.mult, [pp, rstd], [dstb])
            for g, (cs, r) in enumerate(CG_Z):
                proj(cs, r, zz[:, g, :], zz)
            for g, (cs, r) in enumerate(CG_XBC):
                proj(cs, r, xpS[:, g, 3, :], xpS)
            proj(CG_DT[0], 6, dtT[0:6, 0:n], dtT)
            for g, (cs, r) in enumerate(CG_Q + CG_K):
                proj(cs, r, qk[:, g, :], qk)
            for g, (cs, r) in enumerate(CG_V):
                proj(cs, r, vv[:, g, :], vv)
            for g, (cs, r) in enumerate(CG_XR):
                proj(cs, r, xrS[:, g, 3, :], xrS)
            for g, (cs, r) in enumerate(CG_GR):
                proj(cs, r, grr[:, g, :], grr)
            self.dma("pool", O["ssmc_s"][l].rearrange("g p j b -> p g j b"), xpS[:, :, 1:4, :], [xpS], [], sbuf=xpS)
            self.dma("pool", O["rgc_s"][l].rearrange("g p j b -> p g j b"), xrS[:, :, 1:4, :], [xrS], [], sbuf=xrS)
            self.act(sz[:], zz[:], AF.Exp, [zz], [sz], scale=-1.0)
            self.act(sz[:], sz[:], AF.Ln, [sz], [sz], bias=1.0)
            self.act(sz[:], sz[:], AF.Exp, [sz], [sz], scale=-1.0)
            self.tt("pool", sz[:], sz[:], zz[:], ALU.mult, [sz, zz], [sz])
            for g in range(7):
                self.ts("dve", xcv[:, g, :], xpS[:, g, 0, :], pc("cwx", 4 * g), pc("cbx", g), ALU.mult, ALU.add, [xpS, pt], [xcv])
                for j in range(1, 4):
                    self.stt("dve", xcv[:, g, :], xpS[:, g, j, :], pc("cwx", 4 * g + j), xcv[:, g, :], ALU.mult, ALU.add, [xpS, pt, xcv], [xcv])
            self.act(xc[:], xcv[:], AF.Exp, [xcv], [xc], scale=-1.0)
            self.act(xc[:], xc[:], AF.Ln, [xc], [xc], bias=1.0)
            self.act(xc[:], xc[:], AF.Exp, [xc], [xc], scale=-1.0)
            self.tt("dve", xc[:], xc[:], xcv[:], ALU.mult, [xc, xcv], [xc])
            self.act(dtT[0:6, 0:n], dtT[0:6, 0:n], AF.Exp, [dtT, pt], [dtT], bias=pt[0:6, l, PT["dtb"]:PT["dtb"] + 1])
            self.act(dtT[0:6, 0:n], dtT[0:6, 0:n], AF.Ln, [dtT], [dtT], bias=1.0)
            self.act(dtT[0:6, n:2 * n], dtT[0:6, 0:n], AF.Exp, [dtT, der], [dtT], scale=der[0:6, l, 0:1])
            pe_ = ps[4]
            for g3 in range(3):
                self.mm(pe_[:, g3 * 2 * n:(g3 + 1) * 2 * n], self.sel6[0:6, g3, :], dtT[0:6, :], True, True, [self.sel6, dtT], [pe_])
            self.cp("dve", dE[:].rearrange("p g c -> p (g c)"), pe_[:, 0:6 * n], [pe_], [dE])
            self.tt("pool", qksq[:], qk[:], qk[:], ALU.mult, [qk], [qksq])
            for g in range(6):
                pp = ps[psi % 4]
                psi += 1
                self.mm(pp[:, 0:n], self.bo[:, :], qksq[:, g, :], True, True, [self.bo, qksq], [pp])
                self.act(qkr[:, g, :], pp[:, 0:n], AF.Ln, [pp], [qkr], bias=EPS)
            self.act(qkr[:], qkr[:], AF.Exp, [qkr], [qkr], scale=-0.5)
            for g in range(6):
                gain = der[:, l, 9:10] if g < 3 else pc("kn")
                self.stt("dve", qkr[:, g, :], qk[:, g, :], gain, qkr[:, g, :], ALU.mult, ALU.mult, [qk, qkr, der, pt], [qkr])
            self.dma("pool", S["qs"][:, :].rearrange("(g p) b -> p g b", p=128), qkr[:, 0:3, :], [qkr], [], sbuf=qkr)
            self.dma("pool", O["kT_s"][l].rearrange("(g p) b -> p g b", p=128), qkr[:, 3:6, :], [qkr], [], sbuf=qkr)
            self.dma("pool", O["vT_s"][l].rearrange("(g p) b -> p g b", p=128), vv[:, :, :], [vv], [], sbuf=vv)
            for g in range(2):
                self.ts("dve", rga[:, g, :], xrS[:, g, 0, :], pc("cwr", 4 * g), pc("cbr", g), ALU.mult, ALU.add, [xrS, pt], [rga])
                for j in range(1, 4):
                    self.stt("dve", rga[:, g, :], xrS[:, g, j, :], pc("cwr", 4 * g + j), rga[:, g, :], ALU.mult, ALU.add, [xrS, pt, rga], [rga])
            self.cp("dve", rgb_[:], rga[:], [rga], [rgb_])
            R_, I_, A_, U_, G_ = rgt
            for g in range(2):
                for (ai, dst, nb) in ((0, R_, 5), (1, I_, 7)):
                    pp = ps[psi % 4]
                    psi += 1
                    self.mm(pp[:, 0:n], rgw[:, ai * 2 + g, :], rgb_[:, g, :], True, True, [rgw, rgb_], [pp])
                    self.act(dst[:, g, :], pp[:, 0:n], AF.Exp, [pp, der], [dst], bias=der[:, l, nb + g:nb + g + 1], scale=-1.0)
            for dst in (R_, I_):
                self.act(dst[:], dst[:], AF.Ln, [dst], [dst], bias=1.0)
                self.act(dst[:], dst[:], AF.Exp, [dst], [dst], scale=-1.0)
            for g in range(2):
                self.act(A_[:, g, :], R_[:, g, :], AF.Exp, [R_, der], [A_], scale=der[:, l, 1 + g:2 + g])
                self.act(U_[:, g, :], R_[:, g, :], AF.Exp, [R_, der], [U_], scale=der[:, l, 3 + g:4 + g])
            self.ts("dve", U_[:], U_[:], -1.0, 1.0, ALU.mult, ALU.add, [U_], [U_])
            self.act(U_[:], U_[:], AF.Ln, [U_], [U_])
            self.act(U_[:], U_[:], AF.Exp, [U_], [U_], scale=0.5)
            self.tt("dve", U_[:], U_[:], I_[:], ALU.mult, [U_, I_], [U_])
            self.tt("dve", U_[:], U_[:], rga[:], ALU.mult, [U_, rga], [U_])
            self.tt("dve", R_[:], A_[:], h0[:], ALU.mult, [A_, h0], [R_])
            self.tt("dve", R_[:], R_[:], U_[:], ALU.add, [R_, U_], [R_])
            self.dma("pool", O["rg_s"][l].rearrange("g p b -> p g b"), R_[:, :, :], [R_], [], sbuf=R_)
            self.tt("pool", G_[:], grr[:], grr[:], ALU.mult, [grr], [G_])
            self.ts("pool", G_[:], G_[:], 0.044715, 1.0, ALU.mult, ALU.add, [G_], [G_])
            self.tt("pool", G_[:], G_[:], grr[:], ALU.mult, [G_, grr], [G_])
            self.act(G_[:], G_[:], AF.Exp, [G_], [G_], scale=-2.0 * 0.7978845608028654)
            self.act(G_[:], G_[:], AF.Ln, [G_], [G_], bias=1.0)
            self.act(G_[:], G_[:], AF.Exp, [G_], [G_], scale=-1.0)
            self.tt("pool", G_[:], G_[:], grr[:], ALU.mult, [G_, grr], [G_])
            self.tt("dve", I_[:], R_[:], G_[:], ALU.mult, [R_, G_], [I_])
            self.tt("pool", A_[:], I_[:], I_[:], ALU.mult, [I_], [A_])
            self.cp("pool", rgb_[:], A_[:], [A_], [rgb_])
            pm = ps[6]
            for g in range(2):
                self.mm(pm[:, 0:n], self.ones_bf, rgb_[:, g, :], g == 0, g == 1, [rgb_, self.cbf], [pm])
            self.act(yr[:, :], pm[:, 0:n], AF.Ln, [pm], [yr], bias=EPS * RGW, scale=1.0)
            self.act(yr[:, :], yr[:, :], AF.Exp, [yr], [yr], scale=-0.5)
            for g in range(2):
                self.stt("dve", mixr[:, g, :], I_[:, g, :], 16.0, yr[:, :], ALU.mult, ALU.mult, [I_, yr], [mixr])
            self.dma("pool", S["mixs"][768:1024, :].rearrange("(g p) t -> p g t", p=128), mixr[:, :, :], [mixr], [], sbuf=mixr)
            for i, gsrc in enumerate((5, 6, 3, 4)):
                self.tt("dve", diag[:, :, :], self.identf.unsqueeze(1).to_broadcast([128, n, NST]),
                        xc[:, gsrc, :].unsqueeze(2).to_broadcast([128, n, NST]), ALU.mult, [self.cst, xc], [diag])
                for c8 in range(8):
                    pp = ps[psi % 4]
                    psi += 1
                    self.mm(pp[:, 0:512], self.cf[:, 0:128], diag[:, 4 * c8:4 * c8 + 4, :].rearrange("p b n -> p (b n)"), True, True, [self.cf, diag], [pp])
                    self.cp("act" if c8 % 2 else "dve", bc[i][:, 4 * c8:4 * c8 + 4, :].rearrange("p b n -> p (b n)"), pp[:, 0:512], [pp], [bc[i]])
            self.tt("dve", cbp[:, :, :], xc[:, 5:7, :], xc[:, 3:5, :], ALU.mult, [xc], [cbp])
            pp = ps[psi % 4]
            psi += 1
            self.mm(pp[:, 0:2 * n], self.cf[:, 0:128], cbp[:, :, :].rearrange("p g b -> p (g b)"), True, True, [self.cf, cbp], [pp])
            self.cp("dve", CB[:].rearrange("p g b -> p (g b)"), pp[:, 0:2 * n], [pp], [CB])
            for g3 in range(3):
                self.tt("dve", xdt[:, g3, :], xc[:, g3, :], dE[:, g3, 0:n], ALU.mult, [xc, dE], [xdt])
            for g3 in range(3):
                self.dma("sp", St[:, :, :], I["s_ssm"][l, g3], [], [St])
                for hf in range(2):
                    r0 = 64 * hf
                    gr = (2 * g3 + hf) // 3
                    rs = slice(r0, r0 + 64)
                    self.tt("pool", tmpS[rs, :, :], St[rs, :, :], bc[gr][rs, :, :], ALU.mult, [St, bc[gr]], [tmpS])
                    self.P.op("dve", lambda e, rs=rs, g3=g3: e.tensor_reduce(out=y1[rs, g3, :], in_=tmpS[rs, :, :], axis=AX.X, op=ALU.add), [tmpS], [y1])
                    self.tt("dve", St[rs, :, :], St[rs, :, :], dE[rs, g3, n:2 * n].unsqueeze(2).to_broadcast([64, n, NST]), ALU.mult, [St, dE], [St])
                    self.tt("pool", tmpS[rs, :, :], bc[2 + gr][rs, :, :], xdt[rs, g3, :].unsqueeze(2).to_broadcast([64, n, NST]), ALU.mult, [bc[2 + gr], xdt, tmpS], [tmpS])
                    self.tt("dve", St[rs, :, :], St[rs, :, :], tmpS[rs, :, :], ALU.add, [St, tmpS], [St])
                    self.tt("dve", y1[rs, g3, :], y1[rs, g3, :], dE[rs, g3, n:2 * n], ALU.mult, [y1, dE], [y1])
                    self.tt("dve", yg[rs, g3, :], CB[rs, gr, :], xdt[rs, g3, :], ALU.mult, [CB, xdt], [yg])
                    self.tt("dve", yg[rs, g3, :], yg[rs, g3, :], y1[rs, g3, :], ALU.add, [yg, y1], [yg])
                self.dma("pool", O["ssm_s"][l, g3], St[:, :, :], [St], [], sbuf=St)
                self.stt("dve", yg[:, g3, :], xc[:, g3, :], pc("dsk", g3), yg[:, g3, :], ALU.mult, ALU.add, [xc, pt, yg], [yg])
            self.tt("dve", yg[:], yg[:], sz[:], ALU.mult, [yg, sz], [yg])
            self.tt("pool", ysq[:], yg[:], yg[:], ALU.mult, [yg], [ysq])
            pm = ps[6]
            for g in range(3):
                self.mm(pm[:, 0:n], self.ones_bf, ysq[:, g, :], g == 0, g == 2, [ysq, self.cbf], [pm])
            self.act(yr[:, :], pm[:, 0:n], AF.Ln, [pm], [yr], bias=EPS * SSDW, scale=1.0)
            self.act(yr[:, :], yr[:, :], AF.Exp, [yr], [yr], scale=-0.5)
            for g in range(3):
                self.stt("dve", mixb[:, g, :], yg[:, g, :], float(np.sqrt(SSDW)), yr[:, :], ALU.mult, ALU.mult, [yg, yr], [mixb])
            self.dma("pool", S["mixs"][0:384, :].rearrange("(g p) t -> p g t", p=128), mixb[:, :, :], [mixb], [], sbuf=mixb)
            self.P.barrier()

    def phase_b_sample(self, l):
        nc, P, I, O, S = self.nc, self.P, self.I, self.O, self.S
        pt, ps = self.ptab, self.ps
        n = DB
        NP = self.NPAGES
        with contextlib.ExitStack() as st:
            hsel = self.sb(st, "hsel", [128, 3, HD])
            qs = self.sb(st, "qs", [128, 3, n])
            sbh = self.sb(st, "sbh", [128, DEPTH])
            pgi = self.sb(st, "pgi", [NP, n], I32)
            qsel = self.sb(st, "qsel", [HD, n])
            qd = self.sb(st, "qd", [HD, n, HD])
            qb = self.sb(st, "qb", [NP, n, HD])
            Kg = [self.sb(st, f"Kg{i}", [NP, 128, HD]) for i in range(2)]
            Vg = [self.sb(st, f"Vg{i}", [NP, 128, HD]) for i in range(2)]
            z = self.sb(st, "az", [NP, 128])
            E = self.sb(st, "aE", [NP, 128])
            Lp = self.sb(st, "aLp", [NP, 128])
            Fc = self.sb(st, "aF", [NP, 128])
            Wt = self.sb(st, "aW", [NP, 128])
            nb = self.sb(st, "anb", [NP, 1])
            Rr = self.sb(st, "aR", [NP, HD])
            osb = self.sb(st, "aosb", [HD, n])
            oall = self.sb(st, "aoall", [128, 3, n])
            osq = self.sb(st, "aosq", [128, 3, n], BF16)
            orr = self.sb(st, "aorr", [128, n])
            omix = self.sb(st, "aomix", [128, 3, n], BF16)
            self.dma("sp", hsel[:], I["hsel"].rearrange("g p d -> p g d"), [], [hsel])
            self.dma("sp", qs[:], S["qs"][:, :].rearrange("(g p) b -> p g b", p=128), [], [qs])
            self.dma("sp", sbh[:], I["sbh"][:, :], [], [sbh])
            self.dma("sp", pgi[:], I["ptab_pg"][:, :], [], [pgi])
            pq = ps[0]
            for g in range(3):
                self.mm(pq[0:HD, 0:n], hsel[:, g, :], qs[:, g, :], g == 0, g == 2, [hsel, qs], [pq])
            self.cp("dve", qsel[:, :], pq[0:HD, 0:n], [pq], [qsel])
            self.tt("dve", qd[:, :, :], self.identf[0:HD, 0:HD].unsqueeze(1).to_broadcast([HD, n, HD]),
                    qsel[:, :].unsqueeze(2).to_broadcast([HD, n, HD]), ALU.mult, [self.cst, qsel], [qd])
            for c4 in range(4):
                pp = ps[1 + c4 % 2]
                self.mm(pp[0:NP, 0:512], self.cf[0:HD, 0:NP], qd[:, 8 * c4:8 * c4 + 8, :].rearrange("p b d -> p (b d)"), True, True, [self.cf, qd], [pp])
                self.cp("dve", qb[:, 8 * c4:8 * c4 + 8, :].rearrange("p b d -> p (b d)"), pp[0:NP, 0:512], [pp], [qb])
            po = ps[3]
            bias = sbh[0:NP, l:l + 1]
            for b in range(n):
                KG, VG = Kg[b % 2], Vg[b % 2]
                import os
                if os.environ.get("SKIP_GATHER"):
                    self.memset("pool", KG[:, :, :], 0.0, [KG])
                    self.memset("pool", VG[:, :, :], 0.0, [VG])
                else:
                  self.P.dma("pool", lambda e, KG=KG, b=b: e.indirect_dma_start(out=KG[:, :, :].rearrange("p s d -> p (s d)"), out_offset=None, in_=I["ck"][l][:, :],
                           in_offset=bass.IndirectOffsetOnAxis(ap=pgi[:, b:b + 1], axis=0)), [pgi], [KG])
                  self.P.dma("pool", lambda e, VG=VG, b=b: e.indirect_dma_start(out=VG[:, :, :].rearrange("p s d -> p (s d)"), out_offset=None, in_=I["cv"][l][:, :],
                           in_offset=bass.IndirectOffsetOnAxis(ap=pgi[:, b:b + 1], axis=0)), [pgi], [VG])
                self.tt("pool", KG[:, :, :], KG[:, :, :], qb[:, b, :].unsqueeze(1).to_broadcast([NP, 128, HD]), ALU.mult, [KG, qb], [KG])
                self.P.op("dve", lambda e, KG=KG: e.tensor_reduce(out=z[:, :], in_=KG[:, :, :], axis=AX.X, op=ALU.add), [KG], [z])
                self.act(E[:, :], z[:, :], AF.Exp, [z, sbh], [E], bias=bias)
                self.act(Lp[:, :], E[:, :], AF.Ln, [E], [Lp], bias=1.0)
                self.P.op("dve", lambda e: e.tensor_tensor_scan(out=Fc[:, :], data0=Lp[:, :], data1=self.cf[0:NP, 256:384], initial=0.0, op0=ALU.add, op1=ALU.add),
                          [Lp, self.cf], [Fc])
                pc_ = ps[4]
                self.mm(pc_[0:NP, 0:1], self.triU_f[0:NP, 0:NP], Fc[:, 127:128], True, True, [self.cst, Fc], [pc_])
                self.stt("dve", nb[:, :], pc_[0:NP, 0:1], -1.0, bias, ALU.mult, ALU.add, [pc_, sbh], [nb])
                self.tt("dve", z[:, :], z[:, :], Fc[:, :], ALU.add, [z, Fc], [z])
                self.tt("dve", z[:, :], z[:, :], Lp[:, :], ALU.subtract, [z, Lp], [z])
                self.act(Wt[:, :], z[:, :], AF.Exp, [z, nb], [Wt], bias=nb[:, 0:1])
                self.tt("pool", VG[:, :, :], VG[:, :, :], Wt[:, :].unsqueeze(2).to_broadcast([NP, 128, HD]), ALU.mult, [VG, Wt], [VG])
                self.P.op("dve", lambda e, VG=VG: e.tensor_reduce(out=Rr[:, :], in_=VG[:, :, :].rearrange("p s d -> p d s"), axis=AX.X, op=ALU.add), [VG], [Rr])
                self.mm(po[0:HD, b:b + 1], Rr[:, :], self.cf[0:NP, 0:1], True, True, [Rr, self.cf], [po])
            self.cp("dve", osb[:, :], po[0:HD, 0:n], [po], [osb])
            self.dma("pool", S["ag_in"][l][:, :], osb[:, :], [osb], [], sbuf=osb)
            P.barrier()
            import os
            if os.environ.get("SKIP_CC"):
                self.dma("pool", S["ag_out"][l][0:HD, :], osb[:, :], [osb], [], sbuf=osb)
            else:
              tok = P.cc(lambda e: e.collective_compute("AllGather", ALU.bypass, replica_groups=[list(range(NCORES))],
                                                      ins=[S["ag_in"][l].opt()], outs=[S["ag_out"][l].opt()]), f"ccsem{l}")
            P.barrier()
            self.dma("sp", oall[:], S["ag_out"][l][0:SSDW, :].rearrange("(g p) b -> p g b", p=128), [], [oall])
            self.tt("pool", osq[:], oall[:], oall[:], ALU.mult, [oall], [osq])
            pm = ps[6]
            for g in range(3):
                self.mm(pm[:, 0:n], self.ones_bf, osq[:, g, :], g == 0, g == 2, [osq, self.cbf], [pm])
            self.act(orr[:, :], pm[:, 0:n], AF.Ln, [pm], [orr], bias=EPS * SSDW)
            self.act(orr[:, :], orr[:, :], AF.Exp, [orr], [orr], scale=-0.5)
            for g in range(3):
                self.stt("dve", omix[:, g, :], oall[:, g, :], float(np.sqrt(SSDW)), orr[:, :], ALU.mult, ALU.mult, [oall, orr], [omix])
            self.dma("pool", S["mixs"][384:768, :].rearrange("(g p) t -> p g t", p=128), omix[:, :, :], [omix], [], sbuf=omix)

    def phase_b(self, l):
        nc, P, I, O, S = self.nc, self.P, self.I, self.O, self.S
        pt = self.ptab
        T, NB = self.T, self.NB
        with contextlib.ExitStack() as st:
            KA = self.sb(st, "KA", [128, NH, T], BF16)
            Vs = self.sb(st, "Vs", [128, NB, SSDW], BF16)
            for h in range(NH):
                for c0 in range(0, T, 4096):
                    c1 = min(T, c0 + 4096)
                    self.memset("pool", KA[0:96, h, c0:c1], 0.0, [KA])
                    self.memset("pool", KA[64:65, h, c0:c1], 1.0, [KA])
            for h in range(NH):
                self.dma("sp" if h % 2 == 0 else "act", KA[0:64, h, :], S["KT"][h // 2, 64 * (h % 2):64 * (h % 2) + 64, :], [], [KA])
            for j0 in range(1, NB, 16):
                j1 = min(NB, j0 + 16)
                self.dma("act", Vs[:, j0:j1, :], S["Vt"][NMETA + 128 * (j0 - 1):NMETA + 128 * (j1 - 1), :].rearrange("(j p) f -> p j f", p=128), [], [Vs])
            self.dma("act", Vs[0:NMETA, 0, :], S["Vt"][0:NMETA, :], [], [Vs])
            QW = 512
            QA = [[self.sb(st, f"QA{a_}{b_}", [128, QW], BF16) for b_ in range(3)] for a_ in range(2)]
            for a_ in range(2):
                for b_ in range(3):
                    self.memset("pool", QA[a_][b_][0:96, :], 0.0, [QA[a_][b_]])
            csel = self.sb(st, "csel", [128, 65], BF16)
            self.memset("pool", csel[:, :], 0.0, [csel])
            self.memset("pool", csel[:, 64:65], 1.0, [csel])
            Es = [self.sb(st, f"Es{i}", [128, QW]) for i in range(2)]
            Ls = [self.sb(st, f"Ls{i}", [128, QW], BF16) for i in range(3)]
            Lf = self.sb(st, "Lf", [128, QW])
            Ws = [self.sb(st, f"Ws{i}", [128, QW], BF16) for i in range(3)]
            Wf = self.sb(st, "Wf", [128, QW], BF16)
            osb = [self.sb(st, f"osb{h}", [HD, QW]) for h in range(NH)]
            osq = self.sb(st, "osq", [HD, QW], BF16)
            orr = self.sb(st, "orr", [HD, QW])
            omix = [self.sb(st, f"omix{i}", [HD, QW], BF16) for i in range(2)]
            maskd = self.sb(st, "maskd", [128, 4, QW], BF16)
            self.memset("pool", maskd[:], 0.0, [maskd])
            for jj in range(4):
                self.cp("pool", maskd[:, jj, 128 * jj:128 * (jj + 1)], self.tri_st, [self.cst], [maskd])
                if jj < 3:
                    self.memset("pool", maskd[:, jj, 128 * (jj + 1):QW], 1.0, [maskd])
            ps = self.ps
            PS_S = [ps[0], ps[1]]
            PS_C = [ps[2], ps[3]]
            PS_cs = ps[4]
            PS_O = ps[5]
            PS_m = ps[6]
            itS = 0
            itC = 0
            hcount = 0
            qgroups = [(0, NMETA)] + [(NMETA + QW * i, QW) for i in range(self.SEQ // QW)]
            for qi, (q0, qn) in enumerate(qgroups):
                kbl = []
                if qi == 0:
                    kbl.append((0, NMETA, 0, "meta"))
                else:
                    b_hi = 4 * qi
                    for jj in range(3, -1, -1):
                        kbl.append((b_hi - 3 + jj, 128, NMETA + 128 * (b_hi - 3 + jj - 1), jj))
                    for j in range(b_hi - 4, 0, -1):
                        kbl.append((j, 128, NMETA + 128 * (j - 1), None))
                    kbl.append((0, NMETA, 0, None))
                nk = len(kbl)
                for h in range(NH):
                    pr, hb = h // 2, 64 * (h % 2)
                    Qh = QA[hcount % 2]
                    hcount += 1
                    bias = pt[:, l, PT["sbb"] + h:PT["sbb"] + h + 1]
                    for b_ in range(3):
                        self.dma("sp", Qh[b_][0:64, 0:qn], S["QT"][pr, hb:hb + 64, q0:q0 + qn], [], [Qh[b_]])
                    self.memset("pool", Qh[0][64:65, 0:qn], 0.0, [Qh[0]])
                    pCs = {}
                    for k in range(nk + 2):
                        if k < nk:
                            kb, Pn, kc0, mk = kbl[k]
                            pS = PS_S[itS % 2]
                            E = Es[itS % 2]
                            itS += 1
                            Lb = Ls[k % 3]
                            Qk = Qh[k % 3]
                            self.mm(pS[0:Pn, 0:qn], KA[0:64, h, kc0:kc0 + Pn], Qk[0:64, 0:qn], True, True, [KA, Qk], [pS])
                            self.act(E[0:Pn, 0:qn], pS[0:Pn, 0:qn], AF.Exp, [pS, pt], [E], bias=bias[0:Pn, :])
                            if mk is None:
                                self.act(Lb[0:Pn, 0:qn], E[0:Pn, 0:qn], AF.Ln, [E], [Lb], bias=1.0)
                            else:
                                self.act(Lf[0:Pn, 0:qn], E[0:Pn, 0:qn], AF.Ln, [E], [Lf], bias=1.0)
                                map_ = self.tri_st[0:Pn, 0:qn] if mk == "meta" else maskd[:, mk, 0:qn]
                                self.tt("dve", Lb[0:Pn, 0:qn], Lf[0:Pn, 0:qn], map_, ALU.mult, [Lf, maskd, self.cst], [Lb])
                            if k < nk - 1:
                                Qn = Qh[(k + 1) % 3]
                                self.mm(PS_cs[0:65, 0:qn], csel[0:Pn, 0:65], Lb[0:Pn, 0:qn], k == 0, True, [Lb, csel], [PS_cs])
                                self.ts("dve", Qn[64:65, 0:qn], PS_cs[64:65, 0:qn], -1.0, None, ALU.mult, None, [PS_cs], [Qn])
                        if 1 <= k <= nk:
                            k1 = k - 1
                            kb, Pn, kc0, mk = kbl[k1]
                            pC = PS_C[itC % 2]
                            itC += 1
                            Lb = Ls[k1 % 3]
                            Wb = Ws[k1 % 3]
                            Qk = Qh[k1 % 3]
                            self.mm(pC[0:Pn, 0:qn], KA[0:65, h, kc0:kc0 + Pn], Qk[0:65, 0:qn], True, False, [KA, Qk], [pC])
                            self.mm(pC[0:Pn, 0:qn], self.ntriU_bf[0:Pn, 0:Pn], Lb[0:Pn, 0:qn], False, True, [Lb, self.cbf], [pC])
                            if mk is None:
                                self.act(Wb[0:Pn, 0:qn], pC[0:Pn, 0:qn], AF.Exp, [pC, pt], [Wb], bias=bias[0:Pn, :])
                            else:
                                self.act(Wf[0:Pn, 0:qn], pC[0:Pn, 0:qn], AF.Exp, [pC, pt], [Wf], bias=bias[0:Pn, :])
                                map_ = self.tri_st[0:Pn, 0:qn] if mk == "meta" else maskd[:, mk, 0:qn]
                                self.tt("pool", Wb[0:Pn, 0:qn], Wf[0:Pn, 0:qn], map_, ALU.mult, [Wf, maskd, self.cst], [Wb])
                        if k >= 2:
                            k2 = k - 2
                            kb, Pn, kc0, mk = kbl[k2]
                            Wb = Ws[k2 % 3]
                            self.mm(PS_O[0:HD, 0:qn], Vs[0:Pn, kb, h * HD:(h + 1) * HD], Wb[0:Pn, 0:qn], k2 == 0, k2 == nk - 1, [Vs, Wb], [PS_O])
                    self.cp("dve", osb[h][:, 0:qn], PS_O[0:HD, 0:qn], [PS_O], [osb[h]])
                    self.tt("pool", osq[:, 0:qn], osb[h][:, 0:qn], osb[h][:, 0:qn], ALU.mult, [osb[h]], [osq])
                    self.mm(PS_m[0:HD, 0:qn], self.ones_bf[0:HD, 0:HD], osq[:, 0:qn], h == 0, h == NH - 1, [osq, self.cbf], [PS_m])
                self.act(orr[:, 0:qn], PS_m[0:HD, 0:qn], AF.Ln, [PS_m], [orr], bias=EPS * SSDW)
                self.act(orr[:, 0:qn], orr[:, 0:qn], AF.Exp, [orr], [orr], scale=-0.5)
                for h in range(NH):
                    om = omix[h % 2]
                    self.stt("dve", om[:, 0:qn], osb[h][:, 0:qn], float(np.sqrt(SSDW)), orr[:, 0:qn], ALU.mult, ALU.mult, [osb[h], orr], [om])
                    self.dma("sp", S["mixT"][384 + HD * h:384 + HD * (h + 1), q0:q0 + qn], om[:, 0:qn], [om], [], sbuf=om)

    def phase_c(self, l, x_in, x_out, xs_in, xs_out):
        nc, P, I, O, S = self.nc, self.P, self.I, self.O, self.S
        pt = self.ptab
        def pc(name, k=0, n=1):
            return pt[:, l, PT[name] + k:PT[name] + k + n]
        with contextlib.ExitStack() as st:
            gcols = [PT["ssdn"], PT["ssdn"] + 1, PT["ssdn"] + 2, PT["sbn"], PT["sbn"] + 1, PT["sbn"] + 2, PT["rgn"], PT["rgn"] + 1]
            Wo = self.load_cast(st, "wo", I["w_out"][l], D, 8, scale_cols=gcols, l=l, chunk=1024)
            Wu = self.load_cast(st, "wup", I["w_up"][l], 2 * DFF, 8, scale_cols=PT["g2"], l=l, chunk=1408)
            Wd = self.load_cast(st, "wdn", I["w_dn"][l], D, NFC, chunk=1024)
            NW = 256
            X = self.sb(st, "cX", [128, 8, NW])
            MX = self.sb(st, "cMX", [128, 8, NW], BF16)
            H2 = self.sb(st, "cH2", [128, 8, NW], BF16)
            sq = self.sb(st, "csq", [128, 8, NW], BF16)
            rstd = self.sb(st, "crstd", [128, NW])
            AB = self.sb(st, "cAB", [128, NFC, NW], BF16)
            gp = [self.sb(st, f"cgp{i}", [128, 2 + NW]) for i in range(2)]
            gcv = [self.sb(st, f"cgcv{i}", [128, NW]) for i in range(2)]
            gs = [self.sb(st, f"cgs{i}", [128, NW]) for i in range(2)]
            uu = [self.sb(st, f"cuu{i}", [128, NW]) for i in range(2)]
            halo = self.sb(st, "chalo", [128, NFC, 2])
            self.memset("pool", halo[:], 0.0, [halo])
            ps = self.ps
            psi = 0
            tl = [(c0, n, False) for (c0, n) in self.tiles(NW)]
            n_prompt = len(tl)
            if self.do_sample:
                tl.append((0, DB, True))
                ffS = self.sb(st, "cffS", [128, NFC, 3, DB])
                self.dma("act", ffS[:, :, 0:2, :], I["s_ffc"][l].rearrange("f p j b -> p f j b"), [], [ffS])
            for ti, (c0, n, smp) in enumerate(tl):
                last = ti == n_prompt - 1
                xsrc = xs_in[:, :] if smp else x_in[:, c0:c0 + n]
                msrc = S["mixs"][:, :] if smp else S["mixT"][:, c0:c0 + n]
                xdst = xs_out[:, :] if smp else x_out[:, c0:c0 + n]
                self.dma("sp", X[:, :, 0:n], xsrc.rearrange("(k p) t -> p k t", p=128), [], [X])
                self.dma("act", MX[:, :, 0:n], msrc.rearrange("(k p) t -> p k t", p=128), [], [MX])
                for dc in range(8):
                    pp = ps[psi % 6]
                    psi += 1
                    for mc in range(8):
                        self.mm(pp[:, 0:n], Wo[:, mc, dc * 128:(dc + 1) * 128], MX[:, mc, 0:n], mc == 0, mc == 7, [Wo, MX], [pp])
                    self.tt("dve", X[:, dc, 0:n], X[:, dc, 0:n], pp[:, 0:n], ALU.add, [X, pp], [X])
                self.tt("pool", sq[:, :, 0:n], X[:, :, 0:n], X[:, :, 0:n], ALU.mult, [X], [sq])
                self.cp("dve", H2[:, :, 0:n], X[:, :, 0:n], [X], [H2])
                pm = ps[6]
                for kc in range(8):
                    self.mm(pm[:, 0:n], self.ones1024_bf, sq[:, kc, 0:n], kc == 0, kc == 7, [sq, self.cbf], [pm])
                self.act(rstd[:, 0:n], pm[:, 0:n], AF.Ln, [pm], [rstd], bias=EPS)
                self.act(rstd[:, 0:n], rstd[:, 0:n], AF.Exp, [rstd], [rstd], scale=-0.5)
                for fc in range(NFC):
                    G, GC, GS, U = gp[fc % 2], gcv[fc % 2], gs[fc % 2], uu[fc % 2]
                    pg = ps[psi % 6]
                    pu = ps[(psi + 1) % 6]
                    psi += 2
                    for kc in range(8):
                        self.mm(pg[:, 0:n], Wu[:, kc, fc * 128:(fc + 1) * 128], H2[:, kc, 0:n], kc == 0, kc == 7, [Wu, H2], [pg])
                    for kc in range(8):
                        self.mm(pu[:, 0:n], Wu[:, kc, DFF + fc * 128:DFF + (fc + 1) * 128], H2[:, kc, 0:n], kc == 0, kc == 7, [Wu, H2], [pu])
                    if smp:
                        self.tt("dve", ffS[:, fc, 2, :], pg[:, 0:n], rstd[:, 0:n], ALU.mult, [pg, rstd], [ffS])
                        taps = [ffS[:, fc, j, :] for j in range(3)]
                        TR = [ffS, pt]
                    else:
                        self.cp("pool", G[:, 0:2], halo[:, fc, :], [halo], [G])
                        self.tt("dve", G[:, 2:2 + n], pg[:, 0:n], rstd[:, 0:n], ALU.mult, [pg, rstd], [G])
                        self.cp("pool", halo[:, fc, :], G[:, n:n + 2], [G], [halo])
                        taps = [G[:, j:j + n] for j in range(3)]
                        TR = [G, pt]
                    self.ts("dve", GC[:, 0:n], taps[0], pc("cwf", 3 * fc), pc("cbf", fc), ALU.mult, ALU.add, TR, [GC])
                    for j in range(1, 3):
                        self.stt("dve", GC[:, 0:n], taps[j], pc("cwf", 3 * fc + j), GC[:, 0:n], ALU.mult, ALU.add, TR + [GC], [GC])
                    self.act(GS[:, 0:n], GC[:, 0:n], AF.Exp, [GC], [GS], scale=-1.0)
                    self.act(GS[:, 0:n], GS[:, 0:n], AF.Ln, [GS], [GS], bias=1.0)
                    self.act(GS[:, 0:n], GS[:, 0:n], AF.Exp, [GS], [GS], scale=-1.0)
                    self.tt("dve", U[:, 0:n], pu[:, 0:n], rstd[:, 0:n], ALU.mult, [pu, rstd], [U])
                    self.tt("pool", GS[:, 0:n], GS[:, 0:n], GC[:, 0:n], ALU.mult, [GS, GC], [GS])
                    self.tt("pool", AB[:, fc, 0:n], GS[:, 0:n], U[:, 0:n], ALU.mult, [GS, U], [AB])
                if last:
                    self.dma("pool", O["ffc_p"][l].rearrange("f p j -> p f j"), halo[:, :, :], [halo], [], sbuf=halo)
                if smp:
                    self.dma("pool", O["ffc_s"][l].rearrange("f p j b -> p f j b"), ffS[:, :, 1:3, :], [ffS], [], sbuf=ffS)
                for dc in range(8):
                    pp = ps[psi % 6]
                    psi += 1
                    for fc in range(NFC):
                        self.mm(pp[:, 0:n], Wd[:, fc, dc * 128:(dc + 1) * 128], AB[:, fc, 0:n], fc == 0, fc == NFC - 1, [Wd, AB], [pp])
                    self.tt("dve", X[:, dc, 0:n], X[:, dc, 0:n], pp[:, 0:n], ALU.add, [X, pp], [X])
                self.dma("pool", xdst.rearrange("(k p) t -> p k t", p=128), X[:, :, 0:n], [X], [], sbuf=X)


def _consts():
    c = np.zeros((128, 1024), np.float32)
    idx = np.arange(128)
    c[:, 0:128] = np.eye(128, dtype=np.float32)
    c[:, 128:256] = (idx[:, None] >= idx[None, :]).astype(np.float32)
    c[:, 256:384] = (idx[None, :] >= idx[:, None]).astype(np.float32)
    c[:, 384:512] = (idx[None, :] > idx[:, None]).astype(np.float32)
    return c


def _ptab(inp):
    pt = np.zeros((DEPTH, 128, NPT), np.float32)
    f = lambda k: np.asarray(inp[k], np.float32)
    for l in range(DEPTH):
        pt[l, :, PT["g1"]:PT["g1"] + 8] = f("norm1")[l].reshape(8, 128).T
        pt[l, :, PT["g2"]:PT["g2"] + 8] = f("norm2")[l].reshape(8, 128).T
        cw = f("ssd_conv_w")[l]
        pt[l, :, PT["cwx"]:PT["cwx"] + 28] = cw.reshape(4, 7, 128).transpose(2, 1, 0).reshape(128, 28)
        pt[l, :, PT["cbx"]:PT["cbx"] + 7] = f("ssd_conv_b")[l].reshape(7, 128).T
        cw = f("rg_conv_w")[l]
        pt[l, :, PT["cwr"]:PT["cwr"] + 8] = cw.reshape(4, 2, 128).transpose(2, 1, 0).reshape(128, 8)
        pt[l, :, PT["cbr"]:PT["cbr"] + 2] = f("rg_conv_b")[l].reshape(2, 128).T
        cw = f("ffn_conv_w")[l]
        pt[l, :, PT["cwf"]:PT["cwf"] + 66] = cw.reshape(3, NFC, 128).transpose(2, 1, 0).reshape(128, 66)
        pt[l, :, PT["cbf"]:PT["cbf"] + NFC] = f("ffn_conv_b")[l].reshape(NFC, 128).T
        pt[l, 0:6, PT["dtb"]] = f("ssd_dt_bias")[l]
        pt[l, 0:6, PT["alog"]] = f("ssd_a_log")[l]
        pt[l, :, PT["dsk"]:PT["dsk"] + 3] = np.repeat(f("ssd_d")[l], 64).reshape(3, 128).T
        pt[l, :, PT["ssdn"]:PT["ssdn"] + 3] = f("ssd_norm")[l].reshape(3, 128).T
        pt[l, :, PT["qn"]] = np.tile(f("q_norm")[l], 2)
        pt[l, :, PT["kn"]] = np.tile(f("k_norm")[l], 2)
        pt[l, :, PT["sbb"]:PT["sbb"] + 6] = f("sb_bias")[l][None, :]
        pt[l, :, PT["sbn"]:PT["sbn"] + 3] = f("sb_out_norm")[l].reshape(3, 128).T
        pt[l, :, PT["rba"]:PT["rba"] + 2] = f("rg_ba")[l].reshape(2, 128).T
        pt[l, :, PT["rbx"]:PT["rbx"] + 2] = f("rg_bx")[l].reshape(2, 128).T
        pt[l, :, PT["rlam"]:PT["rlam"] + 2] = f("rg_lambda")[l].reshape(2, 128).T
        pt[l, :, PT["rgn"]:PT["rgn"] + 2] = f("rg_out_norm")[l].reshape(2, 128).T
    return pt


def _rgw(inp):
    out = np.zeros((DEPTH, 2, 2, 128, 128), np.float32)
    for l in range(DEPTH):
        for ai, k in enumerate(("rg_wa", "rg_wx")):
            w = np.asarray(inp[k], np.float32)[l]
            for g in range(2):
                for b in range(2):
                    out[l, ai, g, 64 * b:64 * b + 64, 64 * b:64 * b + 64] = w[2 * g + b]
    return out


_CACHE = {}


def kernel(**inp):
    x_prompt = np.asarray(inp["x_prompt"], np.float32)
    B, SEQ, _ = x_prompt.shape
    T = NMETA + SEQ
    ck = np.asarray(inp["cache_k"])
    NPOOL = ck.shape[1]
    page_table = np.asarray(inp["page_table"])
    NPAGES = page_table.shape[1]
    do_sample = True
    key = (SEQ, NPAGES, NPOOL, do_sample)
    if key not in _CACHE:
        _CACHE[key] = K(SEQ, NPAGES, NPOOL, do_sample=do_sample)
    kb = _CACHE[key]
    meta = np.asarray(inp["meta_tokens"], np.float32)
    sel6 = np.zeros((6, 3, 128), np.float32)
    for g in range(3):
        for r in range(128):
            sel6[2 * g + r // 64, g, r] = 1.0
    common = {
        "xT_s": np.ascontiguousarray(np.asarray(inp["x_sample"], np.float32)[:, 0, :].T),
        "w_in": np.asarray(inp["w_in"], np.float32),
        "w_out": np.asarray(inp["w_out"], np.float32),
        "w_up": np.asarray(inp["w_up"], np.float32),
        "w_dn": np.asarray(inp["w_down"], np.float32),
        "ptab": _ptab(inp),
        "rgw": _rgw(inp),
        "consts": _consts(),
        "sel6": sel6,
        "i6": np.eye(6, dtype=np.float32),
    }
    cv = np.asarray(inp["cache_v"])
    f32 = lambda k: np.asarray(inp[k], np.float32)
    zero_pool = None
    common["ptab_pg"] = np.ascontiguousarray(page_table.T.astype(np.int32))
    common["s_ssm"] = np.ascontiguousarray(f32("state_ssm").reshape(DEPTH, DB, 3, 128, NST).transpose(0, 2, 3, 1, 4))
    common["s_ssmc"] = np.ascontiguousarray(f32("state_ssm_conv").reshape(DEPTH, DB, 3, 7, 128).transpose(0, 3, 4, 2, 1))
    common["s_rg"] = np.ascontiguousarray(f32("state_rg").reshape(DEPTH, DB, 2, 128).transpose(0, 2, 3, 1))
    common["s_rgc"] = np.ascontiguousarray(f32("state_rg_conv").reshape(DEPTH, DB, 3, 2, 128).transpose(0, 3, 4, 2, 1))
    common["s_ffc"] = np.ascontiguousarray(f32("state_ffn_conv").reshape(DEPTH, DB, 2, NFC, 128).transpose(0, 3, 4, 2, 1))
    sbb = f32("sb_bias")
    in_maps = []
    for c in range(NCORES):
        m = dict(common)
        hs = np.zeros((3, 128, HD), np.float32)
        if c < NH:
            for l_ in range(DEPTH):
                m[f"ck{l_}"] = np.ascontiguousarray(ck[l_, :, :, c, :]).reshape(NPOOL, 128 * HD)
                m[f"cv{l_}"] = np.ascontiguousarray(cv[l_, :, :, c, :]).reshape(NPOOL, 128 * HD)
            for d in range(HD):
                r = c * HD + d
                hs[r // 128, r % 128, d] = 1.0
            m["sbh"] = np.ascontiguousarray(np.broadcast_to(sbb[:, c][None, :], (128, DEPTH)))
        else:
            if zero_pool is None:
                zero_pool = np.zeros((NPOOL, 128 * HD), np.float32)
            for l_ in range(DEPTH):
                m[f"ck{l_}"] = zero_pool
                m[f"cv{l_}"] = zero_pool
            m["sbh"] = np.zeros((128, DEPTH), np.float32)
        m["hsel"] = hs
        if c < B:
            m["xT_p"] = np.ascontiguousarray(np.concatenate([meta, x_prompt[c]], axis=0).T)
        else:
            m["xT_p"] = np.zeros((D, T), np.float32)
        in_maps.append(m)
    res = run_bass_kernel_spmd(kb.nc, in_maps, core_ids=list(range(NCORES)))
    R = res.results
    def stk(name):
        return np.stack([R[b][name] for b in range(B)], axis=0)
    y_prompt = stk("yT_p").transpose(0, 2, 1)[:, NMETA:, :]
    kT = stk("kT_p")
    new_k = kT.transpose(1, 0, 3, 2).reshape(DEPTH, B, T, NH, HD)
    new_v = stk("vT_p").transpose(1, 0, 3, 2).reshape(DEPTH, B, T, NH, HD)
    ssm = stk("ssm_p")
    new_ssm = ssm.transpose(1, 0, 3, 2).reshape(DEPTH, B, NH, HD, NST)
    ssmc = stk("ssmc_p")
    new_ssmc = ssmc.transpose(1, 0, 4, 2, 3).reshape(DEPTH, B, 3, XBC)
    rg = stk("rg_p").transpose(1, 0, 2, 3).reshape(DEPTH, B, RGW)
    rgc = stk("rgc_p").transpose(1, 0, 4, 2, 3).reshape(DEPTH, B, 3, RGW)
    ffc = stk("ffc_p").transpose(1, 0, 4, 2, 3).reshape(DEPTH, B, 2, DFF)
    r0 = R[0]
    y_sample = r0["yT_s"].T.reshape(DB, 1, D)
    k_s = r0["kT_s"].transpose(0, 2, 1).reshape(DEPTH, DB, 1, NH, HD)
    v_s = r0["vT_s"].transpose(0, 2, 1).reshape(DEPTH, DB, 1, NH, HD)
    ssm_s = r0["ssm_s"].transpose(0, 3, 1, 2, 4).reshape(DEPTH, DB, NH, HD, NST)
    ssmc_s = r0["ssmc_s"].transpose(0, 4, 3, 1, 2).reshape(DEPTH, DB, 3, XBC)
    rg_s = r0["rg_s"].transpose(0, 3, 1, 2).reshape(DEPTH, DB, RGW)
    rgc_s = r0["rgc_s"].transpose(0, 4, 3, 1, 2).reshape(DEPTH, DB, 3, RGW)
    ffc_s = r0["ffc_s"].transpose(0, 4, 3, 1, 2).reshape(DEPTH, DB, 2, DFF)
    outs = [y_prompt, y_sample, new_k, new_v, new_ssm, new_ssmc, rg, rgc, ffc, k_s, v_s, ssm_s, ssmc_s, rg_s, rgc_s, ffc_s]
    return tuple(np.ascontiguousarray(o, dtype=np.float32) for o in outs)
```

```python
import contextlib
import numpy as np
import concourse.bass as bass
import concourse.mybir as mybir
from concourse.bass_utils import run_bass_kernel_spmd

F32 = mybir.dt.float32
BF16 = mybir.dt.bfloat16
I32 = mybir.dt.int32
ALU = mybir.AluOpType
AF = mybir.ActivationFunctionType
AX = mybir.AxisListType

ENGS = ("pe", "dve", "act", "pool", "sp")
NCORES = 8
D = 1024
DEPTH = 2
NMETA = 16
HD = 64
NH = 6
SSDW = 384
XBC = 896
NST = 128
RGW = 256
MIXW = 1024
INC = 2950
DFF = 2816
NFC = DFF // 128
DB = 32
EPS = 1e-6
CG_Z = [(0, 128), (128, 128), (256, 128)]
CG_XBC = [(384 + 128 * i, 128) for i in range(7)]
CG_DT = (1280, 6)
CG_Q = [(1286 + 128 * i, 128) for i in range(3)]
CG_K = [(1670 + 128 * i, 128) for i in range(3)]
CG_V = [(2054 + 128 * i, 128) for i in range(3)]
CG_XR = [(2438 + 128 * i, 128) for i in range(2)]
CG_GR = [(2694 + 128 * i, 128) for i in range(2)]

PT = {}
_c = 0
for _n, _w in [("g1", 8), ("g2", 8), ("cwx", 28), ("cbx", 7), ("cwr", 8), ("cbr", 2), ("cwf", 66), ("cbf", 22),
               ("dtb", 1), ("alog", 1), ("dsk", 3), ("ssdn", 3), ("qn", 1), ("kn", 1), ("sbb", 6), ("sbn", 6),
               ("rba", 2), ("rbx", 2), ("rlam", 2), ("rgn", 2)]:
    PT[_n] = _c
    _c += _w
NPT = _c


class Buf:
    def __init__(self, name):
        self.name = name
        self.writers = []
        self.readers = []
        self.dsem = None


class TB:
    def __init__(self, t, b):
        self.t = t
        self.b = b

    def __getitem__(self, k):
        return self.t[k]


class Prog:
    def __init__(self, nc, stack):
        self.nc = nc
        self.ops = {e: [] for e in ENGS}
        self.cnt = {e: 0 for e in ENGS}
        self.sems = {}
        self.dma_tot = {}
        self.waited = {e: {} for e in ENGS}
        self._stack = stack
        self.n_dma_sems = 78
        self.dma_rr = 0

    def sem(self, key):
        if key not in self.sems:
            self.sems[key] = self._stack.enter_context(self.nc.semaphore("s_" + key))
        return self.sems[key]

    def _dsem(self, b):
        if b.dsem is None:
            b.dsem = f"d{self.dma_rr % self.n_dma_sems}"
            self.dma_rr += 1
            self.dma_tot.setdefault(b.dsem, 0)
        return b.dsem

    def _collect(self, eng, reads, writes):
        need = {}
        def add(tok):
            k, v = tok
            if v > need.get(k, 0):
                need[k] = v
        for b in reads:
            for t in b.writers:
                add(t)
        for b in writes:
            for t in b.writers:
                add(t)
            for t in b.readers:
                add(t)
        waits = []
        w = self.waited[eng]
        for k, v in need.items():
            if eng == "pe" and k == "e_pe":
                continue
            if w.get(k, 0) >= v:
                continue
            w[k] = v
            waits.append((k, v))
        return waits

    def _commit(self, tok, reads, writes):
        for b in reads:
            b.readers.append(tok)
            if len(b.readers) > 64:
                mx = {}
                for k, v in b.readers:
                    mx[k] = max(mx.get(k, 0), v)
                b.readers = list(mx.items())
        for b in writes:
            b.writers = [tok]
            b.readers = []

    def op(self, eng, emit, reads=(), writes=()):
        reads = [r.b if isinstance(r, TB) else r for r in reads]
        writes = [r.b if isinstance(r, TB) else r for r in writes]
        waits = self._collect(eng, reads, writes)
        self.cnt[eng] += 1
        tok = ("e_" + eng, self.cnt[eng])
        self.ops[eng].append((waits, emit, ("e_" + eng, 1)))
        self._commit(tok, reads, writes)

    def dma(self, eng, emit, reads=(), writes=(), sbuf=None):
        reads = [r.b if isinstance(r, TB) else r for r in reads]
        writes = [r.b if isinstance(r, TB) else r for r in writes]
        waits = self._collect(eng, reads, writes)
        b = sbuf if sbuf is not None else (list(writes) + list(reads))[0]
        if isinstance(b, TB):
            b = b.b
        k = self._dsem(b)
        self.dma_tot[k] += 16
        tok = (k, self.dma_tot[k])
        self.ops[eng].append((waits, emit, (k, 16)))
        self._commit(tok, reads, writes)

    def cc(self, emit, key):
        self.dma_tot.setdefault(key, 0)
        self.dma_tot[key] += 1
        self.ops["pool"].append(([], emit, (key, None)))
        return (key, self.dma_tot[key])

    def barrier(self):
        targets = {}
        for e in ENGS:
            if self.cnt[e] > 0:
                targets["e_" + e] = self.cnt[e]
        for k, v in self.dma_tot.items():
            if v > 0:
                targets[k] = v
        for e in ENGS:
            waits = []
            w = self.waited[e]
            for k, v in targets.items():
                if w.get(k, 0) >= v:
                    continue
                w[k] = v
                waits.append((k, v))
            if waits:
                self.ops[e].append((waits, None, None))

    def emit(self, block):
        handles = {"pe": "tensor", "dve": "vector", "act": "scalar", "pool": "gpsimd", "sp": "sync"}
        for e in ENGS:
            for waits, emit, inc in self.ops[e]:
                for k, v in waits:
                    self.sem(k)
                if inc is not None:
                    self.sem(inc[0])
        for e in ENGS:
            ops = self.ops[e]
            if not ops:
                continue
            def body(eh, ops=ops):
                for waits, emit, inc in ops:
                    for k, v in waits:
                        eh.wait_ge(self.sems[k], v)
                    if emit is not None:
                        ins = emit(eh)
                        if inc[1] is None:
                            ins.then_inc(self.sems[inc[0]])
                        else:
                            ins.then_inc(self.sems[inc[0]], inc[1])
            getattr(block, handles[e])(body)


class K:
    def __init__(self, SEQ, NPAGES, NPOOL, do_sample=True):
        self.SEQ = SEQ
        self.T = NMETA + SEQ
        self.NPAGES = NPAGES
        self.NPOOL = NPOOL
        self.NB = 1 + SEQ // 128
        self.do_sample = do_sample
        self.nc = bass.Bass("TRN2", target_bir_lowering=False)
        self.build()

    def din(self, name, shape, dt=F32):
        return self.nc.dram_tensor(name, list(shape), dt, kind="ExternalInput").ap()

    def dout(self, name, shape, dt=F32):
        return self.nc.dram_tensor(name, list(shape), dt, kind="ExternalOutput").ap()

    def dscr(self, name, shape, dt=F32):
        return self.nc.dram_tensor(name, list(shape), dt).ap()

    def sb(self, st, name, shape, dt=F32):
        self._uid = getattr(self, "_uid", 0) + 1
        name = f"sb{self._uid}_{name}"
        t = st.enter_context(self.nc.sbuf_tensor(name, list(shape), dt))
        return TB(t, Buf(name))

    def tt(self, eng, out, in0, in1, op, R, W):
        self.P.op(eng, lambda e: e.tensor_tensor(out=out, in0=in0, in1=in1, op=op), R, W)

    def ts(self, eng, out, in0, s1, s2, op0, op1, R, W):
        if s2 is None:
            self.P.op(eng, lambda e: e.tensor_scalar(out=out, in0=in0, scalar1=s1, scalar2=None, op0=op0), R, W)
        else:
            self.P.op(eng, lambda e: e.tensor_scalar(out=out, in0=in0, scalar1=s1, scalar2=s2, op0=op0, op1=op1), R, W)

    def stt(self, eng, out, in0, scalar, in1, op0, op1, R, W):
        eng = "dve"
        self.P.op(eng, lambda e: e.scalar_tensor_tensor(out=out, in0=in0, scalar=scalar, in1=in1, op0=op0, op1=op1), R, W)

    def act(self, out, in_, func, R, W, bias=None, scale=None):
        kw = {}
        if bias is not None:
            kw["bias"] = bias
        if scale is not None:
            kw["scale"] = scale
        self.P.op("act", lambda e: e.activation(out=out, in_=in_, func=func, **kw), R, W)

    def cp(self, eng, out, in_, R, W):
        if eng == "act":
            self.P.op("act", lambda e: e.activation(out=out, in_=in_, func=AF.Copy), R, W)
        else:
            self.P.op(eng, lambda e: e.tensor_copy(out=out, in_=in_), R, W)

    def mm(self, out, lhsT, rhs, start, stop, R, W):
        self.P.op("pe", lambda e: e.matmul(out, lhsT=lhsT, rhs=rhs, start=start, stop=stop), R, W)

    def tr(self, out, in_, ident, R, W):
        self.P.op("pe", lambda e: e.transpose(out, in_, ident), R, W)

    def dma(self, eng, out, in_, R, W, sbuf=None):
        if eng == "pool":
            eng = "sp"
        self.P.dma(eng, lambda e: e.dma_start(out=out, in_=in_), R, W, sbuf=sbuf)

    def memset(self, eng, ap, val, W):
        self.P.op(eng, lambda e: e.memset(ap, val), [], W)

    def silu_chain(self, v, out, tmp, R, W, n_eng="dve"):
        self.act(tmp, v, AF.Exp, R, W, scale=-1.0)
        self.act(tmp, tmp, AF.Ln, W, W, bias=1.0)
        self.act(tmp, tmp, AF.Exp, W, W, scale=-1.0)

    def build(self):
        nc = self.nc
        T, SEQ = self.T, self.SEQ
        I = self.I = {}
        O = self.O = {}
        I["xT_p"] = self.din("xT_p", [D, T])
        I["xT_s"] = self.din("xT_s", [D, DB])
        I["w_in"] = self.din("w_in", [DEPTH, D, INC])
        I["w_out"] = self.din("w_out", [DEPTH, MIXW, D])
        I["w_up"] = self.din("w_up", [DEPTH, D, 2 * DFF])
        I["w_dn"] = self.din("w_dn", [DEPTH, DFF, D])
        I["ptab"] = self.din("ptab", [DEPTH, 128, NPT])
        I["rgw"] = self.din("rgw", [DEPTH, 2, 2, 128, 128])
        I["consts"] = self.din("consts", [128, 1024])
        I["sel6"] = self.din("sel6", [6, 3, 128])
        I["i6"] = self.din("i6", [6, 6])
        I["ridx"] = self.din("ridx", [128, 3], I32)
        I["sbb2"] = self.din("sbb2", [128, DEPTH, 2])
        if self.do_sample:
            I["ck"] = [self.din(f"ck{l_}", [self.NPOOL, 128 * HD]) for l_ in range(DEPTH)]
            I["cv"] = [self.din(f"cv{l_}", [self.NPOOL, 128 * HD]) for l_ in range(DEPTH)]
            I["ptab_pg"] = self.din("ptab_pg", [self.NPAGES, DB], I32)
            I["hsel"] = self.din("hsel", [3, 128, HD])
            I["sbh"] = self.din("sbh", [128, DEPTH])
            I["s_ssm"] = self.din("s_ssm", [DEPTH, 3, 128, DB, NST])
            I["s_ssmc"] = self.din("s_ssmc", [DEPTH, 7, 128, 3, DB])
            I["s_rg"] = self.din("s_rg", [DEPTH, 2, 128, DB])
            I["s_rgc"] = self.din("s_rgc", [DEPTH, 2, 128, 3, DB])
            I["s_ffc"] = self.din("s_ffc", [DEPTH, NFC, 128, 2, DB])
        O["yT_p"] = self.dout("yT_p", [D, T])
        O["kT_p"] = self.dout("kT_p", [DEPTH, SSDW, T])
        O["vT_p"] = self.dout("vT_p", [DEPTH, SSDW, T])
        O["ssm_p"] = self.dout("ssm_p", [DEPTH, 128, SSDW])
        O["ssmc_p"] = self.dout("ssmc_p", [DEPTH, 7, 128, 3])
        O["rg_p"] = self.dout("rg_p", [DEPTH, 2, 128])
        O["rgc_p"] = self.dout("rgc_p", [DEPTH, 2, 128, 3])
        O["ffc_p"] = self.dout("ffc_p", [DEPTH, NFC, 128, 2])
        if self.do_sample:
            O["yT_s"] = self.dout("yT_s", [D, DB])
            O["kT_s"] = self.dout("kT_s", [DEPTH, SSDW, DB])
            O["vT_s"] = self.dout("vT_s", [DEPTH, SSDW, DB])
            O["ssm_s"] = self.dout("ssm_s", [DEPTH, 3, 128, DB, NST])
            O["ssmc_s"] = self.dout("ssmc_s", [DEPTH, 7, 128, 3, DB])
            O["rg_s"] = self.dout("rg_s", [DEPTH, 2, 128, DB])
            O["rgc_s"] = self.dout("rgc_s", [DEPTH, 2, 128, 3, DB])
            O["ffc_s"] = self.dout("ffc_s", [DEPTH, NFC, 128, 2, DB])
        S = self.S = {}
        S["xT1"] = self.dscr("xT1", [D, T])
        NBW = self.NB * 128
        S["QT"] = self.dscr("QT", [SSDW, T], BF16)
        S["KT"] = self.dscr("KT", [SSDW, T], BF16)
        S["V2"] = self.dscr("V2", [SSDW, NBW], BF16)
        S["QTg"] = [self.dscr(f"QTg{l_}", [4 * SSDW, T], BF16) for l_ in range(DEPTH)]
        S["KTg"] = [self.dscr(f"KTg{l_}", [4 * SSDW, T], BF16) for l_ in range(DEPTH)]
        S["V2g"] = [self.dscr(f"V2g{l_}", [4 * SSDW, NBW], BF16) for l_ in range(DEPTH)]
        NT2 = 1 + self.SEQ // 256
        self.NT2 = NT2
        S["O2"] = [self.dscr(f"O2_{l_}", [NT2 * 128, 256]) for l_ in range(DEPTH)]
        S["O2g"] = [self.dscr(f"O2g{l_}", [NT2 * 4 * 128, 256]) for l_ in range(DEPTH)]
        S["mixT"] = self.dscr("mixT", [MIXW, T], BF16)
        if self.do_sample:
            S["xs1"] = self.dscr("xs1", [D, DB])
            S["mixs"] = self.dscr("mixs", [MIXW, DB], BF16)
            S["qs"] = self.dscr("qs", [SSDW, DB])
            S["ag_in"] = [self.dscr(f"ag_in{l_}", [HD, DB]) for l_ in range(DEPTH)]
            S["ag_out"] = [self.dscr(f"ag_out{l_}", [NCORES * HD, DB]) for l_ in range(DEPTH)]

        with contextlib.ExitStack() as st:
            self.P = Prog(nc, st)
            self.ps = [TB(st.enter_context(nc.psum_tensor(f"ps{i}", [128, 512], F32)), Buf(f"ps{i}")) for i in range(7)]
            self.psb = TB(st.enter_context(nc.psum_tensor("psb", [128, 1024], BF16)), Buf("psb"))
            cst = self.sb(st, "cst", [128, 1024])
            self.dma("sp", cst[:], I["consts"][:, :], [], [cst])
            self.cst = cst
            self.identf = cst[:, 0:128]
            self.triU_f = cst[:, 128:256]
            self.tri_ts = cst[:, 256:384]
            self.tri_st = cst[:, 384:512]
            cbf = self.sb(st, "cbf", [128, 640], BF16)
            self.cbf = cbf
            self.cp("dve", cbf[:, 0:128], cst[:, 0:128], [cst], [cbf])
            self.ts("dve", cbf[:, 128:256], cst[:, 128:256], -1.0, None, ALU.mult, None, [cst], [cbf])
            self.memset("pool", cbf[:, 256:384], 1.0 / 1024.0, [cbf])
            self.memset("pool", cbf[:, 384:512], 1.0, [cbf])
            self.memset("pool", cbf[:, 512:640], 1.0 / 64.0, [cbf])
            self.ident_bf = cbf[:, 0:128]
            self.ntriU_bf = cbf[:, 128:256]
            self.ones1024_bf = cbf[:, 256:384]
            self.ones_bf = cbf[:, 384:512]
            bo = self.sb(st, "bo", [128, 128], BF16)
            self.memset("pool", bo[:], 0.0, [bo])
            self.memset("pool", bo[0:64, 0:64], 1.0 / 64.0, [bo])
            self.memset("pool", bo[64:128, 64:128], 1.0 / 64.0, [bo])
            self.bo = bo
            cf = self.sb(st, "cf", [128, 512])
            self.memset("pool", cf[:, 0:128], 1.0, [cf])
            self.memset("pool", cf[:, 128:256], -1.0, [cf])
            self.memset("pool", cf[:, 256:384], 0.0, [cf])
            self.memset("pool", cf[:, 384:512], 1.0 / 384.0, [cf])
            self.cf = cf
            sel6 = self.sb(st, "sel6", [6, 3, 128])
            self.dma("sp", sel6[:], I["sel6"][:, :, :], [], [sel6])
            self.sel6 = sel6
            i6 = self.sb(st, "i6", [6, 6])
            self.dma("sp", i6[:], I["i6"][:, :], [], [i6])
            self.i6 = i6
            ptab = self.sb(st, "ptab", [128, DEPTH, NPT])
            self.dma("sp", ptab[:], I["ptab"].rearrange("l p c -> p l c"), [], [ptab])
            self.ptab = ptab
            der = self.sb(st, "der", [128, DEPTH, 16])
            self.der = der
            for l in range(DEPTH):
                self.act(der[0:6, l, 0:1], ptab[0:6, l, PT["alog"]:PT["alog"] + 1], AF.Exp, [ptab], [der])
                self.ts("dve", der[0:6, l, 0:1], der[0:6, l, 0:1], -1.0, None, ALU.mult, None, [der], [der])
                lam = ptab[:, l, PT["rlam"]:PT["rlam"] + 2]
                self.act(der[:, l, 1:3], lam, AF.Exp, [ptab], [der], scale=-1.0)
                self.act(der[:, l, 1:3], der[:, l, 1:3], AF.Ln, [der], [der], bias=1.0)
                self.ts("dve", der[:, l, 1:3], der[:, l, 1:3], -8.0, None, ALU.mult, None, [der], [der])
                self.ts("dve", der[:, l, 3:5], der[:, l, 1:3], 2.0, None, ALU.mult, None, [der], [der])
                self.ts("dve", der[:, l, 5:7], ptab[:, l, PT["rba"]:PT["rba"] + 2], -1.0, None, ALU.mult, None, [ptab], [der])
                self.ts("dve", der[:, l, 7:9], ptab[:, l, PT["rbx"]:PT["rbx"] + 2], -1.0, None, ALU.mult, None, [ptab], [der])
                self.ts("dve", der[:, l, 9:10], ptab[:, l, PT["qn"]:PT["qn"] + 1], 0.125, None, ALU.mult, None, [ptab], [der])
            self.P.barrier()

            for l in range(DEPTH):
                x_in = I["xT_p"] if l == 0 else S["xT1"]
                x_out = S["xT1"] if l < DEPTH - 1 else O["yT_p"]
                xs_in = (I["xT_s"] if l == 0 else S["xs1"]) if self.do_sample else None
                xs_out = (S["xs1"] if l < DEPTH - 1 else O["yT_s"]) if self.do_sample else None
                self.phase_a(l, x_in, xs_in)
                self.P.barrier()
                self.phase_b(l)
                self.P.barrier()
                if self.do_sample:
                    self.phase_b_sample(l)
                    self.P.barrier()
                self.phase_c(l, x_in, x_out, xs_in, xs_out)
                self.P.barrier()
            with nc.allow_non_contiguous_dma(reason="small strided state outputs"), nc.Block() as block:
                self.P.emit(block)

    def load_cast(self, st, name, src_rows, ncols, nkc, scale_cols=None, l=0, chunk=2048):
        w = self.sb(st, name, [128, nkc, ncols], BF16)
        st2 = contextlib.ExitStack()
        stg = [self.sb(st2, name + f"_stg{i}", [128, chunk]) for i in range(2)]
        i = 0
        for kc in range(nkc):
            for c0 in range(0, ncols, chunk):
                cw = min(chunk, ncols - c0)
                sg = stg[i % 2]
                self.dma("sp" if i % 2 == 0 else "act", sg[:, 0:cw], src_rows[kc * 128:(kc + 1) * 128, c0:c0 + cw], [], [sg])
                eng = "dve" if i % 2 == 0 else "pool"
                if scale_cols is not None:
                    scc = scale_cols[kc] if isinstance(scale_cols, (list, tuple)) else scale_cols + kc
                    sc = self.ptab[:, l, scc:scc + 1]
                    self.ts(eng, w[:, kc, c0:c0 + cw], sg[:, 0:cw], sc, None, ALU.mult, None, [sg, self.ptab], [w])
                else:
                    self.cp(eng, w[:, kc, c0:c0 + cw], sg[:, 0:cw], [sg], [w])
                i += 1
        self.P.barrier()
        st2.close()
        return w

    def tiles(self, tw=512):
        tl = [(0, NMETA)]
        for c0 in range(NMETA, self.T, tw):
            tl.append((c0, tw))
        return tl

    def phase_a(self, l, x_in, xs_in):
        nc, P, I, O, S = self.nc, self.P, self.I, self.O, self.S
        pt = self.ptab
        der = self.der
        def pc(name, k=0, n=1):
            return pt[:, l, PT[name] + k:PT[name] + k + n]
        with contextlib.ExitStack() as st:
            W = self.load_cast(st, "win", I["w_in"][l], INC, 8, scale_cols=PT["g1"], l=l, chunk=1475)
            rgw = self.sb(st, "rgw", [128, 4, 128], BF16)
            rgs = self.sb(st, "rgs", [128, 4, 128])
            self.dma("sp", rgs[:], I["rgw"][l].rearrange("a g p c -> p (a g) c"), [], [rgs])
            self.cp("dve", rgw[:], rgs[:], [rgs], [rgw])
            if self.do_sample:
                self.phase_a_sample(l, xs_in, W, rgw)
            NW = 256
            xt = [self.sb(st, f"xt{i}", [128, 8, NW]) for i in range(1)]
            hb = [self.sb(st, f"hb{i}", [128, 8, NW], BF16) for i in range(1)]
            sq = self.sb(st, "sq", [128, 8, NW], BF16)
            rstd = self.sb(st, "rstd", [128, NW])
            zz = self.sb(st, "zz", [128, 3, NW])
            sz = self.sb(st, "sz", [128, 3, NW])
            xp = self.sb(st, "xp", [128, 7, 3 + NW])
            xcv = self.sb(st, "xcv", [128, 7, NW])
            xtmp = self.sb(st, "xtmp", [128, 7, NW])
            xcb = self.sb(st, "xcb", [128, 7, NW], BF16)
            dtT = self.sb(st, "dtT", [6, NW])
            aT = self.sb(st, "aT", [6, NW])
            cumT = self.sb(st, "cumT", [6, NW])
            qk = self.sb(st, "qk", [128, 6, NW])
            qksq = self.sb(st, "qksq", [128, 6, NW], BF16)
            qkr = self.sb(st, "qkr", [128, 6, NW])
            qkb = self.sb(st, "qkb", [128, 6, NW], BF16)
            vv = self.sb(st, "vv", [128, 3, NW])
            vb = self.sb(st, "vb", [128, 3, NW], BF16)
            vtok = self.sb(st, "vtok", [128, 4, SSDW], BF16)
            xrp = self.sb(st, "xrp", [128, 2, 3 + NW])
            rga = self.sb(st, "rga", [128, 2, NW])
            rgb_ = self.sb(st, "rgb", [128, 2, NW], BF16)
            rgt = [self.sb(st, f"rgt{i}", [128, 2, NW]) for i in range(5)]
            grr = self.sb(st, "grr", [128, 2, NW])
            hprev = self.sb(st, "hprev", [128, 2])
            state = self.sb(st, "state", [128, SSDW])
            stateb = self.sb(st, "stateb", [128, SSDW], BF16)
            ytk = self.sb(st, "ytk", [128, SSDW])
            ytmp = self.sb(st, "ytmp", [128, SSDW])
            yg = self.sb(st, "yg", [128, 3, NW])
            ysq = self.sb(st, "ysq", [128, 3, NW], BF16)
            yr = self.sb(st, "yr", [128, NW])
            mixb = self.sb(st, "mixb", [128, 3, NW], BF16)
            mixr = self.sb(st, "mixr", [128, 2, NW], BF16)
            dctok = self.sb(st, "dctok", [128, 12])
            smallt = self.sb(st, "smallt", [128, 24])
            rhsb = self.sb(st, "rhsb", [6, 6 * 128])
            dec = self.sb(st, "dec", [128, 6, 128])
            cbm = self.sb(st, "cbm", [128, 2, 128])
            wT = self.sb(st, "wT", [128, 6, 128], BF16)
            xdt = self.sb(st, "xdt", [128, SSDW], BF16)
            xde = self.sb(st, "xde", [128, SSDW], BF16)
            btok = self.sb(st, "btok", [128, 256], BF16)
            self.memset("dve", smallt[:], 0.0, [smallt])
            self.memset("dve", dctok[:], 0.0, [dctok])
            self.memset("dve", state[:], 0.0, [state])
            self.memset("dve", stateb[:], 0.0, [stateb])
            self.memset("dve", hprev[:], 0.0, [hprev])
            self.memset("pool", xp[:, :, 0:3], 0.0, [xp])
            self.memset("pool", xrp[:, :, 0:3], 0.0, [xrp])
            ps = self.ps
            psb = self.psb
            tl = self.tiles(NW)
            psi = 0
            for ti, (c0, n) in enumerate(tl):
                X = xt[0]
                H = hb[0]
                self.dma("sp", X[:, :, 0:n], x_in[:, c0:c0 + n].rearrange("(k p) t -> p k t", p=128), [], [X])
                self.tt("pool", sq[:, :, 0:n], X[:, :, 0:n], X[:, :, 0:n], ALU.mult, [X], [sq])
                self.cp("dve", H[:, :, 0:n], X[:, :, 0:n], [X], [H])
                pm = ps[6]
                for kc in range(8):
                    self.mm(pm[:, 0:n], self.ones1024_bf, sq[:, kc, 0:n], kc == 0, kc == 7, [sq, self.cbf], [pm])
                self.act(rstd[:, 0:n], pm[:, 0:n], AF.Ln, [pm], [rstd], bias=EPS)
                self.act(rstd[:, 0:n], rstd[:, 0:n], AF.Exp, [rstd], [rstd], scale=-0.5)

                def proj(c_start, rows, dst_ap, dstb, eng="dve"):
                    nonlocal psi
                    pp = ps[psi % 4]
                    psi += 1
                    for kc in range(8):
                        self.mm(pp[0:rows, 0:n], W[:, kc, c_start:c_start + rows], H[:, kc, 0:n], kc == 0, kc == 7, [W, H], [pp])
                    self.tt(eng, dst_ap, pp[0:rows, 0:n], rstd[0:rows, 0:n], ALU.mult, [pp, rstd], [dstb])

                for g, (cs, r) in enumerate(CG_Z):
                    proj(cs, r, zz[:, g, 0:n], zz)
                for g, (cs, r) in enumerate(CG_XBC):
                    proj(cs, r, xp[:, g, 3:3 + n], xp)
                proj(CG_DT[0], 6, dtT[0:6, 0:n], dtT)
                for g, (cs, r) in enumerate(CG_Q + CG_K):
                    proj(cs, r, qk[:, g, 0:n], qk)
                for g, (cs, r) in enumerate(CG_V):
                    proj(cs, r, vv[:, g, 0:n], vv)
                for g, (cs, r) in enumerate(CG_XR):
                    proj(cs, r, xrp[:, g, 3:3 + n], xrp)
                for g, (cs, r) in enumerate(CG_GR):
                    proj(cs, r, grr[:, g, 0:n], grr)

                self.act(sz[:, :, 0:n], zz[:, :, 0:n], AF.Exp, [zz], [sz], scale=-1.0)
                self.act(sz[:, :, 0:n], sz[:, :, 0:n], AF.Ln, [sz], [sz], bias=1.0)
                self.act(sz[:, :, 0:n], sz[:, :, 0:n], AF.Exp, [sz], [sz], scale=-1.0)
                self.tt("pool", sz[:, :, 0:n], sz[:, :, 0:n], zz[:, :, 0:n], ALU.mult, [sz, zz], [sz])
                for g in range(7):
                    eng = "dve" if g % 2 == 0 else "pool"
                    self.ts(eng, xcv[:, g, 0:n], xp[:, g, 0:n], pc("cwx", 4 * g), pc("cbx", g), ALU.mult, ALU.add, [xp, pt], [xcv])
                    for j in range(1, 4):
                        self.stt(eng, xcv[:, g, 0:n], xp[:, g, j:j + n], pc("cwx", 4 * g + j), xcv[:, g, 0:n], ALU.mult, ALU.add, [xp, pt, xcv], [xcv])
                if ti == len(tl) - 1:
                    self.dma("pool", O["ssmc_p"][l].rearrange("g p j -> p g j"), xp[:, :, n:n + 3], [xp], [], sbuf=xp)
                self.act(xtmp[:, :, 0:n], xcv[:, :, 0:n], AF.Exp, [xcv], [xtmp], scale=-1.0)
                self.act(xtmp[:, :, 0:n], xtmp[:, :, 0:n], AF.Ln, [xtmp], [xtmp], bias=1.0)
                self.act(xtmp[:, :, 0:n], xtmp[:, :, 0:n], AF.Exp, [xtmp], [xtmp], scale=-1.0)
                self.tt("dve", xtmp[:, :, 0:n], xtmp[:, :, 0:n], xcv[:, :, 0:n], ALU.mult, [xtmp, xcv], [xtmp])
                xc = xtmp
                self.cp("pool", xcb[:, :, 0:n], xc[:, :, 0:n], [xc], [xcb])
                self.cp("pool", xp[:, :, 0:3], xp[:, :, n:n + 3], [xp], [xp])
                self.act(dtT[0:6, 0:n], dtT[0:6, 0:n], AF.Exp, [dtT, pt], [dtT], bias=pt[0:6, l, PT["dtb"]:PT["dtb"] + 1])
                self.act(dtT[0:6, 0:n], dtT[0:6, 0:n], AF.Ln, [dtT], [dtT], bias=1.0)
                self.ts("dve", aT[0:6, 0:n], dtT[0:6, 0:n], der[0:6, l, 0:1], None, ALU.mult, None, [dtT, der], [aT])
                self.tt("pool", qksq[:, :, 0:n], qk[:, :, 0:n], qk[:, :, 0:n], ALU.mult, [qk], [qksq])
                for g in range(6):
                    pp = ps[psi % 4]
                    psi += 1
                    self.mm(pp[:, 0:n], self.bo[:, :], qksq[:, g, 0:n], True, True, [self.bo, qksq], [pp])
                    self.act(qkr[:, g, 0:n], pp[:, 0:n], AF.Ln, [pp], [qkr], bias=EPS)
                self.act(qkr[:, :, 0:n], qkr[:, :, 0:n], AF.Exp, [qkr], [qkr], scale=-0.5)
                for g in range(6):
                    gain = der[:, l, 9:10] if g < 3 else pc("kn")
                    self.stt("dve", qkr[:, g, 0:n], qk[:, g, 0:n], gain, qkr[:, g, 0:n], ALU.mult, ALU.mult, [qk, qkr, der, pt], [qkr])
                qkn = qkr
                self.cp("pool", qkb[:, :, 0:n], qkn[:, :, 0:n], [qkn], [qkb])
                self.dma("pool", S["QT"][:, c0:c0 + n].rearrange("(g p) t -> p g t", p=128), qkb[:, 0:3, 0:n], [qkb], [], sbuf=qkb)
                self.dma("pool", S["KT"][:, c0:c0 + n].rearrange("(g p) t -> p g t", p=128), qkb[:, 3:6, 0:n], [qkb], [], sbuf=qkb)
                self.dma("pool", O["kT_p"][l, :, c0:c0 + n].rearrange("(g p) t -> p g t", p=128), qkn[:, 3:6, 0:n], [qkn], [], sbuf=qkn)
                self.dma("pool", O["vT_p"][l, :, c0:c0 + n].rearrange("(g p) t -> p g t", p=128), vv[:, :, 0:n], [vv], [], sbuf=vv)
                self.cp("pool", vb[:, :, 0:n], vv[:, :, 0:n], [vv], [vb])
                nch = max(1, n // 128)
                L = min(n, 128)
                for ch in range(nch):
                    for g in range(3):
                        self.tr(psb[0:L, g * 128:(g + 1) * 128], vb[:, g, ch * L:(ch + 1) * L], self.ident_bf, [vb, self.cbf], [psb])
                    self.cp("dve", vtok[0:L, ch, :], psb[0:L, 0:SSDW], [psb], [vtok])
                for g in range(3):
                    if n >= 128:
                        blk0 = 1 + (c0 - NMETA) // 128
                        self.dma("pool", S["V2"][g * 128:(g + 1) * 128, blk0 * 128:(blk0 + nch) * 128].rearrange("p (c f) -> p c f", f=128),
                                 vtok[:, 0:nch, g * 128:(g + 1) * 128], [vtok], [], sbuf=vtok)
                    else:
                        self.dma("pool", S["V2"][g * 128:g * 128 + n, 0:128], vtok[0:n, 0, g * 128:(g + 1) * 128], [vtok], [], sbuf=vtok)

                for g in range(2):
                    eng = "pool"
                    self.ts(eng, rga[:, g, 0:n], xrp[:, g, 0:n], pc("cwr", 4 * g), pc("cbr", g), ALU.mult, ALU.add, [xrp, pt], [rga])
                    for j in range(1, 4):
                        self.stt(eng, rga[:, g, 0:n], xrp[:, g, j:j + n], pc("cwr", 4 * g + j), rga[:, g, 0:n], ALU.mult, ALU.add, [xrp, pt, rga], [rga])
                if ti == len(tl) - 1:
                    self.dma("pool", O["rgc_p"][l].rearrange("g p j -> p g j"), xrp[:, :, n:n + 3], [xrp], [], sbuf=xrp)
                self.cp("pool", xrp[:, :, 0:3], xrp[:, :, n:n + 3], [xrp], [xrp])
                self.cp("dve", rgb_[:, :, 0:n], rga[:, :, 0:n], [rga], [rgb_])
                R_, I_, A_, U_, G_ = rgt
                for g in range(2):
                    for (ai, dst, nb) in ((0, R_, 5), (1, I_, 7)):
                        pp = ps[psi % 4]
                        psi += 1
                        self.mm(pp[:, 0:n], rgw[:, ai * 2 + g, :], rgb_[:, g, 0:n], True, True, [rgw, rgb_], [pp])
                        self.act(dst[:, g, 0:n], pp[:, 0:n], AF.Exp, [pp, der], [dst], bias=der[:, l, nb + g:nb + g + 1], scale=-1.0)
                for dst in (R_, I_):
                    self.act(dst[:, :, 0:n], dst[:, :, 0:n], AF.Ln, [dst], [dst], bias=1.0)
                    self.act(dst[:, :, 0:n], dst[:, :, 0:n], AF.Exp, [dst], [dst], scale=-1.0)
                for g in range(2):
                    self.act(A_[:, g, 0:n], R_[:, g, 0:n], AF.Exp, [R_, der], [A_], scale=der[:, l, 1 + g:2 + g])
                    self.act(U_[:, g, 0:n], R_[:, g, 0:n], AF.Exp, [R_, der], [U_], scale=der[:, l, 3 + g:4 + g])
                self.ts("dve", U_[:, :, 0:n], U_[:, :, 0:n], -1.0, 1.0, ALU.mult, ALU.add, [U_], [U_])
                self.act(U_[:, :, 0:n], U_[:, :, 0:n], AF.Ln, [U_], [U_])
                self.act(U_[:, :, 0:n], U_[:, :, 0:n], AF.Exp, [U_], [U_], scale=0.5)
                self.tt("dve", U_[:, :, 0:n], U_[:, :, 0:n], I_[:, :, 0:n], ALU.mult, [U_, I_], [U_])
                self.tt("dve", U_[:, :, 0:n], U_[:, :, 0:n], rga[:, :, 0:n], ALU.mult, [U_, rga], [U_])
                for g in range(2):
                    self.P.op("dve", lambda e, g=g: e.tensor_tensor_scan(out=R_[:, g, 0:n], data0=A_[:, g, 0:n], data1=U_[:, g, 0:n],
                                                                          initial=hprev[:, g:g + 1], op0=ALU.mult, op1=ALU.add),
                              [A_, U_, hprev], [R_])
                self.cp("dve", hprev[:, :], R_[:, :, n - 1], [R_], [hprev])
                if ti == len(tl) - 1:
                    self.dma("pool", O["rg_p"][l].rearrange("g p -> p g"), hprev[:, :], [hprev], [], sbuf=hprev)
                self.tt("pool", G_[:, :, 0:n], grr[:, :, 0:n], grr[:, :, 0:n], ALU.mult, [grr], [G_])
                self.ts("pool", G_[:, :, 0:n], G_[:, :, 0:n], 0.044715, 1.0, ALU.mult, ALU.add, [G_], [G_])
                self.tt("pool", G_[:, :, 0:n], G_[:, :, 0:n], grr[:, :, 0:n], ALU.mult, [G_, grr], [G_])
                self.act(G_[:, :, 0:n], G_[:, :, 0:n], AF.Exp, [G_], [G_], scale=-2.0 * 0.7978845608028654)
                self.act(G_[:, :, 0:n], G_[:, :, 0:n], AF.Ln, [G_], [G_], bias=1.0)
                self.act(G_[:, :, 0:n], G_[:, :, 0:n], AF.Exp, [G_], [G_], scale=-1.0)
                self.tt("pool", G_[:, :, 0:n], G_[:, :, 0:n], grr[:, :, 0:n], ALU.mult, [G_, grr], [G_])
                self.tt("dve", I_[:, :, 0:n], R_[:, :, 0:n], G_[:, :, 0:n], ALU.mult, [R_, G_], [I_])
                self.tt("pool", A_[:, :, 0:n], I_[:, :, 0:n], I_[:, :, 0:n], ALU.mult, [I_], [A_])
                self.cp("pool", rgb_[:, :, 0:n], A_[:, :, 0:n], [A_], [rgb_])
                pm = ps[6]
                for g in range(2):
                    self.mm(pm[:, 0:n], self.ones_bf, rgb_[:, g, 0:n], g == 0, g == 1, [rgb_, self.cbf], [pm])
                self.act(yr[:, 0:n], pm[:, 0:n], AF.Ln, [pm], [yr], bias=EPS * RGW, scale=1.0)
                self.act(yr[:, 0:n], yr[:, 0:n], AF.Exp, [yr], [yr], scale=-0.5)
                for g in range(2):
                    self.stt("dve", mixr[:, g, 0:n], I_[:, g, 0:n], 16.0, yr[:, 0:n], ALU.mult, ALU.mult, [I_, yr], [mixr])
                self.dma("pool", S["mixT"][768:1024, c0:c0 + n].rearrange("(g p) t -> p g t", p=128), mixr[:, :, 0:n], [mixr], [], sbuf=mixr)

                for ch in range(nch):
                    o = ch * L
                    xsb = [xcb[:, g, o:o + L] for g in range(3)]
                    Bb = [xcb[:, 3 + g, o:o + L] for g in range(2)]
                    Cb = [xcb[:, 5 + g, o:o + L] for g in range(2)]
                    self.P.op("dve", lambda e, o=o: e.tensor_tensor_scan(out=cumT[0:6, o:o + L], data0=aT[0:6, o:o + L], data1=self.cf[0:6, 256:256 + L],
                                                                          initial=0.0, op0=ALU.add, op1=ALU.add), [aT, self.cf], [cumT])
                    p7 = ps[5]
                    self.tr(p7[0:L, 0:6], dtT[0:6, o:o + L], self.identf[0:6, 0:6], [dtT, self.cst], [p7])
                    self.tr(p7[0:L, 6:12], cumT[0:6, o:o + L], self.identf[0:6, 0:6], [cumT, self.cst], [p7])
                    self.cp("dve", dctok[0:L, :], p7[0:L, 0:12], [p7], [dctok])
                    self.tt("dve", rhsb[0:6, 0:6 * L].rearrange("p (h t) -> p h t", h=6), cumT[0:6, o:o + L].unsqueeze(1).to_broadcast([6, 6, L]),
                            self.i6[0:6, 0:6].unsqueeze(2).to_broadcast([6, 6, L]), ALU.mult, [cumT, self.i6], [rhsb])
                    pb0, pb1 = ps[psi % 4], ps[(psi + 1) % 4]
                    psi += 2
                    self.mm(pb0[:, 0:3 * L], self.cf[0:6, 0:128], rhsb[0:6, 0:3 * L], True, True, [self.cf, rhsb], [pb0])
                    self.mm(pb1[:, 0:3 * L], self.cf[0:6, 0:128], rhsb[0:6, 3 * L:6 * L], True, True, [self.cf, rhsb], [pb1])
                    for h in range(6):
                        pb = pb0 if h < 3 else pb1
                        hh = h % 3
                        self.ts("dve" if h % 2 == 0 else "dve", dec[0:L, h, 0:L], pb[0:L, hh * L:(hh + 1) * L], dctok[0:L, 6 + h:7 + h], 0.0,
                                ALU.subtract, ALU.min, [pb, dctok], [dec])
                    self.act(dec[0:L, :, 0:L], dec[0:L, :, 0:L], AF.Exp, [dec], [dec])
                    self.cp("dve", smallt[:, 12:15], pb0[:, 0:3 * L].rearrange("p (h t) -> p h t", h=3)[:, :, L - 1], [pb0], [smallt])
                    self.cp("dve", smallt[:, 15:18], pb1[:, 0:3 * L].rearrange("p (h t) -> p h t", h=3)[:, :, L - 1], [pb1], [smallt])
                    self.tt("dve", smallt[0:L, 0:6], smallt[0:L, 12:18], dctok[0:L, 6:12], ALU.subtract, [smallt, dctok], [smallt])
                    self.cp("dve", smallt[0:L, 6:12], dctok[0:L, 6:12], [dctok], [smallt])
                    self.act(smallt[:, 0:18], smallt[:, 0:18], AF.Exp, [smallt], [smallt])
                    pcb = ps[psi % 4]
                    psi += 1
                    for g in range(2):
                        self.mm(pcb[0:L, g * L:(g + 1) * L], Bb[g], Cb[g], True, True, [xcb], [pcb])
                    self.tt("dve", cbm[0:L, :, 0:L], pcb[0:L, 0:2 * L].rearrange("p (g t) -> p g t", g=2),
                            self.tri_ts[0:L, 0:L].unsqueeze(1).to_broadcast([L, 2, L]), ALU.mult, [pcb, self.cst], [cbm])
                    for g in range(2):
                        self.tt("pool", wT[0:L, 3 * g:3 * g + 3, 0:L], dec[0:L, 3 * g:3 * g + 3, 0:L],
                                cbm[0:L, g, 0:L].unsqueeze(1).to_broadcast([L, 3, L]), ALU.mult, [dec, cbm], [wT])
                    for g in range(3):
                        self.tr(psb[0:L, g * 128:(g + 1) * 128], xsb[g], self.ident_bf, [xcb, self.cbf], [psb])
                    for g in range(2):
                        self.tr(psb[0:L, 384 + g * 128:384 + (g + 1) * 128], Bb[g], self.ident_bf, [xcb, self.cbf], [psb])
                    self.tt("dve", xdt[0:L, :].rearrange("p (h d) -> p h d", h=6), psb[0:L, 0:SSDW].rearrange("p (h d) -> p h d", h=6),
                            dctok[0:L, 0:6].unsqueeze(2).to_broadcast([L, 6, HD]), ALU.mult, [psb, dctok], [xdt])
                    self.cp("dve", btok[0:L, :], psb[0:L, 384:640], [psb], [btok])
                    self.tt("pool", xde[0:L, :].rearrange("p (h d) -> p h d", h=6), xdt[0:L, :].rearrange("p (h d) -> p h d", h=6), smallt[0:L, 0:6].unsqueeze(2).to_broadcast([L, 6, HD]), ALU.mult, [xdt, smallt], [xde])
                    py = ps[4]
                    for h in range(6):
                        self.mm(py[0:L, h * HD:(h + 1) * HD], wT[0:L, h, 0:L], xdt[0:L, h * HD:(h + 1) * HD], True, True, [wT, xdt], [py])
                    pyi = ps[psi % 4]
                    psi += 1
                    for g in range(2):
                        self.mm(pyi[0:L, g * 192:(g + 1) * 192], Cb[g], stateb[:, g * 192:(g + 1) * 192], True, True, [xcb, stateb], [pyi])
                    self.tt("dve", ytmp[0:L, :].rearrange("p (h d) -> p h d", h=6), pyi[0:L, 0:SSDW].rearrange("p (h d) -> p h d", h=6),
                            smallt[0:L, 6:12].unsqueeze(2).to_broadcast([L, 6, HD]), ALU.mult, [pyi, smallt], [ytmp])
                    self.tt("dve", ytk[0:L, :], py[0:L, 0:SSDW], ytmp[0:L, :], ALU.add, [py, ytmp], [ytk])
                    pu = ps[psi % 4]
                    psi += 1
                    for g in range(2):
                        self.mm(pu[:, g * 192:(g + 1) * 192], btok[0:L, g * 128:(g + 1) * 128], xde[0:L, g * 192:(g + 1) * 192], True, True, [btok, xde], [pu])
                    self.tt("dve", state[:, :].rearrange("p (h d) -> p h d", h=6), state[:, :].rearrange("p (h d) -> p h d", h=6), smallt[:, 12:18].unsqueeze(2).to_broadcast([128, 6, HD]), ALU.mult, [state, smallt], [state])
                    self.tt("dve", state[:, :], state[:, :], pu[:, 0:SSDW], ALU.add, [state, pu], [state])
                    self.cp("pool", stateb[:, :], state[:, :], [state], [stateb])
                    pt_ = ps[psi % 4]
                    psi += 1
                    for g in range(3):
                        self.tr(pt_[:, g * L:(g + 1) * L], ytk[0:L, g * 128:(g + 1) * 128], self.identf[0:L, 0:L], [ytk, self.cst], [pt_])
                    for g in range(3):
                        self.stt("dve", yg[:, g, o:o + L], xc[:, g, o:o + L], pc("dsk", g), pt_[:, g * L:(g + 1) * L], ALU.mult, ALU.add, [xc, pt, pt_], [yg])
                if ti == len(tl) - 1:
                    self.dma("pool", O["ssm_p"][l], state[:, :], [state], [], sbuf=state)
                self.tt("dve", yg[:, :, 0:n], yg[:, :, 0:n], sz[:, :, 0:n], ALU.mult, [yg, sz], [yg])
                self.tt("pool", ysq[:, :, 0:n], yg[:, :, 0:n], yg[:, :, 0:n], ALU.mult, [yg], [ysq])
                pm = ps[6]
                for g in range(3):
                    self.mm(pm[:, 0:n], self.ones_bf, ysq[:, g, 0:n], g == 0, g == 2, [ysq, self.cbf], [pm])
                self.act(yr[:, 0:n], pm[:, 0:n], AF.Ln, [pm], [yr], bias=EPS * SSDW, scale=1.0)
                self.act(yr[:, 0:n], yr[:, 0:n], AF.Exp, [yr], [yr], scale=-0.5)
                for g in range(3):
                    self.stt("dve", mixb[:, g, 0:n], yg[:, g, 0:n], float(np.sqrt(SSDW)), yr[:, 0:n], ALU.mult, ALU.mult, [yg, yr], [mixb])
                self.dma("pool", S["mixT"][0:384, c0:c0 + n].rearrange("(g p) t -> p g t", p=128), mixb[:, :, 0:n], [mixb], [], sbuf=mixb)


    def phase_a_sample(self, l, xs_in, W, rgw):
        nc, P, I, O, S = self.nc, self.P, self.I, self.O, self.S
        pt, der, ps, psb = self.ptab, self.der, self.ps, self.psb
        n = DB
        def pc(name, k=0, nn=1):
            return pt[:, l, PT[name] + k:PT[name] + k + nn]
        with contextlib.ExitStack() as st:
            X = self.sb(st, "sX", [128, 8, n])
            H = self.sb(st, "sH", [128, 8, n], BF16)
            sq = self.sb(st, "ssq", [128, 8, n], BF16)
            rstd = self.sb(st, "srstd", [128, n])
            zz = self.sb(st, "szz", [128, 3, n])
            sz = self.sb(st, "ssz", [128, 3, n])
            xpS = self.sb(st, "sxpS", [128, 7, 4, n])
            xcv = self.sb(st, "sxcv", [128, 7, n])
            xc = self.sb(st, "sxc", [128, 7, n])
            dtT = self.sb(st, "sdtT", [6, 2 * n])
            qk = self.sb(st, "sqk", [128, 6, n])
            qksq = self.sb(st, "sqksq", [128, 6, n], BF16)
            qkr = self.sb(st, "sqkr", [128, 6, n])
            vv = self.sb(st, "svv", [128, 3, n])
            xrS = self.sb(st, "sxrS", [128, 2, 4, n])
            rga = self.sb(st, "srga", [128, 2, n])
            rgb_ = self.sb(st, "srgb", [128, 2, n], BF16)
            rgt = [self.sb(st, f"srgt{i}", [128, 2, n]) for i in range(5)]
            grr = self.sb(st, "sgrr", [128, 2, n])
            h0 = self.sb(st, "sh0", [128, 2, n])
            dE = self.sb(st, "sdE", [128, 3, 2 * n])
            diag = self.sb(st, "sdiag", [128, n, NST])
            bc = [self.sb(st, f"sbc{i}", [128, n, NST]) for i in range(4)]
            St = self.sb(st, "sSt", [128, n, NST])
            tmpS = self.sb(st, "stmpS", [128, n, NST])
            y1 = self.sb(st, "sy1", [128, 3, n])
            cbp = self.sb(st, "scbp", [128, 2, n])
            CB = self.sb(st, "sCB", [128, 2, n])
            xdt = self.sb(st, "sxdt", [128, 3, n])
            yg = self.sb(st, "syg", [128, 3, n])
            ysq = self.sb(st, "sysq", [128, 3, n], BF16)
            yr = self.sb(st, "syr", [128, n])
            mixb = self.sb(st, "smixb", [128, 3, n], BF16)
            mixr = self.sb(st, "smixr", [128, 2, n], BF16)
            psi = 0
            self.dma("sp", X[:, :, :], xs_in[:, :].rearrange("(k p) t -> p k t", p=128), [], [X])
            self.dma("sp", xpS[:, :, 0:3, :], I["s_ssmc"][l].rearrange("g p j b -> p g j b"), [], [xpS])
            self.dma("sp", xrS[:, :, 0:3, :], I["s_rgc"][l].rearrange("g p j b -> p g j b"), [], [xrS])
            self.dma("sp", h0[:, :, :], I["s_rg"][l].rearrange("g p b -> p g b"), [], [h0])
            self.tt("pool", sq[:], X[:], X[:], ALU.mult, [X], [sq])
            self.cp("dve", H[:], X[:], [X], [H])
            pm = ps[6]
            for kc in range(8):
                self.mm(pm[:, 0:n], self.ones1024_bf, sq[:, kc, :], kc == 0, kc == 7, [sq, self.cbf], [pm])
            self.act(rstd[:, :], pm[:, 0:n], AF.Ln, [pm], [rstd], bias=EPS)
            self.act(rstd[:, :], rstd[:, :], AF.Exp, [rstd], [rstd], scale=-0.5)

            def proj(c_start, rows, dst_ap, dstb):
                nonlocal psi
                pp = ps[psi % 4]
                psi += 1
                for kc in range(8):
                    self.mm(pp[0:rows, 0:n], W[:, kc, c_start:c_start + rows], H[:, kc, :], kc == 0, kc == 7, [W, H], [pp])
                self.tt("dve", dst_ap, pp[0:rows, 0:n], rstd[0:rows, :], ALU.mult, [pp, rstd], [dstb])
            for g, (cs, r) in enumerate(CG_Z):
                proj(cs, r, zz[:, g, :], zz)
            for g, (cs, r) in enumerate(CG_XBC):
                proj(cs, r, xpS[:, g, 3, :], xpS)
            proj(CG_DT[0], 6, dtT[0:6, 0:n], dtT)
            for g, (cs, r) in enumerate(CG_Q + CG_K):
                proj(cs, r, qk[:, g, :], qk)
            for g, (cs, r) in enumerate(CG_V):
                proj(cs, r, vv[:, g, :], vv)
            for g, (cs, r) in enumerate(CG_XR):
                proj(cs, r, xrS[:, g, 3, :], xrS)
            for g, (cs, r) in enumerate(CG_GR):
                proj(cs, r, grr[:, g, :], grr)
            self.dma("pool", O["ssmc_s"][l].rearrange("g p j b -> p g j b"), xpS[:, :, 1:4, :], [xpS], [], sbuf=xpS)
            self.dma("pool", O["rgc_s"][l].rearrange("g p j b -> p g j b"), xrS[:, :, 1:4, :], [xrS], [], sbuf=xrS)
            self.act(sz[:], zz[:], AF.Exp, [zz], [sz], scale=-1.0)
            self.act(sz[:], sz[:], AF.Ln, [sz], [sz], bias=1.0)
            self.act(sz[:], sz[:], AF.Exp, [sz], [sz], scale=-1.0)
            self.tt("pool", sz[:], sz[:], zz[:], ALU.mult, [sz, zz], [sz])
            for g in range(7):
                self.ts("dve", xcv[:, g, :], xpS[:, g, 0, :], pc("cwx", 4 * g), pc("cbx", g), ALU.mult, ALU.add, [xpS, pt], [xcv])
                for j in range(1, 4):
                    self.stt("dve", xcv[:, g, :], xpS[:, g, j, :], pc("cwx", 4 * g + j), xcv[:, g, :], ALU.mult, ALU.add, [xpS, pt, xcv], [xcv])
            self.act(xc[:], xcv[:], AF.Exp, [xcv], [xc], scale=-1.0)
            self.act(xc[:], xc[:], AF.Ln, [xc], [xc], bias=1.0)
            self.act(xc[:], xc[:], AF.Exp, [xc], [xc], scale=-1.0)
            self.tt("dve", xc[:], xc[:], xcv[:], ALU.mult, [xc, xcv], [xc])
            self.act(dtT[0:6, 0:n], dtT[0:6, 0:n], AF.Exp, [dtT, pt], [dtT], bias=pt[0:6, l, PT["dtb"]:PT["dtb"] + 1])
            self.act(dtT[0:6, 0:n], dtT[0:6, 0:n], AF.Ln, [dtT], [dtT], bias=1.0)
            self.act(dtT[0:6, n:2 * n], dtT[0:6, 0:n], AF.Exp, [dtT, der], [dtT], scale=der[0:6, l, 0:1])
            pe_ = ps[4]
            for g3 in range(3):
                self.mm(pe_[:, g3 * 2 * n:(g3 + 1) * 2 * n], self.sel6[0:6, g3, :], dtT[0:6, :], True, True, [self.sel6, dtT], [pe_])
            self.cp("dve", dE[:].rearrange("p g c -> p (g c)"), pe_[:, 0:6 * n], [pe_], [dE])
            self.tt("pool", qksq[:], qk[:], qk[:], ALU.mult, [qk], [qksq])
            for g in range(6):
                pp = ps[psi % 4]
                psi += 1
                self.mm(pp[:, 0:n], self.bo[:, :], qksq[:, g, :], True, True, [self.bo, qksq], [pp])
                self.act(qkr[:, g, :], pp[:, 0:n], AF.Ln, [pp], [qkr], bias=EPS)
            self.act(qkr[:], qkr[:], AF.Exp, [qkr], [qkr], scale=-0.5)
            for g in range(6):
                gain = der[:, l, 9:10] if g < 3 else pc("kn")
                self.stt("dve", qkr[:, g, :], qk[:, g, :], gain, qkr[:, g, :], ALU.mult, ALU.mult, [qk, qkr, der, pt], [qkr])
            self.dma("pool", S["qs"][:, :].rearrange("(g p) b -> p g b", p=128), qkr[:, 0:3, :], [qkr], [], sbuf=qkr)
            self.dma("pool", O["kT_s"][l].rearrange("(g p) b -> p g b", p=128), qkr[:, 3:6, :], [qkr], [], sbuf=qkr)
            self.dma("pool", O["vT_s"][l].rearrange("(g p) b -> p g b", p=128), vv[:, :, :], [vv], [], sbuf=vv)
            for g in range(2):
                self.ts("dve", rga[:, g, :], xrS[:, g, 0, :], pc("cwr", 4 * g), pc("cbr", g), ALU.mult, ALU.add, [xrS, pt], [rga])
                for j in range(1, 4):
                    self.stt("dve", rga[:, g, :], xrS[:, g, j, :], pc("cwr", 4 * g + j), rga[:, g, :], ALU.mult, ALU.add, [xrS, pt, rga], [rga])
            self.cp("dve", rgb_[:], rga[:], [rga], [rgb_])
            R_, I_, A_, U_, G_ = rgt
            for g in range(2):
                for (ai, dst, nb) in ((0, R_, 5), (1, I_, 7)):
                    pp = ps[psi % 4]
                    psi += 1
                    self.mm(pp[:, 0:n], rgw[:, ai * 2 + g, :], rgb_[:, g, :], True, True, [rgw, rgb_], [pp])
                    self.act(dst[:, g, :], pp[:, 0:n], AF.Exp, [pp, der], [dst], bias=der[:, l, nb + g:nb + g + 1], scale=-1.0)
            for dst in (R_, I_):
                self.act(dst[:], dst[:], AF.Ln, [dst], [dst], bias=1.0)
                self.act(dst[:], dst[:], AF.Exp, [dst], [dst], scale=-1.0)
            for g in range(2):
                self.act(A_[:, g, :], R_[:, g, :], AF.Exp, [R_, der], [A_], scale=der[:, l, 1 + g:2 + g])
                self.act(U_[:, g, :], R_[:, g, :], AF.Exp, [R_, der], [U_], scale=der[:, l, 3 + g:4 + g])
            self.ts("dve", U_[:], U_[:], -1.0, 1.0, ALU.mult, ALU.add, [U_], [U_])
            self.act(U_[:], U_[:], AF.Ln, [U_], [U_])
            self.act(U_[:], U_[:], AF.Exp, [U_], [U_], scale=0.5)
            self.tt("dve", U_[:], U_[:], I_[:], ALU.mult, [U_, I_], [U_])
            self.tt("dve", U_[:], U_[:], rga[:], ALU.mult, [U_, rga], [U_])
            self.tt("dve", R_[:], A_[:], h0[:], ALU.mult, [A_, h0], [R_])
            self.tt("dve", R_[:], R_[:], U_[:], ALU.add, [R_, U_], [R_])
            self.dma("pool", O["rg_s"][l].rearrange("g p b -> p g b"), R_[:, :, :], [R_], [], sbuf=R_)
            self.tt("pool", G_[:], grr[:], grr[:], ALU.mult, [grr], [G_])
            self.ts("pool", G_[:], G_[:], 0.044715, 1.0, ALU.mult, ALU.add, [G_], [G_])
            self.tt("pool", G_[:], G_[:], grr[:], ALU.mult, [G_, grr], [G_])
            self.act(G_[:], G_[:], AF.Exp, [G_], [G_], scale=-2.0 * 0.7978845608028654)
            self.act(G_[:], G_[:], AF.Ln, [G_], [G_], bias=1.0)
            self.act(G_[:], G_[:], AF.Exp, [G_], [G_], scale=-1.0)
            self.tt("pool", G_[:], G_[:], grr[:], ALU.mult, [G_, grr], [G_])
            self.tt("dve", I_[:], R_[:], G_[:], ALU.mult, [R_, G_], [I_])
            self.tt("pool", A_[:], I_[:], I_[:], ALU.mult, [I_], [A_])
            self.cp("pool", rgb_[:], A_[:], [A_], [rgb_])
            pm = ps[6]
            for g in range(2):
                self.mm(pm[:, 0:n], self.ones_bf, rgb_[:, g, :], g == 0, g == 1, [rgb_, self.cbf], [pm])
            self.act(yr[:, :], pm[:, 0:n], AF.Ln, [pm], [yr], bias=EPS * RGW, scale=1.0)
            self.act(yr[:, :], yr[:, :], AF.Exp, [yr], [yr], scale=-0.5)
            for g in range(2):
                self.stt("dve", mixr[:, g, :], I_[:, g, :], 16.0, yr[:, :], ALU.mult, ALU.mult, [I_, yr], [mixr])
            self.dma("pool", S["mixs"][768:1024, :].rearrange("(g p) t -> p g t", p=128), mixr[:, :, :], [mixr], [], sbuf=mixr)
            for i, gsrc in enumerate((5, 6, 3, 4)):
                self.tt("dve", diag[:, :, :], self.identf.unsqueeze(1).to_broadcast([128, n, NST]),
                        xc[:, gsrc, :].unsqueeze(2).to_broadcast([128, n, NST]), ALU.mult, [self.cst, xc], [diag])
                for c8 in range(8):
                    pp = ps[psi % 4]
                    psi += 1
                    self.mm(pp[:, 0:512], self.cf[:, 0:128], diag[:, 4 * c8:4 * c8 + 4, :].rearrange("p b n -> p (b n)"), True, True, [self.cf, diag], [pp])
                    self.cp("act" if c8 % 2 else "dve", bc[i][:, 4 * c8:4 * c8 + 4, :].rearrange("p b n -> p (b n)"), pp[:, 0:512], [pp], [bc[i]])
            self.tt("dve", cbp[:, :, :], xc[:, 5:7, :], xc[:, 3:5, :], ALU.mult, [xc], [cbp])
            pp = ps[psi % 4]
            psi += 1
            self.mm(pp[:, 0:2 * n], self.cf[:, 0:128], cbp[:, :, :].rearrange("p g b -> p (g b)"), True, True, [self.cf, cbp], [pp])
            self.cp("dve", CB[:].rearrange("p g b -> p (g b)"), pp[:, 0:2 * n], [pp], [CB])
            for g3 in range(3):
                self.tt("dve", xdt[:, g3, :], xc[:, g3, :], dE[:, g3, 0:n], ALU.mult, [xc, dE], [xdt])
            for g3 in range(3):
                self.dma("sp", St[:, :, :], I["s_ssm"][l, g3], [], [St])
                for hf in range(2):
                    r0 = 64 * hf
                    gr = (2 * g3 + hf) // 3
                    rs = slice(r0, r0 + 64)
                    self.tt("pool", tmpS[rs, :, :], St[rs, :, :], bc[gr][rs, :, :], ALU.mult, [St, bc[gr]], [tmpS])
                    self.P.op("dve", lambda e, rs=rs, g3=g3: e.tensor_reduce(out=y1[rs, g3, :], in_=tmpS[rs, :, :], axis=AX.X, op=ALU.add), [tmpS], [y1])
                    self.tt("dve", St[rs, :, :], St[rs, :, :], dE[rs, g3, n:2 * n].unsqueeze(2).to_broadcast([64, n, NST]), ALU.mult, [St, dE], [St])
                    self.tt("pool", tmpS[rs, :, :], bc[2 + gr][rs, :, :], xdt[rs, g3, :].unsqueeze(2).to_broadcast([64, n, NST]), ALU.mult, [bc[2 + gr], xdt, tmpS], [tmpS])
                    self.tt("dve", St[rs, :, :], St[rs, :, :], tmpS[rs, :, :], ALU.add, [St, tmpS], [St])
                    self.tt("dve", y1[rs, g3, :], y1[rs, g3, :], dE[rs, g3, n:2 * n], ALU.mult, [y1, dE], [y1])
                    self.tt("dve", yg[rs, g3, :], CB[rs, gr, :], xdt[rs, g3, :], ALU.mult, [CB, xdt], [yg])
                    self.tt("dve", yg[rs, g3, :], yg[rs, g3, :], y1[rs, g3, :], ALU.add, [yg, y1], [yg])
                self.dma("pool", O["ssm_s"][l, g3], St[:, :, :], [St], [], sbuf=St)
                self.stt("dve", yg[:, g3, :], xc[:, g3, :], pc("dsk", g3), yg[:, g3, :], ALU.mult, ALU.add, [xc, pt, yg], [yg])
            self.tt("dve", yg[:], yg[:], sz[:], ALU.mult, [yg, sz], [yg])
            self.tt("pool", ysq[:], yg[:], yg[:], ALU.mult, [yg], [ysq])
            pm = ps[6]
            for g in range(3):
                self.mm(pm[:, 0:n], self.ones_bf, ysq[:, g, :], g == 0, g == 2, [ysq, self.cbf], [pm])
            self.act(yr[:, :], pm[:, 0:n], AF.Ln, [pm], [yr], bias=EPS * SSDW, scale=1.0)
            self.act(yr[:, :], yr[:, :], AF.Exp, [yr], [yr], scale=-0.5)
            for g in range(3):
                self.stt("dve", mixb[:, g, :], yg[:, g, :], float(np.sqrt(SSDW)), yr[:, :], ALU.mult, ALU.mult, [yg, yr], [mixb])
            self.dma("pool", S["mixs"][0:384, :].rearrange("(g p) t -> p g t", p=128), mixb[:, :, :], [mixb], [], sbuf=mixb)
            self.P.barrier()

    def phase_b_sample(self, l):
        nc, P, I, O, S = self.nc, self.P, self.I, self.O, self.S
        pt, ps = self.ptab, self.ps
        n = DB
        NP = self.NPAGES
        with contextlib.ExitStack() as st:
            hsel = self.sb(st, "hsel", [128, 3, HD])
            qs = self.sb(st, "qs", [128, 3, n])
            sbh = self.sb(st, "sbh", [128, DEPTH])
            pgi = self.sb(st, "pgi", [NP, n], I32)
            qsel = self.sb(st, "qsel", [HD, n])
            qd = self.sb(st, "qd", [HD, n, HD])
            qb = self.sb(st, "qb", [NP, n, HD])
            Kg = [self.sb(st, f"Kg{i}", [NP, 128, HD]) for i in range(2)]
            Vg = [self.sb(st, f"Vg{i}", [NP, 128, HD]) for i in range(2)]
            z = self.sb(st, "az", [NP, 128])
            E = self.sb(st, "aE", [NP, 128])
            Lp = self.sb(st, "aLp", [NP, 128])
            Fc = self.sb(st, "aF", [NP, 128])
            Wt = self.sb(st, "aW", [NP, 128])
            nb = self.sb(st, "anb", [NP, 1])
            Rr = self.sb(st, "aR", [NP, HD])
            osb = self.sb(st, "aosb", [HD, n])
            oall = self.sb(st, "aoall", [128, 3, n])
            osq = self.sb(st, "aosq", [128, 3, n], BF16)
            orr = self.sb(st, "aorr", [128, n])
            omix = self.sb(st, "aomix", [128, 3, n], BF16)
            self.dma("sp", hsel[:], I["hsel"].rearrange("g p d -> p g d"), [], [hsel])
            self.dma("sp", qs[:], S["qs"][:, :].rearrange("(g p) b -> p g b", p=128), [], [qs])
            self.dma("sp", sbh[:], I["sbh"][:, :], [], [sbh])
            self.dma("sp", pgi[:], I["ptab_pg"][:, :], [], [pgi])
            pq = ps[0]
            for g in range(3):
                self.mm(pq[0:HD, 0:n], hsel[:, g, :], qs[:, g, :], g == 0, g == 2, [hsel, qs], [pq])
            self.cp("dve", qsel[:, :], pq[0:HD, 0:n], [pq], [qsel])
            self.tt("dve", qd[:, :, :], self.identf[0:HD, 0:HD].unsqueeze(1).to_broadcast([HD, n, HD]),
                    qsel[:, :].unsqueeze(2).to_broadcast([HD, n, HD]), ALU.mult, [self.cst, qsel], [qd])
            for c4 in range(4):
                pp = ps[1 + c4 % 2]
                self.mm(pp[0:NP, 0:512], self.cf[0:HD, 0:NP], qd[:, 8 * c4:8 * c4 + 8, :].rearrange("p b d -> p (b d)"), True, True, [self.cf, qd], [pp])
                self.cp("dve", qb[:, 8 * c4:8 * c4 + 8, :].rearrange("p b d -> p (b d)"), pp[0:NP, 0:512], [pp], [qb])
            po = ps[3]
            bias = sbh[0:NP, l:l + 1]
            for b in range(n):
                KG, VG = Kg[b % 2], Vg[b % 2]
                import os
                if os.environ.get("SKIP_GATHER"):
                    self.memset("pool", KG[:, :, :], 0.0, [KG])
                    self.memset("pool", VG[:, :, :], 0.0, [VG])
                else:
                  self.P.dma("pool", lambda e, KG=KG, b=b: e.indirect_dma_start(out=KG[:, :, :].rearrange("p s d -> p (s d)"), out_offset=None, in_=I["ck"][l][:, :],
                           in_offset=bass.IndirectOffsetOnAxis(ap=pgi[:, b:b + 1], axis=0)), [pgi], [KG])
                  self.P.dma("pool", lambda e, VG=VG, b=b: e.indirect_dma_start(out=VG[:, :, :].rearrange("p s d -> p (s d)"), out_offset=None, in_=I["cv"][l][:, :],
                           in_offset=bass.IndirectOffsetOnAxis(ap=pgi[:, b:b + 1], axis=0)), [pgi], [VG])
                self.tt("pool", KG[:, :, :], KG[:, :, :], qb[:, b, :].unsqueeze(1).to_broadcast([NP, 128, HD]), ALU.mult, [KG, qb], [KG])
                self.P.op("dve", lambda e, KG=KG: e.tensor_reduce(out=z[:, :], in_=KG[:, :, :], axis=AX.X, op=ALU.add), [KG], [z])
                self.act(E[:, :], z[:, :], AF.Exp, [z, sbh], [E], bias=bias)
                self.act(Lp[:, :], E[:, :], AF.Ln, [E], [Lp], bias=1.0)
                self.P.op("dve", lambda e: e.tensor_tensor_scan(out=Fc[:, :], data0=Lp[:, :], data1=self.cf[0:NP, 256:384], initial=0.0, op0=ALU.add, op1=ALU.add),
                          [Lp, self.cf], [Fc])
                pc_ = ps[4]
                self.mm(pc_[0:NP, 0:1], self.triU_f[0:NP, 0:NP], Fc[:, 127:128], True, True, [self.cst, Fc], [pc_])
                self.stt("dve", nb[:, :], pc_[0:NP, 0:1], -1.0, bias, ALU.mult, ALU.add, [pc_, sbh], [nb])
                self.tt("dve", z[:, :], z[:, :], Fc[:, :], ALU.add, [z, Fc], [z])
                self.tt("dve", z[:, :], z[:, :], Lp[:, :], ALU.subtract, [z, Lp], [z])
                self.act(Wt[:, :], z[:, :], AF.Exp, [z, nb], [Wt], bias=nb[:, 0:1])
                self.tt("pool", VG[:, :, :], VG[:, :, :], Wt[:, :].unsqueeze(2).to_broadcast([NP, 128, HD]), ALU.mult, [VG, Wt], [VG])
                self.P.op("dve", lambda e, VG=VG: e.tensor_reduce(out=Rr[:, :], in_=VG[:, :, :].rearrange("p s d -> p d s"), axis=AX.X, op=ALU.add), [VG], [Rr])
                self.mm(po[0:HD, b:b + 1], Rr[:, :], self.cf[0:NP, 0:1], True, True, [Rr, self.cf], [po])
            self.cp("dve", osb[:, :], po[0:HD, 0:n], [po], [osb])
            self.dma("pool", S["ag_in"][l][:, :], osb[:, :], [osb], [], sbuf=osb)
            P.barrier()
            import os
            if os.environ.get("SKIP_CC"):
                self.dma("pool", S["ag_out"][l][0:HD, :], osb[:, :], [osb], [], sbuf=osb)
            else:
              tok = P.cc(lambda e: e.collective_compute("AllGather", ALU.bypass, replica_groups=[list(range(NCORES))],
                                                      ins=[S["ag_in"][l].opt()], outs=[S["ag_out"][l].opt()]), f"ccsem{l}")
            P.barrier()
            self.dma("sp", oall[:], S["ag_out"][l][0:SSDW, :].rearrange("(g p) b -> p g b", p=128), [], [oall])
            self.tt("pool", osq[:], oall[:], oall[:], ALU.mult, [oall], [osq])
            pm = ps[6]
            for g in range(3):
                self.mm(pm[:, 0:n], self.ones_bf, osq[:, g, :], g == 0, g == 2, [osq, self.cbf], [pm])
            self.act(orr[:, :], pm[:, 0:n], AF.Ln, [pm], [orr], bias=EPS * SSDW)
            self.act(orr[:, :], orr[:, :], AF.Exp, [orr], [orr], scale=-0.5)
            for g in range(3):
                self.stt("dve", omix[:, g, :], oall[:, g, :], float(np.sqrt(SSDW)), orr[:, :], ALU.mult, ALU.mult, [oall, orr], [omix])
            self.dma("pool", S["mixs"][384:768, :].rearrange("(g p) t -> p g t", p=128), omix[:, :, :], [omix], [], sbuf=omix)

    def phase_b(self, l):
        nc, P, I, O, S = self.nc, self.P, self.I, self.O, self.S
        T, NB = self.T, self.NB
        G4 = [[0, 1, 2, 3], [4, 5, 6, 7]]
        CH = 12
        for nm in ("KT", "QT", "V2"):
            for c_ in range(SSDW // CH):
                P.cc(lambda e, nm=nm, c_=c_: e.collective_compute("AllGather", ALU.bypass, replica_groups=G4,
                                                                  ins=[S[nm][c_ * CH:(c_ + 1) * CH, :].opt()],
                                                                  outs=[S[nm + "g"][l][c_ * 4 * CH:(c_ + 1) * 4 * CH, :].opt()]), f"ccB{l}")
        P.barrier()
        with contextlib.ExitStack() as st:
            ridx = self.sb(st, "ridx", [128, 3], I32)
            sbb2 = self.sb(st, "sbb2", [128, DEPTH, 2])
            self.dma("sp", ridx[:], I["ridx"][:, :], [], [ridx])
            self.dma("sp", sbb2[:], I["sbb2"][:, :, :], [], [sbb2])
            KA = self.sb(st, "KA", [128, 2, T], BF16)
            Qf = self.sb(st, "Qf", [64, 2, T], BF16)
            Vs = self.sb(st, "Vs", [128, NB * 128], BF16)
            for s_ in range(2):
                for c0 in range(0, T, 4096):
                    c1 = min(T, c0 + 4096)
                    self.memset("pool", KA[0:96, s_, c0:c1], 0.0, [KA])
                    self.memset("pool", KA[64:65, s_, c0:c1], 1.0, [KA])
            for s_ in range(2):
                self.P.dma("pool", lambda e, s_=s_: e.indirect_dma_start(out=KA[0:64, s_, :], out_offset=None, in_=S["KTg"][l][:, :],
                           in_offset=bass.IndirectOffsetOnAxis(ap=ridx[0:64, s_:s_ + 1], axis=0)), [ridx], [KA])
                self.P.dma("pool", lambda e, s_=s_: e.indirect_dma_start(out=Qf[0:64, s_, :], out_offset=None, in_=S["QTg"][l][:, :],
                           in_offset=bass.IndirectOffsetOnAxis(ap=ridx[0:64, s_:s_ + 1], axis=0)), [ridx], [Qf])
            self.P.dma("pool", lambda e: e.indirect_dma_start(out=Vs[:, :], out_offset=None, in_=S["V2g"][l][:, :],
                       in_offset=bass.IndirectOffsetOnAxis(ap=ridx[:, 2:3], axis=0)), [ridx], [Vs])
            QW = 512
            QA = [[self.sb(st, f"QA{a_}{b_}", [128, QW], BF16) for b_ in range(3)] for a_ in range(2)]
            for a_ in range(2):
                for b_ in range(3):
                    self.memset("pool", QA[a_][b_][0:96, :], 0.0, [QA[a_][b_]])
            csel = self.sb(st, "csel", [128, 65], BF16)
            self.memset("pool", csel[:, :], 0.0, [csel])
            self.memset("pool", csel[:, 64:65], 1.0, [csel])
            Es = [self.sb(st, f"Es{i}", [128, QW]) for i in range(2)]
            Ls = [self.sb(st, f"Ls{i}", [128, QW], BF16) for i in range(3)]
            Lf = self.sb(st, "Lf", [128, QW])
            Ws = [self.sb(st, f"Ws{i}", [128, QW], BF16) for i in range(3)]
            Wf = self.sb(st, "Wf", [128, QW], BF16)
            osb = [self.sb(st, f"osb{h}", [HD, QW]) for h in range(4)]
            maskd = self.sb(st, "maskd", [128, 4, QW], BF16)
            self.memset("pool", maskd[:], 0.0, [maskd])
            for jj in range(4):
                self.cp("pool", maskd[:, jj, 128 * jj:128 * (jj + 1)], self.tri_st, [self.cst], [maskd])
                if jj < 3:
                    self.memset("pool", maskd[:, jj, 128 * (jj + 1):QW], 1.0, [maskd])
            ps = self.ps
            PS_S = [ps[0], ps[1]]
            PS_C = [ps[2], ps[3]]
            PS_cs = ps[4]
            PS_O = ps[5]
            itS = 0
            itC = 0
            hcount = 0
            qgroups = [(0, NMETA)] + [(NMETA + QW * i, QW) for i in range(self.SEQ // QW)]
            for qi, (q0, qn) in enumerate(qgroups):
                kbl = []
                if qi == 0:
                    kbl.append((0, NMETA, 0, "meta"))
                else:
                    b_hi = 4 * qi
                    for jj in range(3, -1, -1):
                        kbl.append((b_hi - 3 + jj, 128, NMETA + 128 * (b_hi - 3 + jj - 1), jj))
                    for j in range(b_hi - 4, 0, -1):
                        kbl.append((j, 128, NMETA + 128 * (j - 1), None))
                    kbl.append((0, NMETA, 0, None))
                nk = len(kbl)
                for h in range(2):
                    Qh = QA[hcount % 2]
                    ob = osb[hcount % 4]
                    hcount += 1
                    bias = sbb2[:, l, h:h + 1]
                    for b_ in range(3):
                        self.cp("pool", Qh[b_][0:64, 0:qn], Qf[0:64, h, q0:q0 + qn], [Qf], [Qh[b_]])
                    self.memset("pool", Qh[0][64:65, 0:qn], 0.0, [Qh[0]])
                    for k in range(nk + 2):
                        if k < nk:
                            kb, Pn, kc0, mk = kbl[k]
                            pS = PS_S[itS % 2]
                            E = Es[itS % 2]
                            itS += 1
                            Lb = Ls[k % 3]
                            Qk = Qh[k % 3]
                            self.mm(pS[0:Pn, 0:qn], KA[0:64, h, kc0:kc0 + Pn], Qk[0:64, 0:qn], True, True, [KA, Qk], [pS])
                            self.act(E[0:Pn, 0:qn], pS[0:Pn, 0:qn], AF.Exp, [pS, sbb2], [E], bias=bias[0:Pn, :])
                            if mk is None:
                                self.act(Lb[0:Pn, 0:qn], E[0:Pn, 0:qn], AF.Ln, [E], [Lb], bias=1.0)
                            else:
                                self.act(Lf[0:Pn, 0:qn], E[0:Pn, 0:qn], AF.Ln, [E], [Lf], bias=1.0)
                                map_ = self.tri_st[0:Pn, 0:qn] if mk == "meta" else maskd[:, mk, 0:qn]
                                self.tt("dve", Lb[0:Pn, 0:qn], Lf[0:Pn, 0:qn], map_, ALU.mult, [Lf, maskd, self.cst], [Lb])
                            if k < nk - 1:
                                Qn = Qh[(k + 1) % 3]
                                self.mm(PS_cs[0:65, 0:qn], csel[0:Pn, 0:65], Lb[0:Pn, 0:qn], k == 0, True, [Lb, csel], [PS_cs])
                                self.ts("dve", Qn[64:65, 0:qn], PS_cs[64:65, 0:qn], -1.0, None, ALU.mult, None, [PS_cs], [Qn])
                        if 1 <= k <= nk:
                            k1 = k - 1
                            kb, Pn, kc0, mk = kbl[k1]
                            pC = PS_C[itC % 2]
                            itC += 1
                            Lb = Ls[k1 % 3]
                            Wb = Ws[k1 % 3]
                            Qk = Qh[k1 % 3]
                            self.mm(pC[0:Pn, 0:qn], KA[0:65, h, kc0:kc0 + Pn], Qk[0:65, 0:qn], True, False, [KA, Qk], [pC])
                            self.mm(pC[0:Pn, 0:qn], self.ntriU_bf[0:Pn, 0:Pn], Lb[0:Pn, 0:qn], False, True, [Lb, self.cbf], [pC])
                            if mk is None:
                                self.act(Wb[0:Pn, 0:qn], pC[0:Pn, 0:qn], AF.Exp, [pC, sbb2], [Wb], bias=bias[0:Pn, :])
                            else:
                                self.act(Wf[0:Pn, 0:qn], pC[0:Pn, 0:qn], AF.Exp, [pC, sbb2], [Wf], bias=bias[0:Pn, :])
                                map_ = self.tri_st[0:Pn, 0:qn] if mk == "meta" else maskd[:, mk, 0:qn]
                                self.tt("pool", Wb[0:Pn, 0:qn], Wf[0:Pn, 0:qn], map_, ALU.mult, [Wf, maskd, self.cst], [Wb])
                        if k >= 2:
                            k2 = k - 2
                            kb, Pn, kc0, mk = kbl[k2]
                            Wb = Ws[k2 % 3]
                            self.mm(PS_O[0:HD, 0:qn], Vs[0:Pn, kb * 128 + h * HD:kb * 128 + (h + 1) * HD], Wb[0:Pn, 0:qn], k2 == 0, k2 == nk - 1, [Vs, Wb], [PS_O])
                    self.cp("dve", ob[:, 0:qn], PS_O[0:HD, 0:qn], [PS_O], [ob])
                    o2v = S["O2"][l].rearrange("(t p) c -> p t c", p=128)
                    if qi == 0:
                        self.dma("sp", o2v[HD * h:HD * (h + 1), 0, 0:qn], ob[:, 0:qn], [ob], [], sbuf=ob)
                    else:
                        t0_ = 1 + 2 * (qi - 1)
                        self.dma("sp", o2v[HD * h:HD * (h + 1), t0_:t0_ + 2, :], ob[:, 0:qn].rearrange("p (t c) -> p t c", c=256), [ob], [], sbuf=ob)
        P.barrier()
        for t_ in range(self.NT2):
            P.cc(lambda e, t_=t_: e.collective_compute("AllGather", ALU.bypass, replica_groups=G4,
                                                       ins=[S["O2"][l][t_ * 128:(t_ + 1) * 128, :].opt()],
                                                       outs=[S["O2g"][l][t_ * 512:(t_ + 1) * 512, :].opt()]), f"ccO{l}")
        P.barrier()
        with contextlib.ExitStack() as st:
            NW = 256
            og = [self.sb(st, f"og{i}", [128, 3, NW]) for i in range(2)]
            osq = self.sb(st, "osq", [128, 3, NW], BF16)
            orr = self.sb(st, "orr", [128, NW])
            omx = [self.sb(st, f"omx{i}", [128, 3, NW], BF16) for i in range(2)]
            pm = self.ps[6]
            for ti, (c0, n) in enumerate(self.tiles(NW)):
                OG, OM = og[ti % 2], omx[ti % 2]
                self.dma("sp", OG[:, :, 0:n], S["O2g"][l].rearrange("(t r p) c -> p t r c", r=4, p=128)[:, ti, 0:3, 0:n], [], [OG])
                self.tt("pool", osq[:, :, 0:n], OG[:, :, 0:n], OG[:, :, 0:n], ALU.mult, [OG], [osq])
                for g in range(3):
                    self.mm(pm[:, 0:n], self.ones_bf, osq[:, g, 0:n], g == 0, g == 2, [osq, self.cbf], [pm])
                self.act(orr[:, 0:n], pm[:, 0:n], AF.Ln, [pm], [orr], bias=EPS * SSDW)
                self.act(orr[:, 0:n], orr[:, 0:n], AF.Exp, [orr], [orr], scale=-0.5)
                for g in range(3):
                    self.stt("dve", OM[:, g, 0:n], OG[:, g, 0:n], float(np.sqrt(SSDW)), orr[:, 0:n], ALU.mult, ALU.mult, [OG, orr], [OM])
                self.dma("act", S["mixT"][384:768, c0:c0 + n].rearrange("(g p) t -> p g t", p=128), OM[:, :, 0:n], [OM], [], sbuf=OM)

    def phase_c(self, l, x_in, x_out, xs_in, xs_out):
        nc, P, I, O, S = self.nc, self.P, self.I, self.O, self.S
        pt = self.ptab
        def pc(name, k=0, n=1):
            return pt[:, l, PT[name] + k:PT[name] + k + n]
        with contextlib.ExitStack() as st:
            gcols = [PT["ssdn"], PT["ssdn"] + 1, PT["ssdn"] + 2, PT["sbn"], PT["sbn"] + 1, PT["sbn"] + 2, PT["rgn"], PT["rgn"] + 1]
            Wo = self.load_cast(st, "wo", I["w_out"][l], D, 8, scale_cols=gcols, l=l, chunk=1024)
            Wu = self.load_cast(st, "wup", I["w_up"][l], 2 * DFF, 8, scale_cols=PT["g2"], l=l, chunk=1408)
            Wd = self.load_cast(st, "wdn", I["w_dn"][l], D, NFC, chunk=1024)
            NW = 256
            X = self.sb(st, "cX", [128, 8, NW])
            MX = self.sb(st, "cMX", [128, 8, NW], BF16)
            H2 = self.sb(st, "cH2", [128, 8, NW], BF16)
            sq = self.sb(st, "csq", [128, 8, NW], BF16)
            rstd = self.sb(st, "crstd", [128, NW])
            AB = self.sb(st, "cAB", [128, NFC, NW], BF16)
            gp = [self.sb(st, f"cgp{i}", [128, 2 + NW]) for i in range(2)]
            gcv = [self.sb(st, f"cgcv{i}", [128, NW]) for i in range(2)]
            gs = [self.sb(st, f"cgs{i}", [128, NW]) for i in range(2)]
            uu = [self.sb(st, f"cuu{i}", [128, NW]) for i in range(2)]
            halo = self.sb(st, "chalo", [128, NFC, 2])
            self.memset("pool", halo[:], 0.0, [halo])
            ps = self.ps
            psi = 0
            tl = [(c0, n, False) for (c0, n) in self.tiles(NW)]
            n_prompt = len(tl)
            if self.do_sample:
                tl.append((0, DB, True))
                ffS = self.sb(st, "cffS", [128, NFC, 3, DB])
                self.dma("act", ffS[:, :, 0:2, :], I["s_ffc"][l].rearrange("f p j b -> p f j b"), [], [ffS])
            for ti, (c0, n, smp) in enumerate(tl):
                last = ti == n_prompt - 1
                xsrc = xs_in[:, :] if smp else x_in[:, c0:c0 + n]
                msrc = S["mixs"][:, :] if smp else S["mixT"][:, c0:c0 + n]
                xdst = xs_out[:, :] if smp else x_out[:, c0:c0 + n]
                self.dma("sp", X[:, :, 0:n], xsrc.rearrange("(k p) t -> p k t", p=128), [], [X])
                self.dma("act", MX[:, :, 0:n], msrc.rearrange("(k p) t -> p k t", p=128), [], [MX])
                for dc in range(8):
                    pp = ps[psi % 6]
                    psi += 1
                    for mc in range(8):
                        self.mm(pp[:, 0:n], Wo[:, mc, dc * 128:(dc + 1) * 128], MX[:, mc, 0:n], mc == 0, mc == 7, [Wo, MX], [pp])
                    self.tt("dve", X[:, dc, 0:n], X[:, dc, 0:n], pp[:, 0:n], ALU.add, [X, pp], [X])
                self.tt("pool", sq[:, :, 0:n], X[:, :, 0:n], X[:, :, 0:n], ALU.mult, [X], [sq])
                self.cp("dve", H2[:, :, 0:n], X[:, :, 0:n], [X], [H2])
                pm = ps[6]
                for kc in range(8):
                    self.mm(pm[:, 0:n], self.ones1024_bf, sq[:, kc, 0:n], kc == 0, kc == 7, [sq, self.cbf], [pm])
                self.act(rstd[:, 0:n], pm[:, 0:n], AF.Ln, [pm], [rstd], bias=EPS)
                self.act(rstd[:, 0:n], rstd[:, 0:n], AF.Exp, [rstd], [rstd], scale=-0.5)
                for fc in range(NFC):
                    G, GC, GS, U = gp[fc % 2], gcv[fc % 2], gs[fc % 2], uu[fc % 2]
                    pg = ps[psi % 6]
                    pu = ps[(psi + 1) % 6]
                    psi += 2
                    for kc in range(8):
                        self.mm(pg[:, 0:n], Wu[:, kc, fc * 128:(fc + 1) * 128], H2[:, kc, 0:n], kc == 0, kc == 7, [Wu, H2], [pg])
                    for kc in range(8):
                        self.mm(pu[:, 0:n], Wu[:, kc, DFF + fc * 128:DFF + (fc + 1) * 128], H2[:, kc, 0:n], kc == 0, kc == 7, [Wu, H2], [pu])
                    if smp:
                        self.tt("dve", ffS[:, fc, 2, :], pg[:, 0:n], rstd[:, 0:n], ALU.mult, [pg, rstd], [ffS])
                        taps = [ffS[:, fc, j, :] for j in range(3)]
                        TR = [ffS, pt]
                    else:
                        self.cp("pool", G[:, 0:2], halo[:, fc, :], [halo], [G])
                        self.tt("dve", G[:, 2:2 + n], pg[:, 0:n], rstd[:, 0:n], ALU.mult, [pg, rstd], [G])
                        self.cp("pool", halo[:, fc, :], G[:, n:n + 2], [G], [halo])
                        taps = [G[:, j:j + n] for j in range(3)]
                        TR = [G, pt]
                    self.ts("dve", GC[:, 0:n], taps[0], pc("cwf", 3 * fc), pc("cbf", fc), ALU.mult, ALU.add, TR, [GC])
                    for j in range(1, 3):
                        self.stt("dve", GC[:, 0:n], taps[j], pc("cwf", 3 * fc + j), GC[:, 0:n], ALU.mult, ALU.add, TR + [GC], [GC])
                    self.act(GS[:, 0:n], GC[:, 0:n], AF.Exp, [GC], [GS], scale=-1.0)
                    self.act(GS[:, 0:n], GS[:, 0:n], AF.Ln, [GS], [GS], bias=1.0)
                    self.act(GS[:, 0:n], GS[:, 0:n], AF.Exp, [GS], [GS], scale=-1.0)
                    self.tt("dve", U[:, 0:n], pu[:, 0:n], rstd[:, 0:n], ALU.mult, [pu, rstd], [U])
                    self.tt("pool", GS[:, 0:n], GS[:, 0:n], GC[:, 0:n], ALU.mult, [GS, GC], [GS])
                    self.tt("pool", AB[:, fc, 0:n], GS[:, 0:n], U[:, 0:n], ALU.mult, [GS, U], [AB])
                if last:
                    self.dma("pool", O["ffc_p"][l].rearrange("f p j -> p f j"), halo[:, :, :], [halo], [], sbuf=halo)
                if smp:
                    self.dma("pool", O["ffc_s"][l].rearrange("f p j b -> p f j b"), ffS[:, :, 1:3, :], [ffS], [], sbuf=ffS)
                for dc in range(8):
                    pp = ps[psi % 6]
                    psi += 1
                    for fc in range(NFC):
                        self.mm(pp[:, 0:n], Wd[:, fc, dc * 128:(dc + 1) * 128], AB[:, fc, 0:n], fc == 0, fc == NFC - 1, [Wd, AB], [pp])
                    self.tt("dve", X[:, dc, 0:n], X[:, dc, 0:n], pp[:, 0:n], ALU.add, [X, pp], [X])
                self.dma("pool", xdst.rearrange("(k p) t -> p k t", p=128), X[:, :, 0:n], [X], [], sbuf=X)


def _consts():
    c = np.zeros((128, 1024), np.float32)
    idx = np.arange(128)
    c[:, 0:128] = np.eye(128, dtype=np.float32)
    c[:, 128:256] = (idx[:, None] >= idx[None, :]).astype(np.float32)
    c[:, 256:384] = (idx[None, :] >= idx[:, None]).astype(np.float32)
    c[:, 384:512] = (idx[None, :] > idx[:, None]).astype(np.float32)
    return c


def _ptab(inp):
    pt = np.zeros((DEPTH, 128, NPT), np.float32)
    f = lambda k: np.asarray(inp[k], np.float32)
    for l in range(DEPTH):
        pt[l, :, PT["g1"]:PT["g1"] + 8] = f("norm1")[l].reshape(8, 128).T
        pt[l, :, PT["g2"]:PT["g2"] + 8] = f("norm2")[l].reshape(8, 128).T
        cw = f("ssd_conv_w")[l]
        pt[l, :, PT["cwx"]:PT["cwx"] + 28] = cw.reshape(4, 7, 128).transpose(2, 1, 0).reshape(128, 28)
        pt[l, :, PT["cbx"]:PT["cbx"] + 7] = f("ssd_conv_b")[l].reshape(7, 128).T
        cw = f("rg_conv_w")[l]
        pt[l, :, PT["cwr"]:PT["cwr"] + 8] = cw.reshape(4, 2, 128).transpose(2, 1, 0).reshape(128, 8)
        pt[l, :, PT["cbr"]:PT["cbr"] + 2] = f("rg_conv_b")[l].reshape(2, 128).T
        cw = f("ffn_conv_w")[l]
        pt[l, :, PT["cwf"]:PT["cwf"] + 66] = cw.reshape(3, NFC, 128).transpose(2, 1, 0).reshape(128, 66)
        pt[l, :, PT["cbf"]:PT["cbf"] + NFC] = f("ffn_conv_b")[l].reshape(NFC, 128).T
        pt[l, 0:6, PT["dtb"]] = f("ssd_dt_bias")[l]
        pt[l, 0:6, PT["alog"]] = f("ssd_a_log")[l]
        pt[l, :, PT["dsk"]:PT["dsk"] + 3] = np.repeat(f("ssd_d")[l], 64).reshape(3, 128).T
        pt[l, :, PT["ssdn"]:PT["ssdn"] + 3] = f("ssd_norm")[l].reshape(3, 128).T
        pt[l, :, PT["qn"]] = np.tile(f("q_norm")[l], 2)
        pt[l, :, PT["kn"]] = np.tile(f("k_norm")[l], 2)
        pt[l, :, PT["sbb"]:PT["sbb"] + 6] = f("sb_bias")[l][None, :]
        pt[l, :, PT["sbn"]:PT["sbn"] + 3] = f("sb_out_norm")[l].reshape(3, 128).T
        pt[l, :, PT["rba"]:PT["rba"] + 2] = f("rg_ba")[l].reshape(2, 128).T
        pt[l, :, PT["rbx"]:PT["rbx"] + 2] = f("rg_bx")[l].reshape(2, 128).T
        pt[l, :, PT["rlam"]:PT["rlam"] + 2] = f("rg_lambda")[l].reshape(2, 128).T
        pt[l, :, PT["rgn"]:PT["rgn"] + 2] = f("rg_out_norm")[l].reshape(2, 128).T
    return pt


def _rgw(inp):
    out = np.zeros((DEPTH, 2, 2, 128, 128), np.float32)
    for l in range(DEPTH):
        for ai, k in enumerate(("rg_wa", "rg_wx")):
            w = np.asarray(inp[k], np.float32)[l]
            for g in range(2):
                for b in range(2):
                    out[l, ai, g, 64 * b:64 * b + 64, 64 * b:64 * b + 64] = w[2 * g + b]
    return out


_CACHE = {}


def kernel(**inp):
    x_prompt = np.asarray(inp["x_prompt"], np.float32)
    B, SEQ, _ = x_prompt.shape
    T = NMETA + SEQ
    ck = np.asarray(inp["cache_k"])
    NPOOL = ck.shape[1]
    page_table = np.asarray(inp["page_table"])
    NPAGES = page_table.shape[1]
    do_sample = True
    key = (SEQ, NPAGES, NPOOL, do_sample)
    if key not in _CACHE:
        _CACHE[key] = K(SEQ, NPAGES, NPOOL, do_sample=do_sample)
    kb = _CACHE[key]
    meta = np.asarray(inp["meta_tokens"], np.float32)
    sel6 = np.zeros((6, 3, 128), np.float32)
    for g in range(3):
        for r in range(128):
            sel6[2 * g + r // 64, g, r] = 1.0
    common = {
        "xT_s": np.ascontiguousarray(np.asarray(inp["x_sample"], np.float32)[:, 0, :].T),
        "w_in": np.asarray(inp["w_in"], np.float32),
        "w_out": np.asarray(inp["w_out"], np.float32),
        "w_up": np.asarray(inp["w_up"], np.float32),
        "w_dn": np.asarray(inp["w_down"], np.float32),
        "ptab": _ptab(inp),
        "rgw": _rgw(inp),
        "consts": _consts(),
        "sel6": sel6,
        "i6": np.eye(6, dtype=np.float32),
    }
    cv = np.asarray(inp["cache_v"])
    f32 = lambda k: np.asarray(inp[k], np.float32)
    zero_pool = None
    common["ptab_pg"] = np.ascontiguousarray(page_table.T.astype(np.int32))
    common["s_ssm"] = np.ascontiguousarray(f32("state_ssm").reshape(DEPTH, DB, 3, 128, NST).transpose(0, 2, 3, 1, 4))
    common["s_ssmc"] = np.ascontiguousarray(f32("state_ssm_conv").reshape(DEPTH, DB, 3, 7, 128).transpose(0, 3, 4, 2, 1))
    common["s_rg"] = np.ascontiguousarray(f32("state_rg").reshape(DEPTH, DB, 2, 128).transpose(0, 2, 3, 1))
    common["s_rgc"] = np.ascontiguousarray(f32("state_rg_conv").reshape(DEPTH, DB, 3, 2, 128).transpose(0, 3, 4, 2, 1))
    common["s_ffc"] = np.ascontiguousarray(f32("state_ffn_conv").reshape(DEPTH, DB, 2, NFC, 128).transpose(0, 3, 4, 2, 1))
    sbb = f32("sb_bias")
    in_maps = []
    for c in range(NCORES):
        m = dict(common)
        hs = np.zeros((3, 128, HD), np.float32)
        if c < NH:
            for l_ in range(DEPTH):
                m[f"ck{l_}"] = np.ascontiguousarray(ck[l_, :, :, c, :]).reshape(NPOOL, 128 * HD)
                m[f"cv{l_}"] = np.ascontiguousarray(cv[l_, :, :, c, :]).reshape(NPOOL, 128 * HD)
            for d in range(HD):
                r = c * HD + d
                hs[r // 128, r % 128, d] = 1.0
            m["sbh"] = np.ascontiguousarray(np.broadcast_to(sbb[:, c][None, :], (128, DEPTH)))
        else:
            if zero_pool is None:
                zero_pool = np.zeros((NPOOL, 128 * HD), np.float32)
            for l_ in range(DEPTH):
                m[f"ck{l_}"] = zero_pool
                m[f"cv{l_}"] = zero_pool
            m["sbh"] = np.zeros((128, DEPTH), np.float32)
        m["hsel"] = hs
        if c % 4 == 0 and c // 4 < B:
            m["xT_p"] = np.ascontiguousarray(np.concatenate([meta, x_prompt[c // 4]], axis=0).T)
        else:
            m["xT_p"] = np.zeros((D, T), np.float32)
        j_ = c % 4
        ri = np.zeros((128, 3), np.int32)
        pp_ = np.arange(128)
        fmap = lambda g_: (g_ // 12) * 48 + (g_ % 12)
        ri[:, 0] = fmap(j_ * 128 + (pp_ % 64))
        ri[:, 1] = fmap(j_ * 128 + 64 + (pp_ % 64))
        ri[:, 2] = fmap(j_ * 128 + pp_)
        m["ridx"] = ri
        sb2 = np.zeros((128, DEPTH, 2), np.float32)
        if j_ < 3:
            for s_ in range(2):
                sb2[:, :, s_] = sbb[:, 2 * j_ + s_][None, :]
        m["sbb2"] = sb2
        in_maps.append(m)
    res = run_bass_kernel_spmd(kb.nc, in_maps, core_ids=list(range(NCORES)))
    R = res.results
    def stk(name):
        return np.stack([R[4 * b][name] for b in range(B)], axis=0)
    y_prompt = stk("yT_p").transpose(0, 2, 1)[:, NMETA:, :]
    kT = stk("kT_p")
    new_k = kT.transpose(1, 0, 3, 2).reshape(DEPTH, B, T, NH, HD)
    new_v = stk("vT_p").transpose(1, 0, 3, 2).reshape(DEPTH, B, T, NH, HD)
    ssm = stk("ssm_p")
    new_ssm = ssm.transpose(1, 0, 3, 2).reshape(DEPTH, B, NH, HD, NST)
    ssmc = stk("ssmc_p")
    new_ssmc = ssmc.transpose(1, 0, 4, 2, 3).reshape(DEPTH, B, 3, XBC)
    rg = stk("rg_p").transpose(1, 0, 2, 3).reshape(DEPTH, B, RGW)
    rgc = stk("rgc_p").transpose(1, 0, 4, 2, 3).reshape(DEPTH, B, 3, RGW)
    ffc = stk("ffc_p").transpose(1, 0, 4, 2, 3).reshape(DEPTH, B, 2, DFF)
    r0 = R[0]
    y_sample = r0["yT_s"].T.reshape(DB, 1, D)
    k_s = r0["kT_s"].transpose(0, 2, 1).reshape(DEPTH, DB, 1, NH, HD)
    v_s = r0["vT_s"].transpose(0, 2, 1).reshape(DEPTH, DB, 1, NH, HD)
    ssm_s = r0["ssm_s"].transpose(0, 3, 1, 2, 4).reshape(DEPTH, DB, NH, HD, NST)
    ssmc_s = r0["ssmc_s"].transpose(0, 4, 3, 1, 2).reshape(DEPTH, DB, 3, XBC)
    rg_s = r0["rg_s"].transpose(0, 3, 1, 2).reshape(DEPTH, DB, RGW)
    rgc_s = r0["rgc_s"].transpose(0, 4, 3, 1, 2).reshape(DEPTH, DB, 3, RGW)
    ffc_s = r0["ffc_s"].transpose(0, 4, 3, 1, 2).reshape(DEPTH, DB, 2, DFF)
    outs = [y_prompt, y_sample, new_k, new_v, new_ssm, new_ssmc, rg, rgc, ffc, k_s, v_s, ssm_s, ssmc_s, rg_s, rgc_s, ffc_s]
    return tuple(np.ascontiguousarray(o, dtype=np.float32) for o in outs)
```
